# Optimizing a Trainium2 kernel written in Bass

```python
import jax, jax.numpy as jnp
from jax import lax
import numpy as np

D_MODEL = 1024
BATCH = 16
SEQ = 2048
DEPTH = 1
DEC_BATCH = 32
DEC_SEQ = 64
PAST_LEN = 1024

CHUNK = 64
RET_HEADS = 4
RET_DK = 128
RET_DV = 128
GLA_HEADS = 4
GLA_DK = 64
GLA_DV = 128
GLA_RANK = 16
GLA_TAU = 16.0
D_FF = 2816
CONV_W = 3
ROPE_BASE = 10000.0
EPS = 1e-6
RET_W = RET_HEADS * RET_DV
GLA_W = GLA_HEADS * GLA_DV
MIX_W = RET_W + GLA_W
IN_WIDTHS = (RET_HEADS * RET_DK, RET_HEADS * RET_DK, RET_W, RET_W,
             GLA_HEADS * GLA_DK, GLA_HEADS * GLA_DK, GLA_W, GLA_W, GLA_RANK)
D_IN = sum(IN_WIDTHS)

kernel_name = "hybrid_retention_gla_convffn_stream_step"


def _rmsnorm(x, g):
    xf = x.astype(jnp.float32)
    y = xf * lax.rsqrt(jnp.mean(xf * xf, axis=-1, keepdims=True) + EPS)
    return (y * g.astype(jnp.float32)).astype(x.dtype)


def _split_in(z):
    outs, off = [], 0
    for w in IN_WIDTHS:
        outs.append(z[..., off:off + w])
        off += w
    return outs


def _heads(t, h):
    b, T, w = t.shape
    return t.reshape(b, T, h, w // h).transpose(0, 2, 1, 3)


def _merge(t):
    b, h, T, d = t.shape
    return t.transpose(0, 2, 1, 3).reshape(b, T, h * d)


def _rotary(t, pos):
    d = t.shape[-1]
    inv = ROPE_BASE ** (-jnp.arange(0, d, 2, dtype=jnp.float32) / d)
    ang = pos.astype(jnp.float32)[:, None] * inv[None, :]
    cos, sin = jnp.cos(ang), jnp.sin(ang)
    t1, t2 = t[..., : d // 2], t[..., d // 2:]
    return jnp.concatenate([t1 * cos - t2 * sin, t1 * sin + t2 * cos], axis=-1)


def _to_chunks(t, L):
    b, h, T, d = t.shape
    return t.reshape(b, h, T // L, L, d).transpose(2, 0, 1, 3, 4)


def _from_chunks(t):
    n, b, h, L, d = t.shape
    return t.transpose(1, 2, 0, 3, 4).reshape(b, h, n * L, d)


def _head_groupnorm(o):
    mu = jnp.mean(o, axis=-1, keepdims=True)
    var = jnp.mean(jnp.square(o - mu), axis=-1, keepdims=True)
    return (o - mu) * lax.rsqrt(var + EPS)


def _head_rms(o):
    return o * lax.rsqrt(jnp.mean(o * o, axis=-1, keepdims=True) + EPS)


def _retention(q, k, v, r0):
    T = q.shape[2]
    L = min(CHUNK, T)
    log_g = jnp.log1p(-jnp.power(2.0, -5.0 - jnp.arange(RET_HEADS, dtype=jnp.float32)))
    idx = jnp.arange(L, dtype=jnp.float32)
    diff = idx[:, None] - idx[None, :]
    causal = diff >= 0
    dmat = jnp.where(causal, jnp.exp(log_g[:, None, None] * jnp.where(causal, diff, 0.0)), 0.0)
    xi = jnp.exp(log_g[:, None] * (idx + 1.0))[:, :, None]
    zeta = jnp.exp(log_g[:, None] * (L - 1.0 - idx))[:, :, None]
    g_L = jnp.exp(log_g * L)[:, None, None]

    def step(r, qkv):
        qc, kc, vc = qkv
        s = jnp.einsum('bhid,bhjd->bhij', qc, kc) * dmat
        o = jnp.einsum('bhij,bhje->bhie', s, vc) + jnp.einsum('bhid,bhde->bhie', qc, r) * xi
        r = g_L * r + jnp.einsum('bhjd,bhje->bhde', kc * zeta, vc)
        return r, o

    r, o = lax.scan(step, r0, (_to_chunks(q, L), _to_chunks(k, L), _to_chunks(v, L)))
    return _from_chunks(o), r


def _gla(q, k, v, log_a, s0):
    T = q.shape[2]
    L = min(CHUNK, T)
    causal = jnp.tril(jnp.ones((L, L), dtype=bool))[None, None, :, :, None]

    def step(s, inp):
        qc, kc, vc, ac = inp
        b = jnp.cumsum(ac, axis=2)
        bL = b[:, :, -1:, :]
        inter = jnp.einsum('bhtd,bhde->bhte', qc * jnp.exp(b), s)
        rel = jnp.where(causal, b[:, :, :, None, :] - b[:, :, None, :, :], -jnp.inf)
        att = jnp.einsum('bhtd,bhsd,bhtsd->bhts', qc, kc, jnp.exp(rel))
        o = inter + jnp.einsum('bhts,bhse->bhte', att, vc)
        s = jnp.exp(bL[:, :, 0, :])[..., None] * s + jnp.einsum('bhsd,bhse->bhde', kc * jnp.exp(bL - b), vc)
        return s, o

    s, o = lax.scan(step, s0, (_to_chunks(q, L), _to_chunks(k, L), _to_chunks(v, L), _to_chunks(log_a, L)))
    return _from_chunks(o), s


def _layer(x, c, pos0, r0, s0, conv0, w_ada, b_ada, g_norm_mix, w_in, w_a2, b_a2,
           g_ret_norm, g_gla_norm, w_out, g_norm_ffn, w_ffn_in, conv_w, conv_b, w_ffn_out):
    T = x.shape[1]
    dt = x.dtype
    f32 = jnp.float32
    mod = jnp.einsum('bd,de->be', jax.nn.silu(c), w_ada) + b_ada
    sh1, sc1, g1, sh2, sc2, g2 = jnp.split(mod[:, None, :], 6, axis=-1)

    h = _rmsnorm(x, g_norm_mix) * (1.0 + sc1) + sh1
    z = jnp.einsum('btd,de->bte', h, w_in).astype(f32)
    rq, rk, rv, rg, gq, gk, gv, gr, ga = _split_in(z)
    pos = pos0 + jnp.arange(T, dtype=jnp.int32)

    q_r = _rotary(_heads(rq, RET_HEADS), pos)
    k_r = _rotary(_heads(rk, RET_HEADS), pos) * (RET_DK ** -0.5)
    o_r, r_new = _retention(q_r, k_r, _heads(rv, RET_HEADS), r0.astype(f32))
    o_r = _merge(_head_groupnorm(o_r)) * g_ret_norm.astype(f32) * jax.nn.silu(rg)

    log_a = jax.nn.log_sigmoid(jnp.einsum('btr,re->bte', ga, w_a2.astype(f32)) + b_a2.astype(f32)) / GLA_TAU
    o_g, s_new = _gla(_heads(gq, GLA_HEADS) * (GLA_DK ** -0.5), _heads(gk, GLA_HEADS),
                      _heads(gv, GLA_HEADS), _heads(log_a, GLA_HEADS), s0.astype(f32))
    o_g = _merge(_head_rms(o_g)) * g_gla_norm.astype(f32) * jax.nn.silu(gr)

    mix = jnp.concatenate([o_r, o_g], axis=-1).astype(dt)
    x = x + g1 * jnp.einsum('bte,ed->btd', mix, w_out)

    h2 = _rmsnorm(x, g_norm_ffn) * (1.0 + sc2) + sh2
    a, up = jnp.split(jnp.einsum('btd,df->btf', h2, w_ffn_in), 2, axis=-1)
    buf = jnp.concatenate([conv0.astype(a.dtype), a], axis=1)
    conv = conv_b + conv_w[0] * buf[:, 0:T]
    for j in range(1, CONV_W):
        conv = conv + conv_w[j] * buf[:, j:j + T]
    y = jax.nn.silu(conv) * up
    x = x + g2 * jnp.einsum('btf,fd->btd', y, w_ffn_out)
    return x, r_new.astype(r0.dtype), s_new.astype(s0.dtype), buf[:, T:]


def setup_inputs(seed: int = 0) -> dict:
    key = jax.random.key(seed)
    ks = jax.random.split(key, 24)
    n = jax.random.normal
    f = jnp.float32
    return {
        "x_prompt": n(ks[0], (BATCH, SEQ, D_MODEL), f),
        "x_sample": n(ks[1], (DEC_BATCH, DEC_SEQ, D_MODEL), f),
        "c_prompt": n(ks[2], (BATCH, D_MODEL), f),
        "c_sample": n(ks[3], (DEC_BATCH, D_MODEL), f),
        "state_ret": 0.5 * n(ks[4], (DEPTH, DEC_BATCH, RET_HEADS, RET_DK, RET_DV), f),
        "state_gla": 0.5 * n(ks[5], (DEPTH, DEC_BATCH, GLA_HEADS, GLA_DK, GLA_DV), f),
        "cache_ffn_conv": n(ks[6], (DEPTH, DEC_BATCH, CONV_W - 1, D_FF), f),
        "w_ada": 0.3 * D_MODEL ** -0.5 * n(ks[7], (DEPTH, D_MODEL, 6 * D_MODEL), f),
        "b_ada": 0.02 * n(ks[8], (DEPTH, 6 * D_MODEL), f),
        "g_norm_mix": 1.0 + 0.05 * n(ks[9], (DEPTH, D_MODEL), f),
        "w_in": D_MODEL ** -0.5 * n(ks[10], (DEPTH, D_MODEL, D_IN), f),
        "w_a2": GLA_RANK ** -0.5 * n(ks[11], (DEPTH, GLA_RANK, GLA_HEADS * GLA_DK), f),
        "b_a2": 0.1 * n(ks[12], (DEPTH, GLA_HEADS * GLA_DK), f),
        "g_ret_norm": 1.0 + 0.05 * n(ks[13], (DEPTH, RET_W), f),
        "g_gla_norm": 1.0 + 0.05 * n(ks[14], (DEPTH, GLA_W), f),
        "w_out": MIX_W ** -0.5 * n(ks[15], (DEPTH, MIX_W, D_MODEL), f),
        "g_norm_ffn": 1.0 + 0.05 * n(ks[16], (DEPTH, D_MODEL), f),
        "w_ffn_in": D_MODEL ** -0.5 * n(ks[17], (DEPTH, D_MODEL, 2 * D_FF), f),
        "conv_w": 0.5 * n(ks[18], (DEPTH, CONV_W, D_FF), f),
        "conv_b": 0.02 * n(ks[19], (DEPTH, D_FF), f),
        "w_ffn_out": D_FF ** -0.5 * n(ks[20], (DEPTH, D_FF, D_MODEL), f),
        "g_final": 1.0 + 0.05 * n(ks[21], (D_MODEL,), f),
    }


def reference(x_prompt, x_sample, c_prompt, c_sample, state_ret, state_gla, cache_ffn_conv,
              w_ada, b_ada, g_norm_mix, w_in, w_a2, b_a2, g_ret_norm, g_gla_norm, w_out,
              g_norm_ffn, w_ffn_in, conv_w, conv_b, w_ffn_out, g_final):
    bp = x_prompt.shape[0]
    xp, xs = x_prompt, x_sample
    rp_l, sp_l, cp_l, rs_l, ss_l, cs_l = [], [], [], [], [], []
    for l in range(DEPTH):
        params = (w_ada[l], b_ada[l], g_norm_mix[l], w_in[l], w_a2[l], b_a2[l], g_ret_norm[l],
                  g_gla_norm[l], w_out[l], g_norm_ffn[l], w_ffn_in[l], conv_w[l], conv_b[l], w_ffn_out[l])
        r0p = jnp.zeros((bp, RET_HEADS, RET_DK, RET_DV), x_prompt.dtype)
        s0p = jnp.zeros((bp, GLA_HEADS, GLA_DK, GLA_DV), x_prompt.dtype)
        c0p = jnp.zeros((bp, CONV_W - 1, D_FF), x_prompt.dtype)
        xp, rp, sp, cp = _layer(xp, c_prompt, 0, r0p, s0p, c0p, *params)
        xs, rs, ss, cs = _layer(xs, c_sample, PAST_LEN, state_ret[l], state_gla[l], cache_ffn_conv[l], *params)
        rp_l.append(rp); sp_l.append(sp); cp_l.append(cp)
        rs_l.append(rs); ss_l.append(ss); cs_l.append(cs)
    y_prompt = _rmsnorm(xp, g_final)
    y_sample = _rmsnorm(xs, g_final)
    return (y_prompt, y_sample, jnp.stack(rp_l), jnp.stack(sp_l), jnp.stack(cp_l),
            jnp.stack(rs_l), jnp.stack(ss_l), jnp.stack(cs_l))
```

```python
import contextlib
import numpy as np
import ml_dtypes
import concourse.bass as bass
import concourse.mybir as mybir
from concourse.bass_utils import run_bass_kernel_spmd

F32 = mybir.dt.float32
BF16 = mybir.dt.bfloat16
AF = mybir.ActivationFunctionType
ALU = mybir.AluOpType

D = 1024
KC = 8
DFF = 2816
FC = 22
DIN = 3600
NCORE = 8
NPS = 2
NSS = 4
NSEQ = NPS + NSS
SEQ = 2048
DSEQ = 64
PAST = 1024
NTP = 512
TOK = NPS * SEQ + NSS * DSEQ
EPS = 1e-6
NSLOT = 4
COMPUTE = ("pe", "act", "dve", "pool")


class Res:
    __slots__ = ("name", "last_w", "readers", "al", "psum")

    def __init__(self, name, psum=False):
        self.name = name
        self.last_w = None
        self.readers = []
        self.al = (self,)
        self.psum = psum


def alias(a_list, b_list):
    for a in [x.r for x in a_list]:
        for b in [x.r for x in b_list]:
            if b not in a.al:
                a.al = a.al + (b,)
            if a not in b.al:
                b.al = b.al + (a,)


class Op:
    __slots__ = ("eng", "fn", "reads", "writes", "deps", "sig", "cnt", "key", "is_dma")

    def __init__(self, eng, fn, reads, writes, key=None):
        self.eng = eng
        self.fn = fn
        self.reads = reads
        self.writes = writes
        self.deps = ()
        self.sig = False
        self.cnt = 0
        self.key = key
        self.is_dma = key is not None


class Prog:
    def __init__(self, nc):
        self.nc = nc
        self.ops = []

    def op(self, eng, fn, reads=(), writes=()):
        o = Op(eng, fn, tuple(reads), tuple(writes))
        self.ops.append(o)
        return o

    def dma(self, queue, out, in_, reads=(), writes=(), key=None):
        o = Op(queue, (out, in_), tuple(reads), tuple(writes), key=key)
        self.ops.append(o)
        return o

    def _analyse(self):
        for o in self.ops:
            raw = set()
            oth = set()
            for r0 in o.reads:
                for r in r0.al:
                    if r.last_w is not None:
                        raw.add(r.last_w)
                    if r.psum:
                        for rd in r.readers:
                            if rd.eng != o.eng:
                                raw.add(rd)
            for w0 in o.writes:
                for w in w0.al:
                    if w.last_w is not None:
                        oth.add(w.last_w)
                    for rd in w.readers:
                        oth.add(rd)
            raw.discard(o)
            oth.discard(o)
            need = []
            for d in raw | oth:
                if (not d.is_dma) and (not o.is_dma) and d.eng == o.eng:
                    if o.eng == "pe":
                        continue
                need.append(d)
            o.deps = need
            for d in need:
                d.sig = True
            for r in o.reads:
                r.readers.append(o)
            for w in o.writes:
                w.last_w = o
                w.readers = []

    def emit(self, final_eng, final_keys):
        nc = self.nc
        self._analyse()
        cnt = {e: 0 for e in COMPUTE}
        kcnt = {}
        for o in self.ops:
            if o.is_dma:
                kcnt[o.key] = kcnt.get(o.key, 0) + 1
                o.cnt = 16 * kcnt[o.key]
            elif o.sig:
                cnt[o.eng] += 1
                o.cnt = cnt[o.eng]
        stack = contextlib.ExitStack()
        sems = {e: stack.enter_context(nc.semaphore("s_" + e)) for e in COMPUTE}
        ksems = {k: stack.enter_context(nc.semaphore("d_" + str(k))) for k in kcnt}
        per_eng = {}
        for o in self.ops:
            per_eng.setdefault(o.eng, []).append(o)

        def run(eng_name, eng):
            waited = {}
            for o in per_eng.get(eng_name, []):
                want = {}
                for d in o.deps:
                    s = ("k", d.key) if d.is_dma else ("e", d.eng)
                    if d.cnt > want.get(s, 0):
                        want[s] = d.cnt
                for s, v in want.items():
                    if waited.get(s, 0) >= v:
                        continue
                    waited[s] = v
                    eng.wait_ge(ksems[s[1]] if s[0] == "k" else sems[s[1]], v)
                if o.is_dma:
                    out, in_ = o.fn
                    eng.dma_start(out=out, in_=in_).then_inc(ksems[o.key], 16)
                else:
                    ins = o.fn(eng)
                    if o.sig:
                        ins.then_inc(sems[o.eng], 1)
            if eng_name == final_eng:
                for k in final_keys:
                    if k in kcnt:
                        eng.wait_ge(ksems[k], 16 * kcnt[k])

        with nc.Block() as block:
            @block.tensor
            def _(e):
                run("pe", e)

            @block.scalar
            def _(e):
                run("act", e)

            @block.vector
            def _(e):
                run("dve", e)

            @block.gpsimd
            def _(e):
                run("pool", e)

            @block.sync
            def _(e):
                run("sp", e)
        stack.close()


def _consts():
    c = {}
    pos = np.concatenate([np.arange(SEQ)] * NPS + [PAST + np.arange(DSEQ)] * NSS).astype(np.float32)
    inv = (10000.0 ** (-np.arange(0, 128, 2, dtype=np.float32) / 128.0)).astype(np.float32)
    ang = pos[None, :] * inv[:, None]
    cos = np.cos(ang).astype(np.float32)
    sin = np.sin(ang).astype(np.float32)
    c["rotC"] = np.ascontiguousarray(np.concatenate([cos, cos], 0))
    c["rotS"] = np.ascontiguousarray(np.concatenate([sin, -sin], 0))
    h = np.arange(4, dtype=np.float64)
    log_g = np.log1p(-np.power(2.0, -5.0 - h))
    i = np.arange(128, dtype=np.float64)
    diff = i[None, :] - i[:, None]
    m = np.where(diff[None] >= 0, np.exp(log_g[:, None, None] * np.maximum(diff[None], 0)), 0.0)
    c["maskR"] = np.ascontiguousarray((128 ** -0.5 * m).transpose(1, 0, 2)).astype(np.float32)
    c["mask01"] = np.ascontiguousarray(np.broadcast_to((diff >= 0)[:, None, :], (128, 4, 128))).astype(np.float32)
    c["triN"] = ((diff >= 0) * (-1.0 / 16.0)).astype(np.float32)
    c["utriN"] = ((diff.T > 0) * (-1.0 / 16.0)).astype(np.float32)
    xi = np.exp(log_g[:, None] * (i[None, :] + 1.0))
    c["xi"] = np.ascontiguousarray(np.broadcast_to(xi[None], (128, 4, 128))).astype(np.float32)
    z128 = 128 ** -0.5 * np.exp(log_g[None, :] * (127.0 - i[:, None]))
    z64 = np.zeros((128, 4))
    z64[:64] = 128 ** -0.5 * np.exp(log_g[None, :] * (63.0 - i[:64, None]))
    c["zeta"] = np.ascontiguousarray(np.stack([z128, z64], 1)).astype(np.float32)
    c["gL"] = {128: [float(np.exp(log_g[k] * 128)) for k in range(4)],
               64: [float(np.exp(log_g[k] * 64)) for k in range(4)]}
    c["ident"] = np.eye(128, dtype=np.float32).astype(ml_dtypes.bfloat16)
    c["ones_b"] = np.ones((128, 128), dtype=np.float32).astype(ml_dtypes.bfloat16)
    c["ones_f"] = np.ones((128, 128), dtype=np.float32)
    return c


class _Stop(Exception):
    pass


def build_nc(consts, stop_at=None):
    nc = bass.Bass("TRN2", target_bir_lowering=False)

    def stage(name):
        if stop_at is not None and name == stop_at:
            raise _Stop()

    P = Prog(nc)
    es = contextlib.ExitStack()

    def dram(name, shape, dt, kind="ExternalInput"):
        return nc.dram_tensor(name, list(shape), dt, kind=kind).ap()

    xT = dram("xT", [KC, 128, TOK], F32)
    cT = dram("cT", [128, KC, NSEQ], F32)
    st_ret_in = dram("st_ret", [NSS, 4, 128, 128], F32)
    st_gla_in = dram("st_gla", [NSS, 4, 64, 128], F32)
    cc_in = dram("cc_in", [128, FC, NSS, 2], F32)
    w_ada = dram("w_ada", [D, 6 * D], F32)
    b_ada = dram("b_ada", [128, 48], F32)
    gvec = dram("gvec", [128, 3, KC], F32)
    ghead = dram("ghead", [128, 2, 4], F32)
    w_in = dram("w_in", [D, DIN], F32)
    w_a2b = dram("w_a2b", [17, 256], F32)
    w_out = dram("w_out", [D, D], F32)
    w_f1 = dram("w_f1", [D, 2 * DFF], F32)
    convp = dram("convp", [128, FC, 4], F32)
    w_f2 = dram("w_f2", [DFF, D], F32)
    rotC_d = dram("rotC", [128, TOK], F32)
    rotS_d = dram("rotS", [128, TOK], F32)
    maskR_d = dram("maskR", [128, 4, 128], F32)
    mask01_d = dram("mask01", [128, 4, 128], F32)
    triN_d = dram("triN", [128, 128], F32)
    utriN_d = dram("utriN", [128, 128], F32)
    xi_d = dram("xi", [128, 4, 128], F32)
    zeta_d = dram("zeta", [128, 2, 4], F32)
    ident_d = dram("ident", [128, 128], BF16)
    onesb_d = dram("ones_b", [128, 128], BF16)
    onesf_d = dram("ones_f", [128, 128], F32)

    yT = dram("yT", [KC, 128, TOK], F32, "ExternalOutput")
    o_ret = dram("o_ret", [NSEQ, 4, 128, 128], F32, "ExternalOutput")
    o_gla = dram("o_gla", [NSEQ, 4, 64, 128], F32, "ExternalOutput")
    o_cc = dram("o_cc", [128, FC, NSEQ, 2], F32, "ExternalOutput")

    NG = 29
    scratch = dram("wscr", [NG, 128, 4096], BF16, "Internal")
    scr_res = [Res("scr%d" % g) for g in range(NG)]

    class T:
        rk = None

        def __init__(self, name, shape, dt, psum=False, res=None):
            if psum:
                self.t = es.enter_context(nc.psum_tensor(name, list(shape), dt))
            else:
                self.t = es.enter_context(nc.sbuf_tensor(name, list(shape), dt))
            self.r = res if res is not None else Res(name, psum)

    def ring(name, n, shape, dt, psum=False):
        return [T("%s%d" % (name, i), shape, dt, psum) for i in range(n)]

    PJ = ring("pj", 4, [128, 512], F32, True)
    SC = T("sc", [128, 512], F32, True)
    OC = T("oc", [128, 512], F32, True)
    MS = T("ms", [128, 512], F32, True)
    TR = T("tr", [128, 512], F32, True)
    STAT = [MS, SC, OC]

    c_maskR = T("c_maskR", [128, 4, 128], F32)
    c_mask01 = T("c_mask01", [128, 4, 128], F32)
    c_tri = T("c_tri", [128, 128], F32)
    c_utri = T("c_utri", [128, 128], F32)
    c_xi = T("c_xi", [128, 4, 128], F32)
    c_zeta = T("c_zeta", [128, 2, 4], F32)
    c_ident = T("c_ident", [128, 128], BF16)
    c_onesb = T("c_onesb", [128, 128], BF16)
    c_onesf = T("c_onesf", [128, 128], F32)
    c_bada = T("c_bada", [128, 48], F32)
    c_gvec = T("c_gvec", [128, 3, KC], F32)
    c_ghead = T("c_ghead", [128, 2, 4], F32)
    c_conv = T("c_conv", [128, FC, 4], F32)
    c_wa2 = T("c_wa2", [17, 256], F32)
    c_cT = T("c_cT", [128, KC, NSEQ], F32)
    c_scT = T("c_scT", [128, KC, NSEQ], BF16)
    modT = T("modT", [128, 48, NSEQ], F32)
    gm1 = T("gm1", [128, KC, NSEQ], F32)
    gm2 = T("gm2", [128, KC, NSEQ], F32)

    cq = 0

    def cload(dst, src):
        nonlocal cq
        P.dma("sp", dst.t[:], src, writes=[dst.r], key="c%d" % cq)
        cq += 1

    for dst, src in ((c_maskR, maskR_d), (c_mask01, mask01_d), (c_tri, triN_d), (c_utri, utriN_d),
                     (c_xi, xi_d), (c_zeta, zeta_d), (c_ident, ident_d), (c_onesb, onesb_d),
                     (c_onesf, onesf_d), (c_bada, b_ada), (c_gvec, gvec), (c_ghead, ghead),
                     (c_conv, convp), (c_wa2, w_a2b), (c_cT, cT)):
        cload(dst, src)

    x_sb = ring("x", 2, [128, KC, NTP], F32)
    for xb in x_sb:
        xb.rk = [Res(xb.r.name + "_k%d" % k) for k in range(KC)]
    rC = ring("rC", 1, [128, NTP], F32)
    rS = ring("rS", 1, [128, NTP], F32)
    hT = T("hT", [128, KC, NTP], BF16)
    hT.rk = [Res("hT_k%d" % k) for k in range(KC)]
    sqr = ring("sq", 4, [128, NTP], BF16)
    c_eps = T("c_eps", [128, 1], F32)
    tmpr = ring("tmp", 2, [128, NTP], F32)
    rsdr = ring("rsd", 2, [128, NTP], F32)
    rsbr = ring("rsb", 2, [128, NTP], F32)
    rsd, rsb = rsdr[0], rsbr[0]
    wsl = ring("wsl", NSLOT, [128, 4096], BF16)
    ga1 = T("ga1", [17, NTP], F32)
    lsp = T("lsp", [128, 4, 256], F32)
    Aex = T("Aex", [128, 2, NTP], F32)
    Ain = T("Ain", [128, 2, NTP], F32)
    E2 = T("E2", [128, 4, 256], F32)
    qg = T("qg", [64, 4, NTP], BF16)
    kg = T("kg", [64, 4, NTP], BF16)
    eLt = T("eLt", [64, 4, 4], F32)
    kraw = T("kraw", [128, 2, NTP], BF16)
    class V:
        def __init__(self, name, ap):
            self.t = ap
            self.r = Res(name)

    U1 = es.enter_context(nc.sbuf_tensor("U1", [128, FC * NTP], BF16))
    U2 = es.enter_context(nc.sbuf_tensor("U2", [128, 2080], F32))

    def u1v(name, lo, n, k):
        return V(name, U1[:, lo:lo + n].rearrange("p (k c) -> p k c", k=k))

    qrot = u1v("qrot", 0, 2048, 4)
    qxi = u1v("qxi", 2048, 2048, 4)
    krot = u1v("krot", 4096, 2048, 4)
    kztm = u1v("kztm", 6144, 2048, 4)
    k2tm = u1v("k2tm", 8192, 1024, 4)
    vtm = u1v("vtm", 9216, 2048, 4)
    yTt = [V("y%d" % i, U1[:, i * NTP:(i + 1) * NTP]) for i in range(FC)]
    alias([qrot], yTt[0:4])
    alias([qxi], yTt[4:8])
    alias([krot], yTt[8:12])
    alias([kztm], yTt[12:16])
    alias([k2tm], yTt[16:18])
    alias([vtm], yTt[18:22])
    t1r = tmpr
    t2r = ring("t2", 2, [128, NTP], F32)
    sg = V("sg", U2[:, 0:2048].rearrange("p (k c) -> p k c", k=4))
    bufr = [V("buf%d" % i, U2[:, i * 520:(i + 1) * 520]) for i in range(2)]
    c0r = [V("c0%d" % i, U2[:, 1040 + i * 512:1040 + (i + 1) * 512]) for i in range(2)]
    alias([sg], bufr + c0r)
    osb = T("osb", [128, 4, NTP], F32)
    stmr = ring("stmr", 4, [128, 4, 128], BF16)
    stmg = ring("stmg", 4, [128, 4, 128], BF16)
    vtm2 = T("vtm2", [128, 4, 512], BF16)
    osg = [V("osg%d" % i, x_sb[1 - i].t[:, 0:4, :]) for i in range(2)]
    sgg = [V("sgg%d" % i, x_sb[1 - i].t[:, 4:8, :]) for i in range(2)]
    class _R:
        def __init__(self, r):
            self.r = r

    for i in range(2):
        alias([osg[i]], [_R(r) for r in x_sb[1 - i].rk[0:4]])
        alias([sgg[i]], [_R(r) for r in x_sb[1 - i].rk[4:8]])
    mixT = [T("mix%d" % i, [128, NTP], BF16) for i in range(8)]
    Rst = ring("Rst", 2, [128, 4, 128], F32)
    Rbf = T("Rbf", [128, 4, 128], BF16)
    Gst = ring("Gst", 2, [64, 4, 128], F32)
    Gbf = T("Gbf", [64, 4, 128], BF16)
    hist = T("hist", [128, FC, NSS, 2], F32)

    def mm(out, lhsT, rhs, start, stop, rd, wr):
        P.op("pe", lambda e: e.matmul(out, lhsT, rhs, start=start, stop=stop), rd, wr)

    def trp(out, in_, rd, wr):
        P.op("pe", lambda e: e.transpose(out, in_, c_ident.t[:]), list(rd) + [c_ident.r], wr)

    def act(out, in_, func, rd, wr, bias=None, scale=None):
        kw = {}
        if bias is not None:
            kw["bias"] = bias
        if scale is not None:
            kw["scale"] = scale
        P.op("act", lambda e: e.activation(out=out, in_=in_, func=func, **kw), rd, wr)

    def tt(eng, out, in0, in1, op, rd, wr):
        P.op(eng, lambda e: e.tensor_tensor(out=out, in0=in0, in1=in1, op=op), rd, wr)

    def stt(eng, out, in0, scalar, in1, op0, op1, rd, wr):
        P.op(eng, lambda e: e.scalar_tensor_tensor(out=out, in0=in0, scalar=scalar, in1=in1,
                                                   op0=op0, op1=op1), rd, wr)

    def ts(eng, out, in0, s1, s2, op0, op1, rd, wr):
        P.op(eng, lambda e: e.tensor_scalar(out=out, in0=in0, scalar1=s1, scalar2=s2, op0=op0, op1=op1), rd, wr)

    def cp(eng, out, in_, rd, wr):
        P.op(eng, lambda e: e.tensor_copy(out=out, in_=in_), rd, wr)

    def mset(eng, ap, val, wr):
        P.op(eng, lambda e: e.memset(ap, val), (), wr)

    def recip(out, in_, rd, wr):
        P.op("dve", lambda e: e.reciprocal(out=out, in_=in_), rd, wr)

    tiles = []
    for s in range(NPS):
        for t in range(SEQ // NTP):
            tiles.append(dict(kind="p", col0=s * SEQ + t * NTP, NT=NTP, L=128, first=(t == 0),
                              last=(t == SEQ // NTP - 1), segs=[(s, 0, NTP)], slot=s % 2))
    tiles.append(dict(kind="s", col0=NPS * SEQ, NT=NSS * DSEQ, L=64, first=True, last=True,
                      segs=[(NPS + i, i * DSEQ, DSEQ) for i in range(NSS)], slot=None))

    def group_srcs(g):
        if g == 0:
            return [(0, 8, 16, w_in[:, 3584:3600].rearrange("(k p) c -> p k c", p=128))]
        if 1 <= g <= 7:
            c0 = (g - 1) * 512
            return [(0, 8, 512, w_in[:, c0:c0 + 512].rearrange("(k p) c -> p k c", p=128))]
        if g in (8, 9):
            c0 = (g - 8) * 512
            return [(0, 8, 512, w_out[:, c0:c0 + 512].rearrange("(k p) c -> p k c", p=128))]
        if 10 <= g <= 20:
            j = g - 10
            return [(0, 8, 256, w_f1[:, 256 * j:256 * j + 256].rearrange("(k p) c -> p k c", p=128)),
                    (8 * 256, 8, 256, w_f1[:, DFF + 256 * j:DFF + 256 * j + 256].rearrange("(k p) c -> p k c", p=128))]
        j = g - 21
        return [(0, FC, 128, w_f2[:, 128 * j:128 * j + 128].rearrange("(k p) c -> p k c", p=128))]

    sched = [("ada", i) for i in range(12)]
    GORDER = [0, 1, 2, 5, 3, 6, 4, 7, 8, 9] + list(range(10, 21)) + list(range(21, 29))
    sched.append(("w", 0, 0))
    for ti in range(len(tiles)):
        for g in [1, 3, 2, 6, 5, 4, 7, 8, 9] + list(range(10, 21)):
            sched.append(("w", ti, g))
        for g in range(21, 24):
            sched.append(("w", ti, g))
        if ti + 1 < len(tiles):
            sched.append(("w", ti + 1, 0))
        for g in range(24, 29):
            sched.append(("w", ti, g))
    loaded = [0]

    def slot_view(si, nk, ncols, off=0):
        return wsl[si].t[:, off:off + nk * ncols].rearrange("p (k c) -> p k c", k=nk)

    def ensure(idx, la=NSLOT - 1):
        while loaded[0] <= min(idx + la, len(sched) - 1):
            i = loaded[0]
            si = i % NSLOT
            ent = sched[i]
            key = "w%d" % si
            if ent[0] == "ada":
                pc = ent[1]
                src = w_ada[:, pc * 512:(pc + 1) * 512].rearrange("(k p) c -> p k c", p=128)
                P.dma("pool", slot_view(si, 8, 512), src, writes=[wsl[si].r], key="wc%d" % si)
            else:
                _, ti, g = ent
                if ti == 0:
                    for (off, nk, ncols, src) in group_srcs(g):
                        P.dma("pool", slot_view(si, nk, ncols, off), src, writes=[wsl[si].r], key="wc%d" % si)
                    gsz = 128 if g == 0 else (FC * 128 if g >= 21 else 4096)
                    P.dma("sp", scratch[g][:, 0:gsz], wsl[si].t[:, 0:gsz], reads=[wsl[si].r], writes=[scr_res[g]],
                          key="ws%d" % si)
                else:
                    gsz = 128 if g == 0 else (FC * 128 if g >= 21 else 4096)
                    P.dma("sp", wsl[si].t[:, 0:gsz], scratch[g][:, 0:gsz], reads=[scr_res[g]], writes=[wsl[si].r], key=key)
            loaded[0] += 1

    spos = [0]

    def next_w(expect=None, la=NSLOT - 1):
        i = spos[0]
        spos[0] += 1
        assert expect is None or sched[i][-1] == expect or sched[i] == expect, (i, sched[i], expect)
        ensure(i, la)
        return i % NSLOT

    out_keys = set()

    def body():
        act(c_scT.t[:], c_cT.t[:], AF.Silu, [c_cT.r], [c_scT.r])
        for pc in range(12):
            si = next_w(("ada", pc))
            wv = slot_view(si, 8, 512)
            for jj in range(4):
                j = pc * 4 + jj
                for kc in range(KC):
                    mm(MS.t[:, j * NSEQ:(j + 1) * NSEQ], wv[:, kc, jj * 128:(jj + 1) * 128], c_scT.t[:, kc, :],
                       kc == 0, kc == KC - 1, [wsl[si].r, c_scT.r], [MS.r])
        tt("dve", modT.t[:], MS.t[:, 0:48 * NSEQ].rearrange("p (j s) -> p j s", s=NSEQ),
           c_bada.t[:, :].unsqueeze(2).to_broadcast([128, 48, NSEQ]),
           ALU.add, [MS.r, c_bada.r], [modT.r])
        for (gmx, gi, sc0) in ((gm1, 0, 8), (gm2, 1, 32)):
            stt("dve", gmx.t[:], modT.t[:, sc0:sc0 + 8, :], 1.0,
                c_gvec.t[:, gi, :].unsqueeze(2).to_broadcast([128, KC, NSEQ]), ALU.add, ALU.mult,
                [modT.r, c_gvec.r], [gmx.r])
        stage("prologue")
        mset("pool", ga1.t[:], 1.0, [ga1.r])
        mset("pool", c_eps.t[:], EPS, [c_eps.r])

        SH1, G1, SH2, G2 = 0, 16, 24, 40

        def norm(*a, **k):
            for _ in norm_gen(*a, **k):
                pass

        def norm_gen(ti, xs, NT, segs, gm, sh0, out_fn, final=False, sq_pool=False):
            st = TR if sq_pool else next_bank()

            def sqmm(kc, do_sq, do_mm):
                sqk = sqr[kc % 4]
                if do_sq:
                    if sq_pool:
                        tt("pool", sqk.t[:, 0:NT], xs.t[:, kc, 0:NT], xs.t[:, kc, 0:NT], ALU.mult, [xs.rk[kc]], [sqk.r])
                    else:
                        act(sqk.t[:, 0:NT], xs.t[:, kc, 0:NT], AF.Square, [xs.rk[kc]], [sqk.r])
                if do_mm:
                    mm(f32v(st)[:, 0:NT], c_onesb.t[:], sqk.t[:, 0:NT], kc == 0, kc == KC - 1, [c_onesb.r, sqk.r], [st.r])

            if sq_pool:
                for kc in range(4):
                    sqmm(kc, True, False)
                yield
                for kc in range(4):
                    sqmm(kc, False, True)
                for kc in range(4, 8):
                    sqmm(kc, True, False)
                yield
                for kc in range(4, 8):
                    sqmm(kc, False, True)
                yield
            else:
                for kc in range(KC):
                    sqmm(kc, True, True)
                yield
            rb = rstd_from(st, NT, 1.0 / D)
            for kc in range(KC):
                if final:
                    stt("dve", out_fn(kc)[0], xs.t[:, kc, 0:NT], c_gvec.t[:, 2, kc:kc + 1], rb.t[:, 0:NT],
                        ALU.mult, ALU.mult, [xs.rk[kc], c_gvec.r, rb.r], [out_fn(kc)[1]])
                    continue
                tm = tmpr[kc % 2]
                if len(segs) > 1:
                    ns_, s0_ = len(segs), segs[0][0]
                    ls_ = NT // ns_
                    v3 = lambda ap: ap.rearrange("p (s l) -> p s l", s=ns_)
                    o_ap, o_r = out_fn(kc)
                    tt("dve", v3(tm.t[:, 0:NT]), v3(xs.t[:, kc, 0:NT]), v3(rb.t[:, 0:NT]), ALU.mult,
                       [xs.rk[kc], rb.r], [tm.r])
                    tt("dve", v3(tm.t[:, 0:NT]), v3(tm.t[:, 0:NT]),
                       gm.t[:, kc, s0_:s0_ + ns_].unsqueeze(2).to_broadcast([128, ns_, ls_]), ALU.mult, [tm.r, gm.r], [tm.r])
                    tt("pool", v3(o_ap[:, 0:NT]), v3(tm.t[:, 0:NT]),
                       modT.t[:, sh0 + kc, s0_:s0_ + ns_].unsqueeze(2).to_broadcast([128, ns_, ls_]), ALU.add,
                       [tm.r, modT.r], [o_r])
                    continue
                for (sid, c0, n) in segs:
                    stt("dve", tm.t[:, c0:c0 + n], xs.t[:, kc, c0:c0 + n], gm.t[:, kc, sid:sid + 1],
                        rb.t[:, c0:c0 + n], ALU.mult, ALU.mult, [xs.rk[kc], gm.r, rb.r], [tm.r])
                    o_ap, o_r = out_fn(kc)
                    act(o_ap[:, c0:c0 + n], tm.t[:, c0:c0 + n], AF.Identity, [tm.r, modT.r], [o_r],
                        bias=modT.t[:, sh0 + kc, sid:sid + 1])

        rsc = [0]

        def rstd_from(st, NT, scale):
            i = rsc[0] % 4
            rsc[0] += 1
            rd = (rsdr + rsbr)[i]
            act(rd.t[:, 0:NT], f32v(st)[:, 0:NT], AF.Ln, [st.r, c_eps.r], [rd.r], bias=c_eps.t[:, 0:1], scale=scale)
            act(rd.t[:, 0:NT], rd.t[:, 0:NT], AF.Exp, [rd.r], [rd.r], scale=-0.5)
            return rd

        bkc = [0]
        BANKS = [PJ[0], SC, PJ[1], OC, PJ[2], MS, PJ[3]]

        def next_bank():
            b = BANKS[bkc[0] % 7]
            bkc[0] += 1
            return b

        def f32v(b):
            return b.t[:, :]

        def bf16v(b):
            return b.t[:, :].bitcast(BF16)

        next_pj = next_bank

        def proj_fm(si, col, NT, act_t, nk=KC, rhs_fn=None):
            b = next_pj()
            wv = wsl[si]
            for kc in range(nk):
                lhsT = wv.t[:, kc * (col[1]) + col[0]: kc * (col[1]) + col[0] + col[2]]
                rhs, rr = rhs_fn(kc)
                mm(b.t[0:col[2], 0:NT], lhsT, rhs, kc == 0, kc == nk - 1, [wv.r, rr], [b.r])
            return b

        n_tiles = len(tiles)

        def load_x(ti):
            tl = tiles[ti]
            NT, c0 = tl["NT"], tl["col0"]
            xs = x_sb[ti % 2]
            P.dma("pool", xs.t[:, :, 0:NT], xT[:, :, c0:c0 + NT].rearrange("k p t -> p k t"), writes=xs.rk,
                  key="x%d" % (ti % 2))

        def load_rot(ti):
            tl = tiles[ti]
            NT, c0 = tl["NT"], tl["col0"]
            P.dma("pool", rC[0].t[:, 0:NT], rotC_d[:, c0:c0 + NT], writes=[rC[0].r], key="rc")
            P.dma("pool", rS[0].t[:, 0:NT], rotS_d[:, c0:c0 + NT], writes=[rS[0].r], key="rs")

        load_x(0)
        load_rot(0)

        def t_front(ti):
            tl = tiles[ti]
            NT, L, segs = tl["NT"], tl["L"], tl["segs"]
            xs = x_sb[ti % 2]
            nch = NT // L
            zsel = 0 if L == 128 else 1
            if tl["kind"] == "p":
                ch_slot = [tl["slot"]] * nch
                ch_seq = [segs[0][0]] * nch
            else:
                ch_slot = list(range(nch))
                ch_seq = [segs[i][0] for i in range(nch)]
            par = ti % 2
            OSG, SGG = osg[par], sgg[par]

            def h_rhs(kc):
                return hT.t[:, kc, 0:NT], hT.rk[kc]


            yield from norm_gen(ti, xs, NT, segs, gm1, SH1, lambda kc: (hT.t[:, kc, :], hT.rk[kc]), sq_pool=(ti > 0))
            if tl["kind"] == "p" and tl["first"]:
                sl = tl["slot"]
                mset("pool", Rst[sl].t[:], 0.0, [Rst[sl].r])
                mset("pool", Gst[sl].t[:], 0.0, [Gst[sl].r])
                mset("pool", hist.t[:], 0.0, [hist.r])
            if tl["kind"] == "s":
                P.dma("pool", hist.t[:], cc_in, writes=[hist.r], key="hist")

            stage("t%d_init" % ti)
            stage("t%d_norm1" % ti)
            si = next_w(0)
            b = next_pj()
            for kc in range(KC):
                mm(b.t[0:16, 0:NT], wsl[si].t[:, kc * 16:(kc + 1) * 16], hT.t[:, kc, 0:NT], kc == 0, kc == KC - 1,
                   [wsl[si].r, hT.rk[kc]], [b.r])
            act(ga1.t[0:16, 0:NT], b.t[0:16, 0:NT], AF.Copy, [b.r], [ga1.r])
            yield
            for c in range(nch):
                cs = slice(c * L, (c + 1) * L)
                bu_ = next_bank()
                mm(f32v(bu_)[0:L, 0:256], ga1.t[:, cs], c_wa2.t[:], True, True, [ga1.r, c_wa2.r], [bu_.r])
                act(E2.t[0:L, c, :], f32v(bu_)[0:L, 0:256], AF.Exp, [bu_.r], [E2.r], scale=-1.0)
                act(lsp.t[0:L, c, :], E2.t[0:L, c, :], AF.Ln, [E2.r], [lsp.r], bias=1.0)
            yield
            for c in range(nch):
                cs = slice(c * L, (c + 1) * L)
                bb_ = next_bank()
                for p in range(2):
                    mm(f32v(bb_)[:, p * 128:p * 128 + L], lsp.t[0:L, c, p * 128:(p + 1) * 128], c_tri.t[0:L, 0:L],
                       True, True, [lsp.r, c_tri.r], [bb_.r])
                bview = f32v(bb_)[:, 0:256].rearrange("p (a t) -> p a t", a=2)[:, :, 0:L]
                act(Aex.t[:, :, cs], bview, AF.Exp, [bb_.r], [Aex.r])
                act(Ain.t[:, :, cs], bview, AF.Exp, [bb_.r], [Ain.r], scale=-1.0)
                bl_ = next_bank()
                mm(f32v(bl_)[0:L, 0:256], c_utri.t[0:L, 0:L], lsp.t[0:L, c, :], True, True, [c_utri.r, lsp.r], [bl_.r])
                act(E2.t[0:L, c, :], f32v(bl_)[0:L, 0:256], AF.Exp, [bl_.r], [E2.r])
            stage("t%d_glaprep" % ti)

        def t_mid(ti, pre_last=None):
            tl = tiles[ti]
            NT, L, segs = tl["NT"], tl["L"], tl["segs"]
            xs = x_sb[ti % 2]
            nch = NT // L
            zsel = 0 if L == 128 else 1
            if tl["kind"] == "p":
                ch_slot = [tl["slot"]] * nch
                ch_seq = [segs[0][0]] * nch
            else:
                ch_slot = list(range(nch))
                ch_seq = [segs[i][0] for i in range(nch)]
            par = ti % 2
            OSG, SGG = osg[par], sgg[par]

            def h_rhs(kc):
                return hT.t[:, kc, 0:NT], hT.rk[kc]


            PEX = "dve" if ti == 0 else "pool"

            def proj_v(vdst, g):
                si = next_w(g)
                for c in range(nch):
                    b = next_pj()
                    for kc in range(KC):
                        mm(b.t[0:L, 0:512], hT.t[:, kc, c * L:(c + 1) * L], wsl[si].t[:, kc * 512:(kc + 1) * 512],
                           kc == 0, kc == KC - 1, [hT.rk[kc], wsl[si].r], [b.r])
                    act(vdst.t[0:L, c, :], b.t[0:L, 0:512], AF.Copy, [b.r], [vdst.r])

            def gen_gates(gdst, g):
                si = next_w(g)
                for h in range(4):
                    b = proj_fm(si, (h * 128, 512, 128), NT, None, rhs_fn=h_rhs)
                    act(gdst.t[:, h, 0:NT], b.t[:, 0:NT], AF.Silu, [b.r], [gdst.r])
                    yield

            for gi, dst, vdst, gv in ((1, qrot, vtm, 3), (2, krot, vtm2, 6)):
                si = next_w(gi)
                for h in range(4):
                    b = proj_fm(si, (h * 128, 512, 128), NT, None, rhs_fn=h_rhs)
                    t1, t2 = t1r[h % 2], t2r[h % 2]
                    tt("dve", t1.t[:, 0:NT], b.t[:, 0:NT], rC[0].t[:, 0:NT], ALU.mult, [b.r, rC[0].r], [t1.r])
                    tt("dve", t2.t[0:64, 0:NT], b.t[64:128, 0:NT], rS[0].t[64:128, 0:NT], ALU.mult,
                       [b.r, rS[0].r], [t2.r])
                    tt("dve", t2.t[64:128, 0:NT], b.t[0:64, 0:NT], rS[0].t[0:64, 0:NT], ALU.mult,
                       [b.r, rS[0].r], [t2.r])
                    tt(PEX, dst.t[:, h, 0:NT], t1.t[:, 0:NT], t2.t[:, 0:NT], ALU.add, [t1.r, t2.r], [dst.r])
                proj_v(vdst, gv)
            if ti + 1 < n_tiles:
                load_rot(ti + 1)
            tt(PEX, qxi.t[:, :, 0:NT].rearrange("p h (c l) -> p h c l", l=L),
               qrot.t[:, :, 0:NT].rearrange("p h (c l) -> p h c l", l=L),
               c_xi.t[:, :, 0:L].unsqueeze(2).to_broadcast([128, 4, nch, L]), ALU.mult, [qrot.r, c_xi.r], [qxi.r])
            stage("t%d_retproj" % ti)

            si = next_w(5)
            for p in range(2):
                b = proj_fm(si, (p * 128, 512, 128), NT, None, rhs_fn=h_rhs)
                for two in range(2):
                    rows = slice(two * 64, two * 64 + 64)
                    stt("dve", qg.t[0:64, 2 * p + two, 0:NT], b.t[rows, 0:NT], 0.125, Aex.t[rows, p, 0:NT],
                        ALU.mult, ALU.mult, [b.r, Aex.r], [qg.r])
            for p in range(2):
                b = proj_fm(si, (256 + p * 128, 512, 128), NT, None, rhs_fn=h_rhs)
                for two in range(2):
                    rows = slice(two * 64, two * 64 + 64)
                    tt("dve", kg.t[0:64, 2 * p + two, 0:NT], b.t[rows, 0:NT], Ain.t[rows, p, 0:NT], ALU.mult,
                       [b.r, Ain.r], [kg.r])
                act(kraw.t[:, p, 0:NT], b.t[:, 0:NT], AF.Copy, [b.r], [kraw.r])
            stage("t%d_g5" % ti)

            def gen_passA():
                for c in range(nch):
                    cs = slice(c * L, (c + 1) * L)
                    bt = next_bank()
                    for h in range(4):
                        trp(bf16v(bt)[0:L, h * 128:(h + 1) * 128], krot.t[:, h, cs], [krot.r], [bt.r])
                    tt("dve", kztm.t[0:L, c, :].rearrange("p (h d) -> p h d", h=4),
                       bf16v(bt)[0:L, 0:512].rearrange("p (h d) -> p h d", h=4),
                       c_zeta.t[0:L, zsel, :].unsqueeze(2).to_broadcast([L, 4, 128]), ALU.mult, [bt.r, c_zeta.r], [kztm.r])
                    bs = next_bank()
                    for h in range(4):
                        mm(f32v(bs)[0:L, h * 128:h * 128 + L], krot.t[:, h, cs], qrot.t[:, h, cs], True, True,
                           [krot.r, qrot.r], [bs.r])
                    tt("dve", stmr[c].t[0:L, :, 0:L], f32v(bs)[0:L, :].rearrange("p (h t) -> p h t", h=4)[:, :, 0:L],
                       c_maskR.t[0:L, :, 0:L], ALU.mult, [bs.r, c_maskR.r], [stmr[c].r])
                    bt = next_bank()
                    for p in range(2):
                        trp(bf16v(bt)[0:L, p * 128:(p + 1) * 128], kraw.t[:, p, cs], [kraw.r], [bt.r])
                    tt("dve", k2tm.t[0:L, c, :], bf16v(bt)[0:L, 0:256], E2.t[0:L, c, :], ALU.mult, [bt.r, E2.r], [k2tm.r])
                    bs = next_bank()
                    for h in range(4):
                        mm(f32v(bs)[0:L, h * 128:h * 128 + L], kg.t[0:64, h, cs], qg.t[0:64, h, cs], True, True,
                           [kg.r, qg.r], [bs.r])
                    tt("dve", stmg[c].t[0:L, :, 0:L], f32v(bs)[0:L, :].rearrange("p (h t) -> p h t", h=4)[:, :, 0:L],
                       c_mask01.t[0:L, :, 0:L], ALU.mult, [bs.r, c_mask01.r], [stmg[c].r])
                    yield

            def interleave(a, b, ratio):
                for _ in a:
                    for _ in range(ratio):
                        next(b, None)
                for _ in b:
                    pass

            interleave(gen_passA(), gen_gates(sg, 4), 1)
            stage("t%d_passA" % ti)

            class _SV:
                def __init__(self, t, ap):
                    self.t = ap
                    self.r = t.r

            Rl = list(Rst) + [_SV(t, t.t[:, :].rearrange("p (h e) -> p h e", h=4)) for t in tmpr]
            Gl = list(Gst) + [_SV(t, t.t[0:64, :].rearrange("p (h e) -> p h e", h=4)) for t in t2r]

            def gen_passB():
                for c in range(nch):
                    cs = slice(c * L, (c + 1) * L)
                    sl = ch_slot[c]
                    R, G = Rl[sl], Gl[sl]
                    if tl["kind"] == "s":
                        sid = ch_seq[c] - NPS
                        P.dma("pool", R.t[:], st_ret_in[sid].rearrange("h d e -> d h e"), writes=[R.r], key="rin%d" % sl)
                        P.dma("pool", G.t[:], st_gla_in[sid].rearrange("h d e -> d h e"), writes=[G.r], key="gin%d" % sl)
                    act(Rbf.t[:], R.t[:], AF.Copy, [R.r], [Rbf.r])
                    cp("dve", Gbf.t[:], G.t[:], [G.r], [Gbf.r])
                    bu_ = next_bank()
                    for h in range(4):
                        mm(f32v(bu_)[:, h * 128:(h + 1) * 128], kztm.t[0:L, c, h * 128:(h + 1) * 128],
                           vtm.t[0:L, c, h * 128:(h + 1) * 128], True, True, [kztm.r, vtm.r], [bu_.r])
                    for h in range(4):
                        stt("dve", R.t[:, h, :], R.t[:, h, :], consts["gL"][L][h], f32v(bu_)[:, h * 128:(h + 1) * 128],
                            ALU.mult, ALU.add, [R.r, bu_.r], [R.r])
                    bu_ = next_bank()
                    for h in range(4):
                        mm(f32v(bu_)[0:64, h * 128:(h + 1) * 128], k2tm.t[0:L, c, h * 64:(h + 1) * 64],
                           vtm2.t[0:L, c, h * 128:(h + 1) * 128], True, True, [k2tm.r, vtm2.r], [bu_.r])
                    lastc = c * L + L - 1
                    for h in range(4):
                        p, two = h // 2, h % 2
                        rows = slice(two * 64, two * 64 + 64)
                        cp("dve", eLt.t[0:64, h, c:c + 1], Aex.t[rows, p, lastc:lastc + 1], [Aex.r], [eLt.r])
                    for h in range(4):
                        stt("dve", G.t[0:64, h, :], G.t[0:64, h, :], eLt.t[0:64, h, c:c + 1],
                            f32v(bu_)[0:64, h * 128:(h + 1) * 128], ALU.mult, ALU.add, [G.r, eLt.r, bu_.r], [G.r])
                    bo = next_bank()
                    for h in range(4):
                        mm(f32v(bo)[:, h * 128:h * 128 + L], vtm.t[0:L, c, h * 128:(h + 1) * 128], stmr[c].t[0:L, h, 0:L],
                           True, False, [vtm.r, stmr[c].r], [bo.r])
                        mm(f32v(bo)[:, h * 128:h * 128 + L], Rbf.t[:, h, :], qxi.t[:, h, cs], False, True,
                           [Rbf.r, qxi.r], [bo.r])
                    act(osb.t[:, :, cs], f32v(bo).rearrange("p (h t) -> p h t", h=4)[:, :, 0:L], AF.Copy, [bo.r], [osb.r])
                    bo = next_bank()
                    for h in range(4):
                        mm(f32v(bo)[:, h * 128:h * 128 + L], vtm2.t[0:L, c, h * 128:(h + 1) * 128], stmg[c].t[0:L, h, 0:L],
                           True, False, [vtm2.r, stmg[c].r], [bo.r])
                        mm(f32v(bo)[:, h * 128:h * 128 + L], Gbf.t[0:64, h, :], qg.t[0:64, h, cs], False, True,
                           [Gbf.r, qg.r], [bo.r])
                    act(OSG.t[:, :, cs], f32v(bo).rearrange("p (h t) -> p h t", h=4)[:, :, 0:L], AF.Copy, [bo.r], [OSG.r])
                    if tl["kind"] == "s" or (tl["last"] and c == nch - 1):
                        seq = ch_seq[c]
                        out_keys.add("oret%d" % sl)
                        out_keys.add("ogla%d" % sl)
                        P.dma("pool", o_ret[seq].rearrange("h d e -> d h e"), R.t[:], reads=[R.r], key="oret%d" % sl)
                        P.dma("pool", o_gla[seq].rearrange("h d e -> d h e"), G.t[:], reads=[G.r], key="ogla%d" % sl)
                    yield

            interleave(gen_passB(), gen_gates(SGG, 7), 1)
            stage("t%d_passB" % ti)


            for h in range(4):
                st = next_bank()
                mm(f32v(st)[:, 0:NT], c_onesf.t[:], osb.t[:, h, 0:NT], True, True, [c_onesf.r, osb.r], [st.r])
                stt("dve", osb.t[:, h, 0:NT], f32v(st)[:, 0:NT], -1.0 / 128, osb.t[:, h, 0:NT], ALU.mult, ALU.add,
                    [st.r, osb.r], [osb.r])
                sqa = sqr[h % 2]
                act(sqa.t[:, 0:NT], osb.t[:, h, 0:NT], AF.Square, [osb.r], [sqa.r])
                st2 = next_bank()
                mm(f32v(st2)[:, 0:NT], c_onesb.t[:], sqa.t[:, 0:NT], True, True, [c_onesb.r, sqa.r], [st2.r])
                rb = rstd_from(st2, NT, 1.0 / 128)
                tm = tmpr[h % 2]
                stt("dve", tm.t[:, 0:NT], osb.t[:, h, 0:NT], c_ghead.t[:, 0, h:h + 1], rb.t[:, 0:NT], ALU.mult, ALU.mult,
                    [osb.r, c_ghead.r, rb.r], [tm.r])
                tt(PEX, mixT[h].t[:, 0:NT], tm.t[:, 0:NT], sg.t[:, h, 0:NT], ALU.mult, [tm.r, sg.r], [mixT[h].r])
                sqa = sqr[2 + h % 2]
                act(sqa.t[:, 0:NT], OSG.t[:, h, 0:NT], AF.Square, [OSG.r], [sqa.r])
                st = next_bank()
                mm(f32v(st)[:, 0:NT], c_onesb.t[:], sqa.t[:, 0:NT], True, True, [c_onesb.r, sqa.r], [st.r])
                rb = rstd_from(st, NT, 1.0 / 128)
                tm = t2r[h % 2]
                stt("dve", tm.t[:, 0:NT], OSG.t[:, h, 0:NT], c_ghead.t[:, 1, h:h + 1], rb.t[:, 0:NT], ALU.mult, ALU.mult,
                    [OSG.r, c_ghead.r, rb.r], [tm.r])
                tt(PEX, mixT[4 + h].t[:, 0:NT], tm.t[:, 0:NT], SGG.t[:, h, 0:NT], ALU.mult, [tm.r, SGG.r],
                   [mixT[4 + h].r])
            stage("t%d_gla" % ti)
            if ti + 1 < n_tiles:
                load_x(ti + 1)

            def mix_rhs(kc):
                return mixT[kc].t[:, 0:NT], mixT[kc].r

            for g2 in range(2):
                si = next_w(8 + g2)
                for m in range(4):
                    j = g2 * 4 + m
                    b = proj_fm(si, (m * 128, 512, 128), NT, None, rhs_fn=mix_rhs)
                    for (sid, c0, n) in segs:
                        stt("dve", xs.t[:, j, c0:c0 + n], b.t[:, c0:c0 + n], modT.t[:, G1 + j, sid:sid + 1],
                            xs.t[:, j, c0:c0 + n], ALU.mult, ALU.add, [b.r, modT.r, xs.rk[j]], [xs.rk[j]])

            stage("t%d_wout" % ti)
            norm(ti, xs, NT, segs, gm2, SH2, lambda kc: (hT.t[:, kc, :], hT.rk[kc]))
            nseg = len(segs)
            Ls = NT // nseg
            for g in range(11):
                if g in (4, 7, 10) and pre_last is not None:
                    pre_last()
                si = next_w(10 + g)
                banks = []
                for m in range(4):
                    banks.append(proj_fm(si, ((m // 2) * 2048 + (m % 2) * 128, 256, 128), NT, None, rhs_fn=h_rhs))
                for q in range(2):
                    j = 2 * g + q
                    ba, bu = banks[q], banks[2 + q]
                    B = bufr[j % 2]
                    Bv = B.t[:, 0:nseg * (Ls + 2)].rearrange("p (s l) -> p s l", s=nseg)
                    C0 = c0r[j % 2]
                    C0v = C0.t[:, 0:NT].rearrange("p (s l) -> p s l", s=nseg)
                    cp(PEX, Bv[:, :, 0:2], hist.t[:, j, 0:nseg, :], [hist.r], [B.r])
                    act(Bv[:, :, 2:Ls + 2], ba.t[:, 0:NT].rearrange("p (s l) -> p s l", s=nseg), AF.Copy, [ba.r], [B.r])
                    act(C0v, Bv[:, :, 0:Ls], AF.Identity, [B.r, c_conv.r], [C0.r],
                        bias=c_conv.t[:, j, 3:4], scale=c_conv.t[:, j, 0:1])
                    stt("dve", C0v, Bv[:, :, 1:Ls + 1], c_conv.t[:, j, 1:2], C0v, ALU.mult, ALU.add,
                        [B.r, c_conv.r, C0.r], [C0.r])
                    stt("dve", C0v, Bv[:, :, 2:Ls + 2], c_conv.t[:, j, 2:3], C0v, ALU.mult, ALU.add,
                        [B.r, c_conv.r, C0.r], [C0.r])
                    cp(PEX, hist.t[:, j, 0:nseg, :], Bv[:, :, Ls:Ls + 2], [B.r], [hist.r])
                    act(C0.t[:, 0:NT], C0.t[:, 0:NT], AF.Silu, [C0.r], [C0.r])
                    tt("dve", yTt[j].t[:, 0:NT], C0.t[:, 0:NT], bu.t[:, 0:NT], ALU.mult, [C0.r, bu.r], [yTt[j].r])
            if tl["last"]:
                if tl["kind"] == "p":
                    s0 = segs[0][0]
                    out_keys.add("occ")
                    P.dma("pool", o_cc[:, :, s0:s0 + 1, :], hist.t[:, :, 0:1, :], reads=[hist.r], key="occ")
                else:
                    out_keys.add("occ")
                    P.dma("pool", o_cc[:, :, NPS:NPS + NSS, :], hist.t[:], reads=[hist.r], key="occ")

            stage("t%d_ffn1" % ti)

        def t_back(ti):
            tl = tiles[ti]
            NT, L, segs = tl["NT"], tl["L"], tl["segs"]
            xs = x_sb[ti % 2]
            nch = NT // L
            zsel = 0 if L == 128 else 1
            if tl["kind"] == "p":
                ch_slot = [tl["slot"]] * nch
                ch_seq = [segs[0][0]] * nch
            else:
                ch_slot = list(range(nch))
                ch_seq = [segs[i][0] for i in range(nch)]
            par = ti % 2
            OSG, SGG = osg[par], sgg[par]

            def h_rhs(kc):
                return hT.t[:, kc, 0:NT], hT.rk[kc]

            def y_rhs(kc):
                return yTt[kc].t[:, 0:NT], yTt[kc].r

            def resid(j, b):
                for (sid, c0, n) in segs:
                    stt("dve", xs.t[:, j, c0:c0 + n], b.t[:, c0:c0 + n], modT.t[:, G2 + j, sid:sid + 1],
                        xs.t[:, j, c0:c0 + n], ALU.mult, ALU.add, [b.r, modT.r, xs.rk[j]], [xs.rk[j]])

            s0_, s1_ = next_w(21), next_w(22, la=NSLOT - 2)
            b0_, b1_ = next_pj(), next_pj()
            KSPLIT = 16
            for (lo, hi) in ((0, KSPLIT), (KSPLIT, FC)):
                for (si, b) in ((s0_, b0_), (s1_, b1_)):
                    for kc in range(lo, hi):
                        mm(b.t[:, 0:NT], wsl[si].t[:, kc * 128:(kc + 1) * 128], yTt[kc].t[:, 0:NT],
                           kc == 0, kc == FC - 1, [wsl[si].r, yTt[kc].r], [b.r])
            resid(0, b0_)
            yield
            resid(1, b1_)
            yield
            for j in range(2, KC):
                si = next_w(21 + j)
                b = proj_fm(si, (0, 128, 128), NT, None, nk=FC, rhs_fn=y_rhs)
                resid(j, b)
                yield

            stage("t%d_ffn2" % ti)
            norm(ti, xs, NT, segs, None, None, lambda kc: (xs.t[:, kc, 0:NT], xs.rk[kc]), final=True)
            k = "y%d" % (ti % 2)
            out_keys.add(k)
            c0 = tl["col0"]
            P.dma("pool", yT[:, :, c0:c0 + NT].rearrange("k p t -> p k t"), xs.t[:, :, 0:NT], reads=xs.rk, key=k)
            stage("t%d_end" % ti)

        for _ in t_front(0):
            pass
        for ti in range(n_tiles):
            fg = t_front(ti + 1) if ti + 1 < n_tiles else iter(())
            t_mid(ti, pre_last=lambda: next(fg, None))
            bg = t_back(ti)
            plan = {2: 1, 3: 1, 5: 1}
            for j in range(KC):
                next(bg)
                for _ in range(plan.get(j, 0)):
                    next(fg, None)
            for _ in fg:
                pass
            for _ in bg:
                pass

    try:
        body()
    except _Stop:
        pass
    P.emit("pool", sorted(out_keys))
    print("[kernel] sbuf bytes remaining/partition:", nc.sbuf_bytes_remaining, flush=True)
    es.close()
    return nc


_CACHE = {}


def _prep(x_prompt, x_sample, c_prompt, c_sample, state_ret, state_gla, cache_ffn_conv,
          w_ada, b_ada, g_norm_mix, w_in, w_a2, b_a2, g_ret_norm, g_gla_norm, w_out,
          g_norm_ffn, w_ffn_in, conv_w, conv_b, w_ffn_out, g_final, cores=None):
    f = np.float32
    A = lambda a: np.ascontiguousarray(np.asarray(a, dtype=f))
    if "c" not in _CACHE:
        _CACHE["c"] = _consts()
    consts = _CACHE["c"]
    x_prompt, x_sample = A(x_prompt), A(x_sample)
    c_prompt, c_sample = A(c_prompt), A(c_sample)
    shared = {
        "w_ada": A(w_ada[0]),
        "b_ada": A(A(b_ada[0]).reshape(48, 128).T),
        "gvec": A(np.stack([A(g_norm_mix[0]).reshape(KC, 128).T, A(g_norm_ffn[0]).reshape(KC, 128).T,
                            A(g_final).reshape(KC, 128).T], 1)),
        "ghead": A(np.stack([A(g_ret_norm[0]).reshape(4, 128).T, A(g_gla_norm[0]).reshape(4, 128).T], 1)),
        "w_in": A(w_in[0]),
        "w_a2b": A(np.concatenate([A(w_a2[0]), A(b_a2[0])[None, :]], 0)),
        "w_out": A(w_out[0]),
        "w_f1": A(w_ffn_in[0]),
        "convp": A(np.concatenate([A(conv_w[0]), A(conv_b[0])[None, :]], 0).reshape(4, FC, 128).transpose(2, 1, 0)),
        "w_f2": A(w_ffn_out[0]),
        "rotC": consts["rotC"], "rotS": consts["rotS"], "maskR": consts["maskR"], "mask01": consts["mask01"],
        "triN": consts["triN"], "utriN": consts["utriN"], "xi": consts["xi"], "zeta": consts["zeta"],
        "ident": consts["ident"], "ones_b": consts["ones_b"], "ones_f": consts["ones_f"],
    }
    in_maps = []
    for c in (range(NCORE) if cores is None else cores):
        xp = x_prompt[c * NPS:(c + 1) * NPS]
        xsm = x_sample[c * NSS:(c + 1) * NSS]
        cols = np.concatenate([xp.reshape(NPS * SEQ, D), xsm.reshape(NSS * DSEQ, D)], 0)
        xTc = A(cols.T.reshape(KC, 128, TOK))
        cc = np.concatenate([c_prompt[c * NPS:(c + 1) * NPS], c_sample[c * NSS:(c + 1) * NSS]], 0)
        cTc = A(cc.T.reshape(KC, 128, NSEQ).transpose(1, 0, 2))
        cci = A(cache_ffn_conv[0, c * NSS:(c + 1) * NSS])
        cci = A(cci.transpose(2, 0, 1).reshape(FC, 128, NSS, 2).transpose(1, 0, 2, 3))
        m = dict(shared)
        m.update({"xT": xTc, "cT": cTc, "st_ret": A(state_ret[0, c * NSS:(c + 1) * NSS]),
                  "st_gla": A(state_gla[0, c * NSS:(c + 1) * NSS]), "cc_in": cci})
        in_maps.append(m)
    return in_maps


def _gather(results, ncores=NCORE):
    f = np.float32
    B, DB = ncores * NPS, ncores * NSS
    y_p = np.empty((B, SEQ, D), f)
    y_s = np.empty((DB, DSEQ, D), f)
    r_p = np.empty((1, B, 4, 128, 128), f)
    g_p = np.empty((1, B, 4, 64, 128), f)
    c_p = np.empty((1, B, 2, DFF), f)
    r_s = np.empty((1, DB, 4, 128, 128), f)
    g_s = np.empty((1, DB, 4, 64, 128), f)
    c_s = np.empty((1, DB, 2, DFF), f)
    for c in range(ncores):
        r = results[c]
        yc = np.asarray(r["yT"]).reshape(D, TOK).T
        y_p[c * NPS:(c + 1) * NPS] = yc[:NPS * SEQ].reshape(NPS, SEQ, D)
        y_s[c * NSS:(c + 1) * NSS] = yc[NPS * SEQ:].reshape(NSS, DSEQ, D)
        orr, og = np.asarray(r["o_ret"]), np.asarray(r["o_gla"])
        r_p[0, c * NPS:(c + 1) * NPS] = orr[:NPS]
        r_s[0, c * NSS:(c + 1) * NSS] = orr[NPS:]
        g_p[0, c * NPS:(c + 1) * NPS] = og[:NPS]
        g_s[0, c * NSS:(c + 1) * NSS] = og[NPS:]
        occ = np.asarray(r["o_cc"])
        occ = occ.transpose(2, 3, 1, 0).reshape(NSEQ, 2, DFF)
        c_p[0, c * NPS:(c + 1) * NPS] = occ[:NPS]
        c_s[0, c * NSS:(c + 1) * NSS] = occ[NPS:]
    return (y_p, y_s, r_p, g_p, c_p, r_s, g_s, c_s)


def kernel(**inputs):
    in_maps = _prep(**inputs)
    if "nc" not in _CACHE:
        _CACHE["nc"] = build_nc(_CACHE["c"])
    res = run_bass_kernel_spmd(_CACHE["nc"], in_maps, core_ids=list(range(NCORE)))
    return _gather(res.results)
```

```python
import contextlib
import numpy as np
import ml_dtypes
import concourse.bass as bass
import concourse.mybir as mybir
from concourse.bass_utils import run_bass_kernel_spmd

F32 = mybir.dt.float32
BF16 = mybir.dt.bfloat16
AF = mybir.ActivationFunctionType
ALU = mybir.AluOpType

D = 1024
KC = 8
DFF = 2816
FC = 22
DIN = 3600
NCORE = 8
NPS = 2
NSS = 4
NSEQ = NPS + NSS
SEQ = 2048
DSEQ = 64
PAST = 1024
NTP = 512
TOK = NPS * SEQ + NSS * DSEQ
EPS = 1e-6
NSLOT = 4
COMPUTE = ("pe", "act", "dve", "pool")


class Res:
    __slots__ = ("name", "last_w", "readers", "al", "psum")

    def __init__(self, name, psum=False):
        self.name = name
        self.last_w = None
        self.readers = []
        self.al = (self,)
        self.psum = psum


def alias(a_list, b_list):
    for a in [x.r for x in a_list]:
        for b in [x.r for x in b_list]:
            if b not in a.al:
                a.al = a.al + (b,)
            if a not in b.al:
                b.al = b.al + (a,)


class Op:
    __slots__ = ("eng", "fn", "reads", "writes", "deps", "sig", "cnt", "key", "is_dma")

    def __init__(self, eng, fn, reads, writes, key=None):
        self.eng = eng
        self.fn = fn
        self.reads = reads
        self.writes = writes
        self.deps = ()
        self.sig = False
        self.cnt = 0
        self.key = key
        self.is_dma = key is not None


class Prog:
    def __init__(self, nc):
        self.nc = nc
        self.ops = []

    def op(self, eng, fn, reads=(), writes=()):
        o = Op(eng, fn, tuple(reads), tuple(writes))
        self.ops.append(o)
        return o

    def dma(self, queue, out, in_, reads=(), writes=(), key=None):
        o = Op(queue, (out, in_), tuple(reads), tuple(writes), key=key)
        self.ops.append(o)
        return o

    def _analyse(self):
        for o in self.ops:
            raw = set()
            oth = set()
            for r0 in o.reads:
                for r in r0.al:
                    if r.last_w is not None:
                        raw.add(r.last_w)
                    if r.psum:
                        for rd in r.readers:
                            if rd.eng != o.eng:
                                raw.add(rd)
            for w0 in o.writes:
                for w in w0.al:
                    if w.last_w is not None:
                        oth.add(w.last_w)
                    for rd in w.readers:
                        oth.add(rd)
            raw.discard(o)
            oth.discard(o)
            need = []
            for d in raw | oth:
                if (not d.is_dma) and (not o.is_dma) and d.eng == o.eng:
                    if o.eng == "pe":
                        continue
                need.append(d)
            o.deps = need
            for d in need:
                d.sig = True
            for r in o.reads:
                r.readers.append(o)
            for w in o.writes:
                w.last_w = o
                w.readers = []

    def emit(self, final_eng, final_keys):
        nc = self.nc
        self._analyse()
        cnt = {e: 0 for e in COMPUTE}
        kcnt = {}
        for o in self.ops:
            if o.is_dma:
                kcnt[o.key] = kcnt.get(o.key, 0) + 1
                o.cnt = 16 * kcnt[o.key]
            elif o.sig:
                cnt[o.eng] += 1
                o.cnt = cnt[o.eng]
        stack = contextlib.ExitStack()
        sems = {e: stack.enter_context(nc.semaphore("s_" + e)) for e in COMPUTE}
        ksems = {k: stack.enter_context(nc.semaphore("d_" + str(k))) for k in kcnt}
        per_eng = {}
        for o in self.ops:
            per_eng.setdefault(o.eng, []).append(o)

        def run(eng_name, eng):
            waited = {}
            for o in per_eng.get(eng_name, []):
                want = {}
                for d in o.deps:
                    s = ("k", d.key) if d.is_dma else ("e", d.eng)
                    if d.cnt > want.get(s, 0):
                        want[s] = d.cnt
                for s, v in want.items():
                    if waited.get(s, 0) >= v:
                        continue
                    waited[s] = v
                    eng.wait_ge(ksems[s[1]] if s[0] == "k" else sems[s[1]], v)
                if o.is_dma:
                    out, in_ = o.fn
                    eng.dma_start(out=out, in_=in_).then_inc(ksems[o.key], 16)
                else:
                    ins = o.fn(eng)
                    if o.sig:
                        ins.then_inc(sems[o.eng], 1)
            if eng_name == final_eng:
                for k in final_keys:
                    if k in kcnt:
                        eng.wait_ge(ksems[k], 16 * kcnt[k])

        with nc.Block() as block:
            @block.tensor
            def _(e):
                run("pe", e)

            @block.scalar
            def _(e):
                run("act", e)

            @block.vector
            def _(e):
                run("dve", e)

            @block.gpsimd
            def _(e):
                run("pool", e)

            @block.sync
            def _(e):
                run("sp", e)
        stack.close()


def _consts():
    c = {}
    pos = np.concatenate([np.arange(SEQ)] * NPS + [PAST + np.arange(DSEQ)] * NSS).astype(np.float32)
    inv = (10000.0 ** (-np.arange(0, 128, 2, dtype=np.float32) / 128.0)).astype(np.float32)
    ang = pos[None, :] * inv[:, None]
    cos = np.cos(ang).astype(np.float32)
    sin = np.sin(ang).astype(np.float32)
    c["rotC"] = np.ascontiguousarray(np.concatenate([cos, cos], 0))
    c["rotS"] = np.ascontiguousarray(np.concatenate([sin, -sin], 0))
    h = np.arange(4, dtype=np.float64)
    log_g = np.log1p(-np.power(2.0, -5.0 - h))
    i = np.arange(128, dtype=np.float64)
    diff = i[None, :] - i[:, None]
    m = np.where(diff[None] >= 0, np.exp(log_g[:, None, None] * np.maximum(diff[None], 0)), 0.0)
    c["maskR"] = np.ascontiguousarray((128 ** -0.5 * m).transpose(1, 0, 2)).astype(np.float32)
    c["mask01"] = np.ascontiguousarray(np.broadcast_to((diff >= 0)[:, None, :], (128, 4, 128))).astype(np.float32)
    c["triN"] = ((diff >= 0) * (-1.0 / 16.0)).astype(np.float32)
    c["utriN"] = ((diff.T > 0) * (-1.0 / 16.0)).astype(np.float32)
    xi = np.exp(log_g[:, None] * (i[None, :] + 1.0))
    c["xi"] = np.ascontiguousarray(np.broadcast_to(xi[None], (128, 4, 128))).astype(np.float32)
    z128 = 128 ** -0.5 * np.exp(log_g[None, :] * (127.0 - i[:, None]))
    z64 = np.zeros((128, 4))
    z64[:64] = 128 ** -0.5 * np.exp(log_g[None, :] * (63.0 - i[:64, None]))
    c["zeta"] = np.ascontiguousarray(np.stack([z128, z64], 1)).astype(np.float32)
    c["gL"] = {128: [float(np.exp(log_g[k] * 128)) for k in range(4)],
               64: [float(np.exp(log_g[k] * 64)) for k in range(4)]}
    c["ident"] = np.eye(128, dtype=np.float32).astype(ml_dtypes.bfloat16)
    c["ones_b"] = np.ones((128, 128), dtype=np.float32).astype(ml_dtypes.bfloat16)
    c["ones_f"] = np.ones((128, 128), dtype=np.float32)
    return c


class _Stop(Exception):
    pass


def build_nc(consts, stop_at=None):
    nc = bass.Bass("TRN2", target_bir_lowering=False)

    def stage(name):
        if stop_at is not None and name == stop_at:
            raise _Stop()

    P = Prog(nc)
    es = contextlib.ExitStack()

    def dram(name, shape, dt, kind="ExternalInput"):
        return nc.dram_tensor(name, list(shape), dt, kind=kind).ap()

    xT = dram("xT", [KC, 128, TOK], F32)
    cT = dram("cT", [128, KC, NSEQ], F32)
    st_ret_in = dram("st_ret", [NSS, 4, 128, 128], F32)
    st_gla_in = dram("st_gla", [NSS, 4, 64, 128], F32)
    cc_in = dram("cc_in", [128, FC, NSS, 2], F32)
    w_ada = dram("w_ada", [D, 6 * D], F32)
    b_ada = dram("b_ada", [128, 48], F32)
    gvec = dram("gvec", [128, 3, KC], F32)
    ghead = dram("ghead", [128, 2, 4], F32)
    w_in = dram("w_in", [D, DIN], F32)
    w_a2b = dram("w_a2b", [17, 256], F32)
    w_out = dram("w_out", [D, D], F32)
    w_f1 = dram("w_f1", [D, 2 * DFF], F32)
    convp = dram("convp", [128, FC, 4], F32)
    w_f2 = dram("w_f2", [DFF, D], F32)
    rotC_d = dram("rotC", [128, TOK], F32)
    rotS_d = dram("rotS", [128, TOK], F32)
    maskR_d = dram("maskR", [128, 4, 128], F32)
    mask01_d = dram("mask01", [128, 4, 128], F32)
    triN_d = dram("triN", [128, 128], F32)
    utriN_d = dram("utriN", [128, 128], F32)
    xi_d = dram("xi", [128, 4, 128], F32)
    zeta_d = dram("zeta", [128, 2, 4], F32)
    ident_d = dram("ident", [128, 128], BF16)
    onesb_d = dram("ones_b", [128, 128], BF16)
    onesf_d = dram("ones_f", [128, 128], F32)

    yT = dram("yT", [KC, 128, TOK], F32, "ExternalOutput")
    o_ret = dram("o_ret", [NSEQ, 4, 128, 128], F32, "ExternalOutput")
    o_gla = dram("o_gla", [NSEQ, 4, 64, 128], F32, "ExternalOutput")
    o_cc = dram("o_cc", [128, FC, NSEQ, 2], F32, "ExternalOutput")

    NG = 29
    scratch = dram("wscr", [NG, 128, 4096], BF16, "Internal")
    scr_res = [Res("scr%d" % g) for g in range(NG)]

    class T:
        rk = None

        def __init__(self, name, shape, dt, psum=False, res=None):
            if psum:
                self.t = es.enter_context(nc.psum_tensor(name, list(shape), dt))
            else:
                self.t = es.enter_context(nc.sbuf_tensor(name, list(shape), dt))
            self.r = res if res is not None else Res(name, psum)

    def ring(name, n, shape, dt, psum=False):
        return [T("%s%d" % (name, i), shape, dt, psum) for i in range(n)]

    PJ = ring("pj", 4, [128, 512], F32, True)
    SC = T("sc", [128, 512], F32, True)
    OC = T("oc", [128, 512], F32, True)
    MS = T("ms", [128, 512], F32, True)
    TR = T("tr", [128, 512], F32, True)
    STAT = [MS, SC, OC]

    c_maskR = T("c_maskR", [128, 4, 128], F32)
    c_mask01 = T("c_mask01", [128, 4, 128], F32)
    c_tri = T("c_tri", [128, 128], F32)
    c_utri = T("c_utri", [128, 128], F32)
    c_xi = T("c_xi", [128, 4, 128], F32)
    c_zeta = T("c_zeta", [128, 2, 4], F32)
    c_ident = T("c_ident", [128, 128], BF16)
    c_onesb = T("c_onesb", [128, 128], BF16)
    c_onesf = T("c_onesf", [128, 128], F32)
    c_bada = T("c_bada", [128, 48], F32)
    c_gvec = T("c_gvec", [128, 3, KC], F32)
    c_ghead = T("c_ghead", [128, 2, 4], F32)
    c_conv = T("c_conv", [128, FC, 4], F32)
    c_wa2 = T("c_wa2", [17, 256], F32)
    c_cT = T("c_cT", [128, KC, NSEQ], F32)
    c_scT = T("c_scT", [128, KC, NSEQ], BF16)
    modT = T("modT", [128, 48, NSEQ], F32)
    gm1 = T("gm1", [128, KC, NSEQ], F32)
    gm2 = T("gm2", [128, KC, NSEQ], F32)

    cq = 0

    def cload(dst, src):
        nonlocal cq
        P.dma("sp", dst.t[:], src, writes=[dst.r], key="c%d" % cq)
        cq += 1

    for dst, src in ((c_maskR, maskR_d), (c_mask01, mask01_d), (c_tri, triN_d), (c_utri, utriN_d),
                     (c_xi, xi_d), (c_zeta, zeta_d), (c_ident, ident_d), (c_onesb, onesb_d),
                     (c_onesf, onesf_d), (c_bada, b_ada), (c_gvec, gvec), (c_ghead, ghead),
                     (c_conv, convp), (c_wa2, w_a2b), (c_cT, cT)):
        cload(dst, src)

    x_sb = ring("x", 2, [128, KC, NTP], F32)
    for xb in x_sb:
        xb.rk = [Res(xb.r.name + "_k%d" % k) for k in range(KC)]
    rC = ring("rC", 1, [128, NTP], F32)
    rS = ring("rS", 1, [128, NTP], F32)
    hT = T("hT", [128, KC, NTP], BF16)
    hT.rk = [Res("hT_k%d" % k) for k in range(KC)]
    sqr = ring("sq", 4, [128, NTP], BF16)
    c_eps = T("c_eps", [128, 1], F32)
    tmpr = ring("tmp", 2, [128, NTP], F32)
    rsdr = ring("rsd", 2, [128, NTP], F32)
    rsbr = ring("rsb", 2, [128, NTP], F32)
    rsd, rsb = rsdr[0], rsbr[0]
    wsl = ring("wsl", NSLOT, [128, 4096], BF16)
    ga1 = T("ga1", [17, NTP], F32)
    lsp = T("lsp", [128, 4, 256], F32)
    Aex = T("Aex", [128, 2, NTP], F32)
    Ain = T("Ain", [128, 2, NTP], F32)
    E2 = T("E2", [128, 4, 256], F32)
    qg = T("qg", [64, 4, NTP], BF16)
    kg = T("kg", [64, 4, NTP], BF16)
    eLt = T("eLt", [64, 4, 4], F32)
    kraw = T("kraw", [128, 2, NTP], BF16)
    class V:
        def __init__(self, name, ap):
            self.t = ap
            self.r = Res(name)

    U1 = es.enter_context(nc.sbuf_tensor("U1", [128, FC * NTP], BF16))
    U2 = es.enter_context(nc.sbuf_tensor("U2", [128, 2080], F32))

    def u1v(name, lo, n, k):
        return V(name, U1[:, lo:lo + n].rearrange("p (k c) -> p k c", k=k))

    qrot = u1v("qrot", 0, 2048, 4)
    qxi = u1v("qxi", 2048, 2048, 4)
    krot = u1v("krot", 4096, 2048, 4)
    kztm = u1v("kztm", 6144, 2048, 4)
    k2tm = u1v("k2tm", 8192, 1024, 4)
    vtm = u1v("vtm", 9216, 2048, 4)
    yTt = [V("y%d" % i, U1[:, i * NTP:(i + 1) * NTP]) for i in range(FC)]
    alias([qrot], yTt[0:4])
    alias([qxi], yTt[4:8])
    alias([krot], yTt[8:12])
    alias([kztm], yTt[12:16])
    alias([k2tm], yTt[16:18])
    alias([vtm], yTt[18:22])
    t1r = tmpr
    t2r = ring("t2", 2, [128, NTP], F32)
    sg = V("sg", U2[:, 0:2048].rearrange("p (k c) -> p k c", k=4))
    bufr = [V("buf%d" % i, U2[:, i * 520:(i + 1) * 520]) for i in range(2)]
    c0r = [V("c0%d" % i, U2[:, 1040 + i * 512:1040 + (i + 1) * 512]) for i in range(2)]
    alias([sg], bufr + c0r)
    osb = T("osb", [128, 4, NTP], F32)
    stmr = ring("stmr", 4, [128, 4, 128], BF16)
    stmg = ring("stmg", 4, [128, 4, 128], BF16)
    vtm2 = T("vtm2", [128, 4, 512], BF16)
    osg = [V("osg%d" % i, x_sb[1 - i].t[:, 0:4, :]) for i in range(2)]
    sgg = [V("sgg%d" % i, x_sb[1 - i].t[:, 4:8, :]) for i in range(2)]
    class _R:
        def __init__(self, r):
            self.r = r

    for i in range(2):
        alias([osg[i]], [_R(r) for r in x_sb[1 - i].rk[0:4]])
        alias([sgg[i]], [_R(r) for r in x_sb[1 - i].rk[4:8]])
    mixT = [T("mix%d" % i, [128, NTP], BF16) for i in range(8)]
    Rst = ring("Rst", 2, [128, 4, 128], F32)
    Rbf = T("Rbf", [128, 4, 128], BF16)
    Gst = ring("Gst", 2, [64, 4, 128], F32)
    Gbf = T("Gbf", [64, 4, 128], BF16)
    hist = T("hist", [128, FC, NSS, 2], F32)

    def mm(out, lhsT, rhs, start, stop, rd, wr):
        P.op("pe", lambda e: e.matmul(out, lhsT, rhs, start=start, stop=stop), rd, wr)

    def trp(out, in_, rd, wr):
        P.op("pe", lambda e: e.transpose(out, in_, c_ident.t[:]), list(rd) + [c_ident.r], wr)

    def act(out, in_, func, rd, wr, bias=None, scale=None):
        kw = {}
        if bias is not None:
            kw["bias"] = bias
        if scale is not None:
            kw["scale"] = scale
        P.op("act", lambda e: e.activation(out=out, in_=in_, func=func, **kw), rd, wr)

    def tt(eng, out, in0, in1, op, rd, wr):
        P.op(eng, lambda e: e.tensor_tensor(out=out, in0=in0, in1=in1, op=op), rd, wr)

    def stt(eng, out, in0, scalar, in1, op0, op1, rd, wr):
        P.op(eng, lambda e: e.scalar_tensor_tensor(out=out, in0=in0, scalar=scalar, in1=in1,
                                                   op0=op0, op1=op1), rd, wr)

    def ts(eng, out, in0, s1, s2, op0, op1, rd, wr):
        P.op(eng, lambda e: e.tensor_scalar(out=out, in0=in0, scalar1=s1, scalar2=s2, op0=op0, op1=op1), rd, wr)

    def cp(eng, out, in_, rd, wr):
        P.op(eng, lambda e: e.tensor_copy(out=out, in_=in_), rd, wr)

    def mset(eng, ap, val, wr):
        P.op(eng, lambda e: e.memset(ap, val), (), wr)

    def recip(out, in_, rd, wr):
        P.op("dve", lambda e: e.reciprocal(out=out, in_=in_), rd, wr)

    tiles = []
    for s in range(NPS):
        for t in range(SEQ // NTP):
            tiles.append(dict(kind="p", col0=s * SEQ + t * NTP, NT=NTP, L=128, first=(t == 0),
                              last=(t == SEQ // NTP - 1), segs=[(s, 0, NTP)], slot=s % 2))
    tiles.append(dict(kind="s", col0=NPS * SEQ, NT=NSS * DSEQ, L=64, first=True, last=True,
                      segs=[(NPS + i, i * DSEQ, DSEQ) for i in range(NSS)], slot=None))

    def group_srcs(g):
        if g == 0:
            return [(0, 8, 16, w_in[:, 3584:3600].rearrange("(k p) c -> p k c", p=128))]
        if 1 <= g <= 7:
            c0 = (g - 1) * 512
            return [(0, 8, 512, w_in[:, c0:c0 + 512].rearrange("(k p) c -> p k c", p=128))]
        if g in (8, 9):
            c0 = (g - 8) * 512
            return [(0, 8, 512, w_out[:, c0:c0 + 512].rearrange("(k p) c -> p k c", p=128))]
        if 10 <= g <= 20:
            j = g - 10
            return [(0, 8, 256, w_f1[:, 256 * j:256 * j + 256].rearrange("(k p) c -> p k c", p=128)),
                    (8 * 256, 8, 256, w_f1[:, DFF + 256 * j:DFF + 256 * j + 256].rearrange("(k p) c -> p k c", p=128))]
        j = g - 21
        return [(0, FC, 128, w_f2[:, 128 * j:128 * j + 128].rearrange("(k p) c -> p k c", p=128))]

    sched = [("ada", i) for i in range(6)]
    GORDER = [0, 1, 2, 5, 3, 6, 4, 7, 8, 9] + list(range(10, 21)) + list(range(21, 29))
    sched.append(("w", 0, 0))
    sched += [("ada", i) for i in range(6, 12)]
    for ti in range(len(tiles)):
        for g in [1, 3, 2, 6, 5, 4, 7, 8, 9] + list(range(10, 21)):
            sched.append(("w", ti, g))
        for g in range(21, 24):
            sched.append(("w", ti, g))
        if ti + 1 < len(tiles):
            sched.append(("w", ti + 1, 0))
        for g in range(24, 29):
            sched.append(("w", ti, g))
    loaded = [0]

    def slot_view(si, nk, ncols, off=0):
        return wsl[si].t[:, off:off + nk * ncols].rearrange("p (k c) -> p k c", k=nk)

    def ensure(idx, la=NSLOT - 1):
        while loaded[0] <= min(idx + la, len(sched) - 1):
            i = loaded[0]
            si = i % NSLOT
            ent = sched[i]
            key = "w%d" % si
            if ent[0] == "ada":
                pc = ent[1]
                src = w_ada[:, pc * 512:(pc + 1) * 512].rearrange("(k p) c -> p k c", p=128)
                P.dma("pool", slot_view(si, 8, 512), src, writes=[wsl[si].r], key="wc%d" % si)
            else:
                _, ti, g = ent
                if ti == 0:
                    for (off, nk, ncols, src) in group_srcs(g):
                        P.dma("pool", slot_view(si, nk, ncols, off), src, writes=[wsl[si].r], key="wc%d" % si)
                    gsz = 128 if g == 0 else (FC * 128 if g >= 21 else 4096)
                    P.dma("sp", scratch[g][:, 0:gsz], wsl[si].t[:, 0:gsz], reads=[wsl[si].r], writes=[scr_res[g]],
                          key="ws%d" % si)
                else:
                    gsz = 128 if g == 0 else (FC * 128 if g >= 21 else 4096)
                    P.dma("sp", wsl[si].t[:, 0:gsz], scratch[g][:, 0:gsz], reads=[scr_res[g]], writes=[wsl[si].r], key=key)
            loaded[0] += 1

    spos = [0]

    def next_w(expect=None, la=NSLOT - 1):
        i = spos[0]
        spos[0] += 1
        assert expect is None or sched[i][-1] == expect or sched[i] == expect, (i, sched[i], expect)
        ensure(i, la)
        return i % NSLOT

    out_keys = set()

    def body():
        act(c_scT.t[:], c_cT.t[:], AF.Silu, [c_cT.r], [c_scT.r])

        def ada_piece(pc):
            si = next_w(("ada", pc))
            wv = slot_view(si, 8, 512)
            for jj in range(4):
                j = pc * 4 + jj
                for kc in range(KC):
                    mm(TR.t[:, j * NSEQ:(j + 1) * NSEQ], wv[:, kc, jj * 128:(jj + 1) * 128], c_scT.t[:, kc, :],
                       kc == 0, kc == KC - 1, [wsl[si].r, c_scT.r], [TR.r])

        def ada_evac(j0, j1):
            tt("dve", modT.t[:, j0:j1, :], TR.t[:, j0 * NSEQ:j1 * NSEQ].rearrange("p (j s) -> p j s", s=NSEQ),
               c_bada.t[:, j0:j1].unsqueeze(2).to_broadcast([128, j1 - j0, NSEQ]),
               ALU.add, [TR.r, c_bada.r], [modT.r])

        def ada_gm(gmx, gi, sc0):
            stt("dve", gmx.t[:], modT.t[:, sc0:sc0 + 8, :], 1.0,
                c_gvec.t[:, gi, :].unsqueeze(2).to_broadcast([128, KC, NSEQ]), ALU.add, ALU.mult,
                [modT.r, c_gvec.r], [gmx.r])

        for pc in range(4):
            ada_piece(pc)
        ada_evac(0, 16)
        ada_gm(gm1, 0, 8)
        stage("prologue")
        mset("pool", ga1.t[:], 1.0, [ga1.r])
        mset("pool", c_eps.t[:], EPS, [c_eps.r])

        SH1, G1, SH2, G2 = 0, 16, 24, 40

        def norm(*a, **k):
            for _ in norm_gen(*a, **k):
                pass

        def norm_gen(ti, xs, NT, segs, gm, sh0, out_fn, final=False, sq_pool=False):
            st = TR if sq_pool else next_bank()

            def sqmm(kc, do_sq, do_mm):
                sqk = sqr[kc % 4]
                if do_sq:
                    if sq_pool:
                        tt("pool", sqk.t[:, 0:NT], xs.t[:, kc, 0:NT], xs.t[:, kc, 0:NT], ALU.mult, [xs.rk[kc]], [sqk.r])
                    else:
                        act(sqk.t[:, 0:NT], xs.t[:, kc, 0:NT], AF.Square, [xs.rk[kc]], [sqk.r])
                if do_mm:
                    mm(f32v(st)[:, 0:NT], c_onesb.t[:], sqk.t[:, 0:NT], kc == 0, kc == KC - 1, [c_onesb.r, sqk.r], [st.r])

            if sq_pool:
                for kc in range(4):
                    sqmm(kc, True, False)
                yield
                for kc in range(4):
                    sqmm(kc, False, True)
                for kc in range(4, 8):
                    sqmm(kc, True, False)
                yield
                for kc in range(4, 8):
                    sqmm(kc, False, True)
                yield
            else:
                for kc in range(KC):
                    sqmm(kc, True, True)
                yield
            rb = rstd_from(st, NT, 1.0 / D)
            for kc in range(KC):
                if final:
                    stt("dve", out_fn(kc)[0], xs.t[:, kc, 0:NT], c_gvec.t[:, 2, kc:kc + 1], rb.t[:, 0:NT],
                        ALU.mult, ALU.mult, [xs.rk[kc], c_gvec.r, rb.r], [out_fn(kc)[1]])
                    continue
                tm = tmpr[kc % 2]
                if len(segs) > 1:
                    ns_, s0_ = len(segs), segs[0][0]
                    ls_ = NT // ns_
                    v3 = lambda ap: ap.rearrange("p (s l) -> p s l", s=ns_)
                    o_ap, o_r = out_fn(kc)
                    tt("dve", v3(tm.t[:, 0:NT]), v3(xs.t[:, kc, 0:NT]), v3(rb.t[:, 0:NT]), ALU.mult,
                       [xs.rk[kc], rb.r], [tm.r])
                    tt("dve", v3(tm.t[:, 0:NT]), v3(tm.t[:, 0:NT]),
                       gm.t[:, kc, s0_:s0_ + ns_].unsqueeze(2).to_broadcast([128, ns_, ls_]), ALU.mult, [tm.r, gm.r], [tm.r])
                    tt("pool", v3(o_ap[:, 0:NT]), v3(tm.t[:, 0:NT]),
                       modT.t[:, sh0 + kc, s0_:s0_ + ns_].unsqueeze(2).to_broadcast([128, ns_, ls_]), ALU.add,
                       [tm.r, modT.r], [o_r])
                    continue
                for (sid, c0, n) in segs:
                    stt("dve", tm.t[:, c0:c0 + n], xs.t[:, kc, c0:c0 + n], gm.t[:, kc, sid:sid + 1],
                        rb.t[:, c0:c0 + n], ALU.mult, ALU.mult, [xs.rk[kc], gm.r, rb.r], [tm.r])
                    o_ap, o_r = out_fn(kc)
                    act(o_ap[:, c0:c0 + n], tm.t[:, c0:c0 + n], AF.Identity, [tm.r, modT.r], [o_r],
                        bias=modT.t[:, sh0 + kc, sid:sid + 1])

        rsc = [0]

        def rstd_from(st, NT, scale):
            i = rsc[0] % 4
            rsc[0] += 1
            rd = (rsdr + rsbr)[i]
            act(rd.t[:, 0:NT], f32v(st)[:, 0:NT], AF.Ln, [st.r, c_eps.r], [rd.r], bias=c_eps.t[:, 0:1], scale=scale)
            act(rd.t[:, 0:NT], rd.t[:, 0:NT], AF.Exp, [rd.r], [rd.r], scale=-0.5)
            return rd

        bkc = [0]
        BANKS = [PJ[0], SC, PJ[1], OC, PJ[2], MS, PJ[3]]

        def next_bank():
            b = BANKS[bkc[0] % 7]
            bkc[0] += 1
            return b

        def f32v(b):
            return b.t[:, :]

        def bf16v(b):
            return b.t[:, :].bitcast(BF16)

        next_pj = next_bank

        def proj_fm(si, col, NT, act_t, nk=KC, rhs_fn=None):
            b = next_pj()
            wv = wsl[si]
            for kc in range(nk):
                lhsT = wv.t[:, kc * (col[1]) + col[0]: kc * (col[1]) + col[0] + col[2]]
                rhs, rr = rhs_fn(kc)
                mm(b.t[0:col[2], 0:NT], lhsT, rhs, kc == 0, kc == nk - 1, [wv.r, rr], [b.r])
            return b

        n_tiles = len(tiles)

        def load_x(ti):
            tl = tiles[ti]
            NT, c0 = tl["NT"], tl["col0"]
            xs = x_sb[ti % 2]
            P.dma("pool", xs.t[:, :, 0:NT], xT[:, :, c0:c0 + NT].rearrange("k p t -> p k t"), writes=xs.rk,
                  key="x%d" % (ti % 2))

        def load_rot(ti):
            tl = tiles[ti]
            NT, c0 = tl["NT"], tl["col0"]
            P.dma("pool", rC[0].t[:, 0:NT], rotC_d[:, c0:c0 + NT], writes=[rC[0].r], key="rc")
            P.dma("pool", rS[0].t[:, 0:NT], rotS_d[:, c0:c0 + NT], writes=[rS[0].r], key="rs")

        load_x(0)
        load_rot(0)

        def t_front(ti):
            tl = tiles[ti]
            NT, L, segs = tl["NT"], tl["L"], tl["segs"]
            xs = x_sb[ti % 2]
            nch = NT // L
            zsel = 0 if L == 128 else 1
            if tl["kind"] == "p":
                ch_slot = [tl["slot"]] * nch
                ch_seq = [segs[0][0]] * nch
            else:
                ch_slot = [i % 2 for i in range(nch)]
                ch_seq = [segs[i][0] for i in range(nch)]
            par = ti % 2
            OSG, SGG = osg[par], sgg[par]

            def h_rhs(kc):
                return hT.t[:, kc, 0:NT], hT.rk[kc]


            yield from norm_gen(ti, xs, NT, segs, gm1, SH1, lambda kc: (hT.t[:, kc, :], hT.rk[kc]), sq_pool=(ti > 0))
            if tl["kind"] == "p" and tl["first"]:
                sl = tl["slot"]
                mset("pool", Rst[sl].t[:], 0.0, [Rst[sl].r])
                mset("pool", Gst[sl].t[:], 0.0, [Gst[sl].r])
                mset("pool", hist.t[:], 0.0, [hist.r])
            if tl["kind"] == "s":
                P.dma("pool", hist.t[:], cc_in, writes=[hist.r], key="hist")

            stage("t%d_init" % ti)
            stage("t%d_norm1" % ti)
            si = next_w(0)
            b = next_pj()
            for kc in range(KC):
                mm(b.t[0:16, 0:NT], wsl[si].t[:, kc * 16:(kc + 1) * 16], hT.t[:, kc, 0:NT], kc == 0, kc == KC - 1,
                   [wsl[si].r, hT.rk[kc]], [b.r])
            act(ga1.t[0:16, 0:NT], b.t[0:16, 0:NT], AF.Copy, [b.r], [ga1.r])
            yield
            for c in range(nch):
                cs = slice(c * L, (c + 1) * L)
                bu_ = next_bank()
                mm(f32v(bu_)[0:L, 0:256], ga1.t[:, cs], c_wa2.t[:], True, True, [ga1.r, c_wa2.r], [bu_.r])
                act(E2.t[0:L, c, :], f32v(bu_)[0:L, 0:256], AF.Exp, [bu_.r], [E2.r], scale=-1.0)
                act(lsp.t[0:L, c, :], E2.t[0:L, c, :], AF.Ln, [E2.r], [lsp.r], bias=1.0)
            yield
            for c in range(nch):
                cs = slice(c * L, (c + 1) * L)
                bb_ = next_bank()
                for p in range(2):
                    mm(f32v(bb_)[:, p * 128:p * 128 + L], lsp.t[0:L, c, p * 128:(p + 1) * 128], c_tri.t[0:L, 0:L],
                       True, True, [lsp.r, c_tri.r], [bb_.r])
                bview = f32v(bb_)[:, 0:256].rearrange("p (a t) -> p a t", a=2)[:, :, 0:L]
                act(Aex.t[:, :, cs], bview, AF.Exp, [bb_.r], [Aex.r])
                act(Ain.t[:, :, cs], bview, AF.Exp, [bb_.r], [Ain.r], scale=-1.0)
                bl_ = next_bank()
                mm(f32v(bl_)[0:L, 0:256], c_utri.t[0:L, 0:L], lsp.t[0:L, c, :], True, True, [c_utri.r, lsp.r], [bl_.r])
                act(E2.t[0:L, c, :], f32v(bl_)[0:L, 0:256], AF.Exp, [bl_.r], [E2.r])
            stage("t%d_glaprep" % ti)

        def t_mid(ti, pre_last=None):
            tl = tiles[ti]
            NT, L, segs = tl["NT"], tl["L"], tl["segs"]
            xs = x_sb[ti % 2]
            nch = NT // L
            zsel = 0 if L == 128 else 1
            if tl["kind"] == "p":
                ch_slot = [tl["slot"]] * nch
                ch_seq = [segs[0][0]] * nch
            else:
                ch_slot = [i % 2 for i in range(nch)]
                ch_seq = [segs[i][0] for i in range(nch)]
            par = ti % 2
            OSG, SGG = osg[par], sgg[par]

            def h_rhs(kc):
                return hT.t[:, kc, 0:NT], hT.rk[kc]


            PEX = "dve" if ti == 0 else "pool"

            def proj_v(vdst, g):
                si = next_w(g)
                for c in range(nch):
                    b = next_pj()
                    for kc in range(KC):
                        mm(b.t[0:L, 0:512], hT.t[:, kc, c * L:(c + 1) * L], wsl[si].t[:, kc * 512:(kc + 1) * 512],
                           kc == 0, kc == KC - 1, [hT.rk[kc], wsl[si].r], [b.r])
                    act(vdst.t[0:L, c, :], b.t[0:L, 0:512], AF.Copy, [b.r], [vdst.r])

            def gen_gates(gdst, g):
                si = next_w(g)
                for h in range(4):
                    b = proj_fm(si, (h * 128, 512, 128), NT, None, rhs_fn=h_rhs)
                    act(gdst.t[:, h, 0:NT], b.t[:, 0:NT], AF.Silu, [b.r], [gdst.r])
                    yield

            for gi, dst, vdst, gv in ((1, qrot, vtm, 3), (2, krot, vtm2, 6)):
                si = next_w(gi)
                for h in range(4):
                    b = proj_fm(si, (h * 128, 512, 128), NT, None, rhs_fn=h_rhs)
                    t1, t2 = t1r[h % 2], t2r[h % 2]
                    tt("dve", t1.t[:, 0:NT], b.t[:, 0:NT], rC[0].t[:, 0:NT], ALU.mult, [b.r, rC[0].r], [t1.r])
                    tt("dve", t2.t[0:64, 0:NT], b.t[64:128, 0:NT], rS[0].t[64:128, 0:NT], ALU.mult,
                       [b.r, rS[0].r], [t2.r])
                    tt("dve", t2.t[64:128, 0:NT], b.t[0:64, 0:NT], rS[0].t[0:64, 0:NT], ALU.mult,
                       [b.r, rS[0].r], [t2.r])
                    tt(PEX, dst.t[:, h, 0:NT], t1.t[:, 0:NT], t2.t[:, 0:NT], ALU.add, [t1.r, t2.r], [dst.r])
                proj_v(vdst, gv)
            if ti + 1 < n_tiles:
                load_rot(ti + 1)
            tt(PEX, qxi.t[:, :, 0:NT].rearrange("p h (c l) -> p h c l", l=L),
               qrot.t[:, :, 0:NT].rearrange("p h (c l) -> p h c l", l=L),
               c_xi.t[:, :, 0:L].unsqueeze(2).to_broadcast([128, 4, nch, L]), ALU.mult, [qrot.r, c_xi.r], [qxi.r])
            stage("t%d_retproj" % ti)

            si = next_w(5)
            for p in range(2):
                b = proj_fm(si, (p * 128, 512, 128), NT, None, rhs_fn=h_rhs)
                for two in range(2):
                    rows = slice(two * 64, two * 64 + 64)
                    stt("dve", qg.t[0:64, 2 * p + two, 0:NT], b.t[rows, 0:NT], 0.125, Aex.t[rows, p, 0:NT],
                        ALU.mult, ALU.mult, [b.r, Aex.r], [qg.r])
            for p in range(2):
                b = proj_fm(si, (256 + p * 128, 512, 128), NT, None, rhs_fn=h_rhs)
                for two in range(2):
                    rows = slice(two * 64, two * 64 + 64)
                    tt("dve", kg.t[0:64, 2 * p + two, 0:NT], b.t[rows, 0:NT], Ain.t[rows, p, 0:NT], ALU.mult,
                       [b.r, Ain.r], [kg.r])
                act(kraw.t[:, p, 0:NT], b.t[:, 0:NT], AF.Copy, [b.r], [kraw.r])
            stage("t%d_g5" % ti)

            def gen_passA():
                for c in range(nch):
                    cs = slice(c * L, (c + 1) * L)
                    bt = next_bank()
                    for h in range(4):
                        trp(bf16v(bt)[0:L, h * 128:(h + 1) * 128], krot.t[:, h, cs], [krot.r], [bt.r])
                    tt("dve", kztm.t[0:L, c, :].rearrange("p (h d) -> p h d", h=4),
                       bf16v(bt)[0:L, 0:512].rearrange("p (h d) -> p h d", h=4),
                       c_zeta.t[0:L, zsel, :].unsqueeze(2).to_broadcast([L, 4, 128]), ALU.mult, [bt.r, c_zeta.r], [kztm.r])
                    bs = next_bank()
                    for h in range(4):
                        mm(f32v(bs)[0:L, h * 128:h * 128 + L], krot.t[:, h, cs], qrot.t[:, h, cs], True, True,
                           [krot.r, qrot.r], [bs.r])
                    tt("dve", stmr[c].t[0:L, :, 0:L], f32v(bs)[0:L, :].rearrange("p (h t) -> p h t", h=4)[:, :, 0:L],
                       c_maskR.t[0:L, :, 0:L], ALU.mult, [bs.r, c_maskR.r], [stmr[c].r])
                    bt = next_bank()
                    for p in range(2):
                        trp(bf16v(bt)[0:L, p * 128:(p + 1) * 128], kraw.t[:, p, cs], [kraw.r], [bt.r])
                    tt("dve", k2tm.t[0:L, c, :], bf16v(bt)[0:L, 0:256], E2.t[0:L, c, :], ALU.mult, [bt.r, E2.r], [k2tm.r])
                    bs = next_bank()
                    for h in range(4):
                        mm(f32v(bs)[0:L, h * 128:h * 128 + L], kg.t[0:64, h, cs], qg.t[0:64, h, cs], True, True,
                           [kg.r, qg.r], [bs.r])
                    tt("dve", stmg[c].t[0:L, :, 0:L], f32v(bs)[0:L, :].rearrange("p (h t) -> p h t", h=4)[:, :, 0:L],
                       c_mask01.t[0:L, :, 0:L], ALU.mult, [bs.r, c_mask01.r], [stmg[c].r])
                    yield

            def interleave(a, b, ratio):
                for _ in a:
                    for _ in range(ratio):
                        next(b, None)
                for _ in b:
                    pass

            interleave(gen_passA(), gen_gates(sg, 4), 1)
            stage("t%d_passA" % ti)

            def gen_passB():
                for c in range(nch):
                    cs = slice(c * L, (c + 1) * L)
                    sl = ch_slot[c]
                    R, G = Rst[sl], Gst[sl]
                    if tl["kind"] == "s":
                        sid = ch_seq[c] - NPS
                        P.dma("pool", R.t[:], st_ret_in[sid].rearrange("h d e -> d h e"), writes=[R.r], key="rin%d" % sl)
                        P.dma("pool", G.t[:], st_gla_in[sid].rearrange("h d e -> d h e"), writes=[G.r], key="gin%d" % sl)
                    act(Rbf.t[:], R.t[:], AF.Copy, [R.r], [Rbf.r])
                    cp("dve", Gbf.t[:], G.t[:], [G.r], [Gbf.r])
                    bu_ = next_bank()
                    for h in range(4):
                        mm(f32v(bu_)[:, h * 128:(h + 1) * 128], kztm.t[0:L, c, h * 128:(h + 1) * 128],
                           vtm.t[0:L, c, h * 128:(h + 1) * 128], True, True, [kztm.r, vtm.r], [bu_.r])
                    for h in range(4):
                        stt("dve", R.t[:, h, :], R.t[:, h, :], consts["gL"][L][h], f32v(bu_)[:, h * 128:(h + 1) * 128],
                            ALU.mult, ALU.add, [R.r, bu_.r], [R.r])
                    bu_ = next_bank()
                    for h in range(4):
                        mm(f32v(bu_)[0:64, h * 128:(h + 1) * 128], k2tm.t[0:L, c, h * 64:(h + 1) * 64],
                           vtm2.t[0:L, c, h * 128:(h + 1) * 128], True, True, [k2tm.r, vtm2.r], [bu_.r])
                    lastc = c * L + L - 1
                    for h in range(4):
                        p, two = h // 2, h % 2
                        rows = slice(two * 64, two * 64 + 64)
                        cp("dve", eLt.t[0:64, h, c:c + 1], Aex.t[rows, p, lastc:lastc + 1], [Aex.r], [eLt.r])
                    for h in range(4):
                        stt("dve", G.t[0:64, h, :], G.t[0:64, h, :], eLt.t[0:64, h, c:c + 1],
                            f32v(bu_)[0:64, h * 128:(h + 1) * 128], ALU.mult, ALU.add, [G.r, eLt.r, bu_.r], [G.r])
                    bo = next_bank()
                    for h in range(4):
                        mm(f32v(bo)[:, h * 128:h * 128 + L], vtm.t[0:L, c, h * 128:(h + 1) * 128], stmr[c].t[0:L, h, 0:L],
                           True, False, [vtm.r, stmr[c].r], [bo.r])
                        mm(f32v(bo)[:, h * 128:h * 128 + L], Rbf.t[:, h, :], qxi.t[:, h, cs], False, True,
                           [Rbf.r, qxi.r], [bo.r])
                    act(osb.t[:, :, cs], f32v(bo).rearrange("p (h t) -> p h t", h=4)[:, :, 0:L], AF.Copy, [bo.r], [osb.r])
                    bo = next_bank()
                    for h in range(4):
                        mm(f32v(bo)[:, h * 128:h * 128 + L], vtm2.t[0:L, c, h * 128:(h + 1) * 128], stmg[c].t[0:L, h, 0:L],
                           True, False, [vtm2.r, stmg[c].r], [bo.r])
                        mm(f32v(bo)[:, h * 128:h * 128 + L], Gbf.t[0:64, h, :], qg.t[0:64, h, cs], False, True,
                           [Gbf.r, qg.r], [bo.r])
                    act(OSG.t[:, :, cs], f32v(bo).rearrange("p (h t) -> p h t", h=4)[:, :, 0:L], AF.Copy, [bo.r], [OSG.r])
                    if tl["kind"] == "s" or (tl["last"] and c == nch - 1):
                        seq = ch_seq[c]
                        out_keys.add("oret%d" % sl)
                        out_keys.add("ogla%d" % sl)
                        P.dma("pool", o_ret[seq].rearrange("h d e -> d h e"), R.t[:], reads=[R.r], key="oret%d" % sl)
                        P.dma("pool", o_gla[seq].rearrange("h d e -> d h e"), G.t[:], reads=[G.r], key="ogla%d" % sl)
                    yield

            interleave(gen_passB(), gen_gates(SGG, 7), 1)
            stage("t%d_passB" % ti)


            for h in range(4):
                st = next_bank()
                mm(f32v(st)[:, 0:NT], c_onesf.t[:], osb.t[:, h, 0:NT], True, True, [c_onesf.r, osb.r], [st.r])
                stt("dve", osb.t[:, h, 0:NT], f32v(st)[:, 0:NT], -1.0 / 128, osb.t[:, h, 0:NT], ALU.mult, ALU.add,
                    [st.r, osb.r], [osb.r])
                sqa = sqr[h % 2]
                act(sqa.t[:, 0:NT], osb.t[:, h, 0:NT], AF.Square, [osb.r], [sqa.r])
                st2 = next_bank()
                mm(f32v(st2)[:, 0:NT], c_onesb.t[:], sqa.t[:, 0:NT], True, True, [c_onesb.r, sqa.r], [st2.r])
                rb = rstd_from(st2, NT, 1.0 / 128)
                tm = tmpr[h % 2]
                stt("dve", tm.t[:, 0:NT], osb.t[:, h, 0:NT], c_ghead.t[:, 0, h:h + 1], rb.t[:, 0:NT], ALU.mult, ALU.mult,
                    [osb.r, c_ghead.r, rb.r], [tm.r])
                tt(PEX, mixT[h].t[:, 0:NT], tm.t[:, 0:NT], sg.t[:, h, 0:NT], ALU.mult, [tm.r, sg.r], [mixT[h].r])
                sqa = sqr[2 + h % 2]
                act(sqa.t[:, 0:NT], OSG.t[:, h, 0:NT], AF.Square, [OSG.r], [sqa.r])
                st = next_bank()
                mm(f32v(st)[:, 0:NT], c_onesb.t[:], sqa.t[:, 0:NT], True, True, [c_onesb.r, sqa.r], [st.r])
                rb = rstd_from(st, NT, 1.0 / 128)
                tm = t2r[h % 2]
                stt("dve", tm.t[:, 0:NT], OSG.t[:, h, 0:NT], c_ghead.t[:, 1, h:h + 1], rb.t[:, 0:NT], ALU.mult, ALU.mult,
                    [OSG.r, c_ghead.r, rb.r], [tm.r])
                tt(PEX, mixT[4 + h].t[:, 0:NT], tm.t[:, 0:NT], SGG.t[:, h, 0:NT], ALU.mult, [tm.r, SGG.r],
                   [mixT[4 + h].r])
            stage("t%d_gla" % ti)
            if ti + 1 < n_tiles:
                load_x(ti + 1)

            def mix_rhs(kc):
                return mixT[kc].t[:, 0:NT], mixT[kc].r

            for g2 in range(2):
                si = next_w(8 + g2)
                for m in range(4):
                    j = g2 * 4 + m
                    b = proj_fm(si, (m * 128, 512, 128), NT, None, rhs_fn=mix_rhs)
                    for (sid, c0, n) in segs:
                        stt("dve", xs.t[:, j, c0:c0 + n], b.t[:, c0:c0 + n], modT.t[:, G1 + j, sid:sid + 1],
                            xs.t[:, j, c0:c0 + n], ALU.mult, ALU.add, [b.r, modT.r, xs.rk[j]], [xs.rk[j]])

            stage("t%d_wout" % ti)
            norm(ti, xs, NT, segs, gm2, SH2, lambda kc: (hT.t[:, kc, :], hT.rk[kc]))
            nseg = len(segs)
            Ls = NT // nseg
            for g in range(11):
                if g in (4, 7, 10) and pre_last is not None:
                    pre_last()
                si = next_w(10 + g)
                banks = []
                for m in range(4):
                    banks.append(proj_fm(si, ((m // 2) * 2048 + (m % 2) * 128, 256, 128), NT, None, rhs_fn=h_rhs))
                for q in range(2):
                    j = 2 * g + q
                    ba, bu = banks[q], banks[2 + q]
                    B = bufr[j % 2]
                    Bv = B.t[:, 0:nseg * (Ls + 2)].rearrange("p (s l) -> p s l", s=nseg)
                    C0 = c0r[j % 2]
                    C0v = C0.t[:, 0:NT].rearrange("p (s l) -> p s l", s=nseg)
                    cp(PEX, Bv[:, :, 0:2], hist.t[:, j, 0:nseg, :], [hist.r], [B.r])
                    act(Bv[:, :, 2:Ls + 2], ba.t[:, 0:NT].rearrange("p (s l) -> p s l", s=nseg), AF.Copy, [ba.r], [B.r])
                    act(C0v, Bv[:, :, 0:Ls], AF.Identity, [B.r, c_conv.r], [C0.r],
                        bias=c_conv.t[:, j, 3:4], scale=c_conv.t[:, j, 0:1])
                    stt("dve", C0v, Bv[:, :, 1:Ls + 1], c_conv.t[:, j, 1:2], C0v, ALU.mult, ALU.add,
                        [B.r, c_conv.r, C0.r], [C0.r])
                    stt("dve", C0v, Bv[:, :, 2:Ls + 2], c_conv.t[:, j, 2:3], C0v, ALU.mult, ALU.add,
                        [B.r, c_conv.r, C0.r], [C0.r])
                    cp(PEX, hist.t[:, j, 0:nseg, :], Bv[:, :, Ls:Ls + 2], [B.r], [hist.r])
                    act(C0.t[:, 0:NT], C0.t[:, 0:NT], AF.Silu, [C0.r], [C0.r])
                    tt("dve", yTt[j].t[:, 0:NT], C0.t[:, 0:NT], bu.t[:, 0:NT], ALU.mult, [C0.r, bu.r], [yTt[j].r])
            if tl["last"]:
                if tl["kind"] == "p":
                    s0 = segs[0][0]
                    out_keys.add("occ")
                    P.dma("pool", o_cc[:, :, s0:s0 + 1, :], hist.t[:, :, 0:1, :], reads=[hist.r], key="occ")
                else:
                    out_keys.add("occ")
                    P.dma("pool", o_cc[:, :, NPS:NPS + NSS, :], hist.t[:], reads=[hist.r], key="occ")

            stage("t%d_ffn1" % ti)

        def t_back(ti):
            tl = tiles[ti]
            NT, L, segs = tl["NT"], tl["L"], tl["segs"]
            xs = x_sb[ti % 2]
            nch = NT // L
            zsel = 0 if L == 128 else 1
            if tl["kind"] == "p":
                ch_slot = [tl["slot"]] * nch
                ch_seq = [segs[0][0]] * nch
            else:
                ch_slot = [i % 2 for i in range(nch)]
                ch_seq = [segs[i][0] for i in range(nch)]
            par = ti % 2
            OSG, SGG = osg[par], sgg[par]

            def h_rhs(kc):
                return hT.t[:, kc, 0:NT], hT.rk[kc]

            def y_rhs(kc):
                return yTt[kc].t[:, 0:NT], yTt[kc].r

            def resid(j, b):
                for (sid, c0, n) in segs:
                    stt("dve", xs.t[:, j, c0:c0 + n], b.t[:, c0:c0 + n], modT.t[:, G2 + j, sid:sid + 1],
                        xs.t[:, j, c0:c0 + n], ALU.mult, ALU.add, [b.r, modT.r, xs.rk[j]], [xs.rk[j]])

            s0_, s1_ = next_w(21), next_w(22, la=NSLOT - 2)
            b0_, b1_ = next_pj(), next_pj()
            KSPLIT = 16
            for (lo, hi) in ((0, KSPLIT), (KSPLIT, FC)):
                for (si, b) in ((s0_, b0_), (s1_, b1_)):
                    for kc in range(lo, hi):
                        mm(b.t[:, 0:NT], wsl[si].t[:, kc * 128:(kc + 1) * 128], yTt[kc].t[:, 0:NT],
                           kc == 0, kc == FC - 1, [wsl[si].r, yTt[kc].r], [b.r])
            resid(0, b0_)
            yield
            resid(1, b1_)
            yield
            for j in range(2, KC):
                si = next_w(21 + j)
                b = proj_fm(si, (0, 128, 128), NT, None, nk=FC, rhs_fn=y_rhs)
                resid(j, b)
                yield

            stage("t%d_ffn2" % ti)
            norm(ti, xs, NT, segs, None, None, lambda kc: (xs.t[:, kc, 0:NT], xs.rk[kc]), final=True)
            k = "y%d" % (ti % 2)
            out_keys.add(k)
            c0 = tl["col0"]
            P.dma("pool", yT[:, :, c0:c0 + NT].rearrange("k p t -> p k t"), xs.t[:, :, 0:NT], reads=xs.rk, key=k)
            stage("t%d_end" % ti)

        fg0 = t_front(0)
        for pc in range(4, 12):
            ada_piece(pc)
            if pc in (4, 5, 6, 7):
                next(fg0, None)
        for _ in fg0:
            pass
        ada_evac(16, 48)
        ada_gm(gm2, 1, 32)
        for ti in range(n_tiles):
            fg = t_front(ti + 1) if ti + 1 < n_tiles else iter(())
            t_mid(ti, pre_last=lambda: next(fg, None))
            bg = t_back(ti)
            plan = {2: 1, 3: 1, 5: 1}
            for j in range(KC):
                next(bg)
                for _ in range(plan.get(j, 0)):
                    next(fg, None)
            for _ in fg:
                pass
            for _ in bg:
                pass

    try:
        body()
    except _Stop:
        pass
    P.emit("pool", sorted(out_keys))
    print("[kernel] sbuf bytes remaining/partition:", nc.sbuf_bytes_remaining, flush=True)
    es.close()
    return nc


_CACHE = {}


def _prep(x_prompt, x_sample, c_prompt, c_sample, state_ret, state_gla, cache_ffn_conv,
          w_ada, b_ada, g_norm_mix, w_in, w_a2, b_a2, g_ret_norm, g_gla_norm, w_out,
          g_norm_ffn, w_ffn_in, conv_w, conv_b, w_ffn_out, g_final, cores=None):
    f = np.float32
    A = lambda a: np.ascontiguousarray(np.asarray(a, dtype=f))
    if "c" not in _CACHE:
        _CACHE["c"] = _consts()
    consts = _CACHE["c"]
    x_prompt, x_sample = A(x_prompt), A(x_sample)
    c_prompt, c_sample = A(c_prompt), A(c_sample)
    shared = {
        "w_ada": A(w_ada[0]),
        "b_ada": A(A(b_ada[0]).reshape(48, 128).T),
        "gvec": A(np.stack([A(g_norm_mix[0]).reshape(KC, 128).T, A(g_norm_ffn[0]).reshape(KC, 128).T,
                            A(g_final).reshape(KC, 128).T], 1)),
        "ghead": A(np.stack([A(g_ret_norm[0]).reshape(4, 128).T, A(g_gla_norm[0]).reshape(4, 128).T], 1)),
        "w_in": A(w_in[0]),
        "w_a2b": A(np.concatenate([A(w_a2[0]), A(b_a2[0])[None, :]], 0)),
        "w_out": A(w_out[0]),
        "w_f1": A(w_ffn_in[0]),
        "convp": A(np.concatenate([A(conv_w[0]), A(conv_b[0])[None, :]], 0).reshape(4, FC, 128).transpose(2, 1, 0)),
        "w_f2": A(w_ffn_out[0]),
        "rotC": consts["rotC"], "rotS": consts["rotS"], "maskR": consts["maskR"], "mask01": consts["mask01"],
        "triN": consts["triN"], "utriN": consts["utriN"], "xi": consts["xi"], "zeta": consts["zeta"],
        "ident": consts["ident"], "ones_b": consts["ones_b"], "ones_f": consts["ones_f"],
    }
    in_maps = []
    for c in (range(NCORE) if cores is None else cores):
        xp = x_prompt[c * NPS:(c + 1) * NPS]
        xsm = x_sample[c * NSS:(c + 1) * NSS]
        cols = np.concatenate([xp.reshape(NPS * SEQ, D), xsm.reshape(NSS * DSEQ, D)], 0)
        xTc = A(cols.T.reshape(KC, 128, TOK))
        cc = np.concatenate([c_prompt[c * NPS:(c + 1) * NPS], c_sample[c * NSS:(c + 1) * NSS]], 0)
        cTc = A(cc.T.reshape(KC, 128, NSEQ).transpose(1, 0, 2))
        cci = A(cache_ffn_conv[0, c * NSS:(c + 1) * NSS])
        cci = A(cci.transpose(2, 0, 1).reshape(FC, 128, NSS, 2).transpose(1, 0, 2, 3))
        m = dict(shared)
        m.update({"xT": xTc, "cT": cTc, "st_ret": A(state_ret[0, c * NSS:(c + 1) * NSS]),
                  "st_gla": A(state_gla[0, c * NSS:(c + 1) * NSS]), "cc_in": cci})
        in_maps.append(m)
    return in_maps


def _gather(results, ncores=NCORE):
    f = np.float32
    B, DB = ncores * NPS, ncores * NSS
    y_p = np.empty((B, SEQ, D), f)
    y_s = np.empty((DB, DSEQ, D), f)
    r_p = np.empty((1, B, 4, 128, 128), f)
    g_p = np.empty((1, B, 4, 64, 128), f)
    c_p = np.empty((1, B, 2, DFF), f)
    r_s = np.empty((1, DB, 4, 128, 128), f)
    g_s = np.empty((1, DB, 4, 64, 128), f)
    c_s = np.empty((1, DB, 2, DFF), f)
    for c in range(ncores):
        r = results[c]
        yc = np.asarray(r["yT"]).reshape(D, TOK).T
        y_p[c * NPS:(c + 1) * NPS] = yc[:NPS * SEQ].reshape(NPS, SEQ, D)
        y_s[c * NSS:(c + 1) * NSS] = yc[NPS * SEQ:].reshape(NSS, DSEQ, D)
        orr, og = np.asarray(r["o_ret"]), np.asarray(r["o_gla"])
        r_p[0, c * NPS:(c + 1) * NPS] = orr[:NPS]
        r_s[0, c * NSS:(c + 1) * NSS] = orr[NPS:]
        g_p[0, c * NPS:(c + 1) * NPS] = og[:NPS]
        g_s[0, c * NSS:(c + 1) * NSS] = og[NPS:]
        occ = np.asarray(r["o_cc"])
        occ = occ.transpose(2, 3, 1, 0).reshape(NSEQ, 2, DFF)
        c_p[0, c * NPS:(c + 1) * NPS] = occ[:NPS]
        c_s[0, c * NSS:(c + 1) * NSS] = occ[NPS:]
    return (y_p, y_s, r_p, g_p, c_p, r_s, g_s, c_s)


def kernel(**inputs):
    in_maps = _prep(**inputs)
    if "nc" not in _CACHE:
        _CACHE["nc"] = build_nc(_CACHE["c"])
    res = run_bass_kernel_spmd(_CACHE["nc"], in_maps, core_ids=list(range(NCORE)))
    return _gather(res.results)
```

```python
import contextlib
import numpy as np
import ml_dtypes
import concourse.bass as bass
import concourse.mybir as mybir
from concourse.bass_utils import run_bass_kernel_spmd

F32 = mybir.dt.float32
BF16 = mybir.dt.bfloat16
AF = mybir.ActivationFunctionType
ALU = mybir.AluOpType

D = 1024
KC = 8
DFF = 2816
FC = 22
DIN = 3600
NCORE = 8
NPS = 2
NSS = 4
NSEQ = NPS + NSS
SEQ = 2048
DSEQ = 64
PAST = 1024
NTP = 512
TOK = NPS * SEQ + NSS * DSEQ
EPS = 1e-6
NSLOT = 4
COMPUTE = ("pe", "act", "dve", "pool")


class Res:
    __slots__ = ("name", "last_w", "readers", "al", "psum")

    def __init__(self, name, psum=False):
        self.name = name
        self.last_w = None
        self.readers = []
        self.al = (self,)
        self.psum = psum


def alias(a_list, b_list):
    for a in [x.r for x in a_list]:
        for b in [x.r for x in b_list]:
            if b not in a.al:
                a.al = a.al + (b,)
            if a not in b.al:
                b.al = b.al + (a,)


class Op:
    __slots__ = ("eng", "fn", "reads", "writes", "deps", "sig", "cnt", "key", "is_dma")

    def __init__(self, eng, fn, reads, writes, key=None):
        self.eng = eng
        self.fn = fn
        self.reads = reads
        self.writes = writes
        self.deps = ()
        self.sig = False
        self.cnt = 0
        self.key = key
        self.is_dma = key is not None


class Prog:
    def __init__(self, nc):
        self.nc = nc
        self.ops = []

    def op(self, eng, fn, reads=(), writes=()):
        o = Op(eng, fn, tuple(reads), tuple(writes))
        self.ops.append(o)
        return o

    def dma(self, queue, out, in_, reads=(), writes=(), key=None):
        o = Op(queue, (out, in_), tuple(reads), tuple(writes), key=key)
        self.ops.append(o)
        return o

    def _analyse(self):
        for o in self.ops:
            raw = set()
            oth = set()
            for r0 in o.reads:
                for r in r0.al:
                    if r.last_w is not None:
                        raw.add(r.last_w)
                    if r.psum:
                        for rd in r.readers:
                            if rd.eng != o.eng:
                                raw.add(rd)
            for w0 in o.writes:
                for w in w0.al:
                    if w.last_w is not None:
                        oth.add(w.last_w)
                    for rd in w.readers:
                        oth.add(rd)
            raw.discard(o)
            oth.discard(o)
            need = []
            for d in raw | oth:
                if (not d.is_dma) and (not o.is_dma) and d.eng == o.eng:
                    if o.eng == "pe":
                        continue
                need.append(d)
            o.deps = need
            for d in need:
                d.sig = True
            for r in o.reads:
                r.readers.append(o)
            for w in o.writes:
                w.last_w = o
                w.readers = []

    def emit(self, final_eng, final_keys):
        nc = self.nc
        self._analyse()
        cnt = {e: 0 for e in COMPUTE}
        kcnt = {}
        for o in self.ops:
            if o.is_dma:
                kcnt[o.key] = kcnt.get(o.key, 0) + 1
                o.cnt = 16 * kcnt[o.key]
            elif o.sig:
                cnt[o.eng] += 1
                o.cnt = cnt[o.eng]
        stack = contextlib.ExitStack()
        sems = {e: stack.enter_context(nc.semaphore("s_" + e)) for e in COMPUTE}
        ksems = {k: stack.enter_context(nc.semaphore("d_" + str(k))) for k in kcnt}
        per_eng = {}
        for o in self.ops:
            per_eng.setdefault(o.eng, []).append(o)

        def run(eng_name, eng):
            waited = {}
            for o in per_eng.get(eng_name, []):
                want = {}
                for d in o.deps:
                    s = ("k", d.key) if d.is_dma else ("e", d.eng)
                    if d.cnt > want.get(s, 0):
                        want[s] = d.cnt
                for s, v in want.items():
                    if waited.get(s, 0) >= v:
                        continue
                    waited[s] = v
                    eng.wait_ge(ksems[s[1]] if s[0] == "k" else sems[s[1]], v)
                if o.is_dma:
                    out, in_ = o.fn
                    eng.dma_start(out=out, in_=in_).then_inc(ksems[o.key], 16)
                else:
                    ins = o.fn(eng)
                    if o.sig:
                        ins.then_inc(sems[o.eng], 1)
            if eng_name == final_eng:
                for k in final_keys:
                    if k in kcnt:
                        eng.wait_ge(ksems[k], 16 * kcnt[k])

        with nc.Block() as block:
            @block.tensor
            def _(e):
                run("pe", e)

            @block.scalar
            def _(e):
                run("act", e)

            @block.vector
            def _(e):
                run("dve", e)

            @block.gpsimd
            def _(e):
                run("pool", e)

            @block.sync
            def _(e):
                run("sp", e)
        stack.close()


def _consts():
    c = {}
    pos = np.concatenate([np.arange(SEQ)] * NPS + [PAST + np.arange(DSEQ)] * NSS).astype(np.float32)
    inv = (10000.0 ** (-np.arange(0, 128, 2, dtype=np.float32) / 128.0)).astype(np.float32)
    ang = pos[None, :] * inv[:, None]
    cos = np.cos(ang).astype(np.float32)
    sin = np.sin(ang).astype(np.float32)
    c["rotC"] = np.ascontiguousarray(np.concatenate([cos, cos], 0))
    c["rotS"] = np.ascontiguousarray(np.concatenate([sin, -sin], 0))
    h = np.arange(4, dtype=np.float64)
    log_g = np.log1p(-np.power(2.0, -5.0 - h))
    i = np.arange(128, dtype=np.float64)
    diff = i[None, :] - i[:, None]
    m = np.where(diff[None] >= 0, np.exp(log_g[:, None, None] * np.maximum(diff[None], 0)), 0.0)
    c["maskR"] = np.ascontiguousarray((128 ** -0.5 * m).transpose(1, 0, 2)).astype(np.float32)
    c["mask01"] = np.ascontiguousarray(np.broadcast_to((diff >= 0)[:, None, :], (128, 4, 128))).astype(np.float32)
    c["triN"] = ((diff >= 0) * (-1.0 / 16.0)).astype(np.float32)
    c["utriN"] = ((diff.T > 0) * (-1.0 / 16.0)).astype(np.float32)
    xi = np.exp(log_g[:, None] * (i[None, :] + 1.0))
    c["xi"] = np.ascontiguousarray(np.broadcast_to(xi[None], (128, 4, 128))).astype(np.float32)
    z128 = 128 ** -0.5 * np.exp(log_g[None, :] * (127.0 - i[:, None]))
    z64 = np.zeros((128, 4))
    z64[:64] = 128 ** -0.5 * np.exp(log_g[None, :] * (63.0 - i[:64, None]))
    c["zeta"] = np.ascontiguousarray(np.stack([z128, z64], 1)).astype(np.float32)
    c["gL"] = {128: [float(np.exp(log_g[k] * 128)) for k in range(4)],
               64: [float(np.exp(log_g[k] * 64)) for k in range(4)]}
    c["ident"] = np.eye(128, dtype=np.float32).astype(ml_dtypes.bfloat16)
    c["ones_b"] = np.ones((128, 128), dtype=np.float32).astype(ml_dtypes.bfloat16)
    c["ones_f"] = np.ones((128, 128), dtype=np.float32)
    return c


class _Stop(Exception):
    pass


def build_nc(consts, stop_at=None):
    nc = bass.Bass("TRN2", target_bir_lowering=False)

    def stage(name):
        if stop_at is not None and name == stop_at:
            raise _Stop()

    P = Prog(nc)
    es = contextlib.ExitStack()

    def dram(name, shape, dt, kind="ExternalInput"):
        return nc.dram_tensor(name, list(shape), dt, kind=kind).ap()

    xT = dram("xT", [KC, 128, TOK], F32)
    cT = dram("cT", [128, KC, NSEQ], F32)
    st_ret_in = dram("st_ret", [NSS, 4, 128, 128], F32)
    st_gla_in = dram("st_gla", [NSS, 4, 64, 128], F32)
    cc_in = dram("cc_in", [128, FC, NSS, 2], F32)
    w_ada = dram("w_ada", [D, 6 * D], F32)
    b_ada = dram("b_ada", [128, 48], F32)
    gvec = dram("gvec", [128, 3, KC], F32)
    ghead = dram("ghead", [128, 2, 4], F32)
    w_in = dram("w_in", [D, DIN], F32)
    w_a2b = dram("w_a2b", [17, 256], F32)
    w_out = dram("w_out", [D, D], F32)
    w_f1 = dram("w_f1", [D, 2 * DFF], F32)
    convp = dram("convp", [128, FC, 4], F32)
    w_f2 = dram("w_f2", [DFF, D], F32)
    rotC_d = dram("rotC", [128, TOK], F32)
    rotS_d = dram("rotS", [128, TOK], F32)
    maskR_d = dram("maskR", [128, 4, 128], F32)
    mask01_d = dram("mask01", [128, 4, 128], F32)
    triN_d = dram("triN", [128, 128], F32)
    utriN_d = dram("utriN", [128, 128], F32)
    xi_d = dram("xi", [128, 4, 128], F32)
    zeta_d = dram("zeta", [128, 2, 4], F32)
    ident_d = dram("ident", [128, 128], BF16)
    onesb_d = dram("ones_b", [128, 128], BF16)
    onesf_d = dram("ones_f", [128, 128], F32)

    yT = dram("yT", [KC, 128, TOK], F32, "ExternalOutput")
    o_ret = dram("o_ret", [NSEQ, 4, 128, 128], F32, "ExternalOutput")
    o_gla = dram("o_gla", [NSEQ, 4, 64, 128], F32, "ExternalOutput")
    o_cc = dram("o_cc", [128, FC, NSEQ, 2], F32, "ExternalOutput")

    NG = 29
    scratch = dram("wscr", [NG, 128, 4096], BF16, "Internal")
    scr_res = [Res("scr%d" % g) for g in range(NG)]

    class T:
        rk = None

        def __init__(self, name, shape, dt, psum=False, res=None):
            if psum:
                self.t = es.enter_context(nc.psum_tensor(name, list(shape), dt))
            else:
                self.t = es.enter_context(nc.sbuf_tensor(name, list(shape), dt))
            self.r = res if res is not None else Res(name, psum)

    def ring(name, n, shape, dt, psum=False):
        return [T("%s%d" % (name, i), shape, dt, psum) for i in range(n)]

    PJ = ring("pj", 4, [128, 512], F32, True)
    SC = T("sc", [128, 512], F32, True)
    OC = T("oc", [128, 512], F32, True)
    MS = T("ms", [128, 512], F32, True)
    TR = T("tr", [128, 512], F32, True)
    STAT = [MS, SC, OC]

    c_maskR = T("c_maskR", [128, 4, 128], F32)
    c_mask01 = T("c_mask01", [128, 4, 128], F32)
    c_tri = T("c_tri", [128, 128], F32)
    c_utri = T("c_utri", [128, 128], F32)
    c_xi = T("c_xi", [128, 4, 128], F32)
    c_zeta = T("c_zeta", [128, 2, 4], F32)
    c_ident = T("c_ident", [128, 128], BF16)
    c_onesb = T("c_onesb", [128, 128], BF16)
    c_onesf = T("c_onesf", [128, 128], F32)
    c_bada = T("c_bada", [128, 48], F32)
    c_gvec = T("c_gvec", [128, 3, KC], F32)
    c_ghead = T("c_ghead", [128, 2, 4], F32)
    c_conv = T("c_conv", [128, FC, 4], F32)
    c_wa2 = T("c_wa2", [17, 256], F32)
    c_cT = T("c_cT", [128, KC, NSEQ], F32)
    c_scT = T("c_scT", [128, KC, NSEQ], BF16)
    modT = T("modT", [128, 48, NSEQ], F32)
    gm1 = T("gm1", [128, KC, NSEQ], F32)
    gm2 = T("gm2", [128, KC, NSEQ], F32)

    cq = 0

    def cload(dst, src):
        nonlocal cq
        P.dma("sp", dst.t[:], src, writes=[dst.r], key="c%d" % cq)
        cq += 1

    for dst, src in ((c_maskR, maskR_d), (c_mask01, mask01_d), (c_tri, triN_d), (c_utri, utriN_d),
                     (c_xi, xi_d), (c_zeta, zeta_d), (c_ident, ident_d), (c_onesb, onesb_d),
                     (c_onesf, onesf_d), (c_bada, b_ada), (c_gvec, gvec), (c_ghead, ghead),
                     (c_conv, convp), (c_wa2, w_a2b), (c_cT, cT)):
        cload(dst, src)

    x_sb = ring("x", 2, [128, KC, NTP], F32)
    for xb in x_sb:
        xb.rk = [Res(xb.r.name + "_k%d" % k) for k in range(KC)]
    rC = ring("rC", 1, [128, NTP], F32)
    rS = ring("rS", 1, [128, NTP], F32)
    hT = T("hT", [128, KC, NTP], BF16)
    hT.rk = [Res("hT_k%d" % k) for k in range(KC)]
    sqr = ring("sq", 4, [128, NTP], BF16)
    c_eps = T("c_eps", [128, 1], F32)
    tmpr = ring("tmp", 2, [128, NTP], F32)
    rsdr = ring("rsd", 2, [128, NTP], F32)
    rsbr = ring("rsb", 2, [128, NTP], F32)
    rsd, rsb = rsdr[0], rsbr[0]
    wsl = ring("wsl", NSLOT, [128, 4096], BF16)
    ga1 = T("ga1", [17, NTP], F32)
    lsp = T("lsp", [128, 4, 256], F32)
    Aex = T("Aex", [128, 2, NTP], F32)
    Ain = T("Ain", [128, 2, NTP], F32)
    E2 = T("E2", [128, 4, 256], F32)
    qg = T("qg", [64, 4, NTP], BF16)
    kg = T("kg", [64, 4, NTP], BF16)
    eLt = T("eLt", [64, 4, 4], F32)
    kraw = T("kraw", [128, 2, NTP], BF16)
    class V:
        def __init__(self, name, ap):
            self.t = ap
            self.r = Res(name)

    U1 = es.enter_context(nc.sbuf_tensor("U1", [128, FC * NTP], BF16))
    U2 = es.enter_context(nc.sbuf_tensor("U2", [128, 2080], F32))

    def u1v(name, lo, n, k):
        return V(name, U1[:, lo:lo + n].rearrange("p (k c) -> p k c", k=k))

    qrot = u1v("qrot", 0, 2048, 4)
    qxi = u1v("qxi", 2048, 2048, 4)
    krot = u1v("krot", 4096, 2048, 4)
    kztm = u1v("kztm", 6144, 2048, 4)
    k2tm = u1v("k2tm", 8192, 1024, 4)
    vtm = u1v("vtm", 9216, 2048, 4)
    yTt = [V("y%d" % i, U1[:, i * NTP:(i + 1) * NTP]) for i in range(FC)]
    alias([qrot], yTt[0:4])
    alias([qxi], yTt[4:8])
    alias([krot], yTt[8:12])
    alias([kztm], yTt[12:16])
    alias([k2tm], yTt[16:18])
    alias([vtm], yTt[18:22])
    t1r = tmpr
    t2r = ring("t2", 2, [128, NTP], F32)
    sg = V("sg", U2[:, 0:2048].rearrange("p (k c) -> p k c", k=4))
    bufr = [V("buf%d" % i, U2[:, i * 520:(i + 1) * 520]) for i in range(2)]
    c0r = [V("c0%d" % i, U2[:, 1040 + i * 512:1040 + (i + 1) * 512]) for i in range(2)]
    alias([sg], bufr + c0r)
    osb = T("osb", [128, 4, NTP], F32)
    stmr = ring("stmr", 4, [128, 4, 128], BF16)
    stmg = ring("stmg", 4, [128, 4, 128], BF16)
    vtm2 = T("vtm2", [128, 4, 512], BF16)
    osg = [V("osg%d" % i, x_sb[1 - i].t[:, 0:4, :]) for i in range(2)]
    sgg = [V("sgg%d" % i, x_sb[1 - i].t[:, 4:8, :]) for i in range(2)]
    class _R:
        def __init__(self, r):
            self.r = r

    for i in range(2):
        alias([osg[i]], [_R(r) for r in x_sb[1 - i].rk[0:4]])
        alias([sgg[i]], [_R(r) for r in x_sb[1 - i].rk[4:8]])
    mixT = [T("mix%d" % i, [128, NTP], BF16) for i in range(8)]
    Rst = ring("Rst", 2, [128, 4, 128], F32)
    Rbf = T("Rbf", [128, 4, 128], BF16)
    Gst = ring("Gst", 2, [64, 4, 128], F32)
    Gbf = T("Gbf", [64, 4, 128], BF16)
    hist = T("hist", [128, FC, NSS, 2], F32)

    def mm(out, lhsT, rhs, start, stop, rd, wr):
        P.op("pe", lambda e: e.matmul(out, lhsT, rhs, start=start, stop=stop), rd, wr)

    def trp(out, in_, rd, wr):
        P.op("pe", lambda e: e.transpose(out, in_, c_ident.t[:]), list(rd) + [c_ident.r], wr)

    def act(out, in_, func, rd, wr, bias=None, scale=None):
        kw = {}
        if bias is not None:
            kw["bias"] = bias
        if scale is not None:
            kw["scale"] = scale
        P.op("act", lambda e: e.activation(out=out, in_=in_, func=func, **kw), rd, wr)

    def tt(eng, out, in0, in1, op, rd, wr):
        P.op(eng, lambda e: e.tensor_tensor(out=out, in0=in0, in1=in1, op=op), rd, wr)

    def stt(eng, out, in0, scalar, in1, op0, op1, rd, wr):
        P.op(eng, lambda e: e.scalar_tensor_tensor(out=out, in0=in0, scalar=scalar, in1=in1,
                                                   op0=op0, op1=op1), rd, wr)

    def ts(eng, out, in0, s1, s2, op0, op1, rd, wr):
        P.op(eng, lambda e: e.tensor_scalar(out=out, in0=in0, scalar1=s1, scalar2=s2, op0=op0, op1=op1), rd, wr)

    def cp(eng, out, in_, rd, wr):
        P.op(eng, lambda e: e.tensor_copy(out=out, in_=in_), rd, wr)

    def mset(eng, ap, val, wr):
        P.op(eng, lambda e: e.memset(ap, val), (), wr)

    def recip(out, in_, rd, wr):
        P.op("dve", lambda e: e.reciprocal(out=out, in_=in_), rd, wr)

    tiles = []
    for s in range(NPS):
        for t in range(SEQ // NTP):
            tiles.append(dict(kind="p", col0=s * SEQ + t * NTP, NT=NTP, L=128, first=(t == 0),
                              last=(t == SEQ // NTP - 1), segs=[(s, 0, NTP)], slot=s % 2))
    tiles.append(dict(kind="s", col0=NPS * SEQ, NT=NSS * DSEQ, L=64, first=True, last=True,
                      segs=[(NPS + i, i * DSEQ, DSEQ) for i in range(NSS)], slot=None))

    def group_srcs(g):
        if g == 0:
            return [(0, 8, 16, w_in[:, 3584:3600].rearrange("(k p) c -> p k c", p=128))]
        if 1 <= g <= 7:
            c0 = (g - 1) * 512
            return [(0, 8, 512, w_in[:, c0:c0 + 512].rearrange("(k p) c -> p k c", p=128))]
        if g in (8, 9):
            c0 = (g - 8) * 512
            return [(0, 8, 512, w_out[:, c0:c0 + 512].rearrange("(k p) c -> p k c", p=128))]
        if 10 <= g <= 20:
            j = g - 10
            return [(0, 8, 256, w_f1[:, 256 * j:256 * j + 256].rearrange("(k p) c -> p k c", p=128)),
                    (8 * 256, 8, 256, w_f1[:, DFF + 256 * j:DFF + 256 * j + 256].rearrange("(k p) c -> p k c", p=128))]
        j = g - 21
        return [(0, FC, 128, w_f2[:, 128 * j:128 * j + 128].rearrange("(k p) c -> p k c", p=128))]

    sched = [("ada", i) for i in range(6)]
    GORDER = [0, 1, 2, 5, 3, 6, 4, 7, 8, 9] + list(range(10, 21)) + list(range(21, 29))
    sched.append(("w", 0, 0))
    sched += [("ada", i) for i in range(6, 12)]
    for ti in range(len(tiles)):
        for g in [1, 3, 2, 6, 5, 4, 7, 8, 9] + list(range(10, 21)):
            sched.append(("w", ti, g))
        for g in range(21, 24):
            sched.append(("w", ti, g))
        if ti + 1 < len(tiles):
            sched.append(("w", ti + 1, 0))
        for g in range(24, 29):
            sched.append(("w", ti, g))
    loaded = [0]

    def slot_view(si, nk, ncols, off=0):
        return wsl[si].t[:, off:off + nk * ncols].rearrange("p (k c) -> p k c", k=nk)

    def ensure(idx, la=NSLOT - 1):
        while loaded[0] <= min(idx + la, len(sched) - 1):
            i = loaded[0]
            si = i % NSLOT
            ent = sched[i]
            key = "w%d" % si
            if ent[0] == "ada":
                pc = ent[1]
                src = w_ada[:, pc * 512:(pc + 1) * 512].rearrange("(k p) c -> p k c", p=128)
                P.dma("pool", slot_view(si, 8, 512), src, writes=[wsl[si].r], key="wc%d" % si)
            else:
                _, ti, g = ent
                if ti == 0:
                    for (off, nk, ncols, src) in group_srcs(g):
                        P.dma("pool", slot_view(si, nk, ncols, off), src, writes=[wsl[si].r], key="wc%d" % si)
                    gsz = 128 if g == 0 else (FC * 128 if g >= 21 else 4096)
                    P.dma("sp", scratch[g][:, 0:gsz], wsl[si].t[:, 0:gsz], reads=[wsl[si].r], writes=[scr_res[g]],
                          key="ws%d" % si)
                else:
                    gsz = 128 if g == 0 else (FC * 128 if g >= 21 else 4096)
                    P.dma("sp", wsl[si].t[:, 0:gsz], scratch[g][:, 0:gsz], reads=[scr_res[g]], writes=[wsl[si].r], key=key)
            loaded[0] += 1

    spos = [0]

    def next_w(expect=None, la=NSLOT - 1):
        i = spos[0]
        spos[0] += 1
        assert expect is None or sched[i][-1] == expect or sched[i] == expect, (i, sched[i], expect)
        ensure(i, la)
        return i % NSLOT

    out_keys = set()

    def body():
        act(c_scT.t[:], c_cT.t[:], AF.Silu, [c_cT.r], [c_scT.r])

        def ada_piece(pc):
            si = next_w(("ada", pc))
            wv = slot_view(si, 8, 512)
            for jj in range(4):
                j = pc * 4 + jj
                for kc in range(KC):
                    mm(TR.t[:, j * NSEQ:(j + 1) * NSEQ], wv[:, kc, jj * 128:(jj + 1) * 128], c_scT.t[:, kc, :],
                       kc == 0, kc == KC - 1, [wsl[si].r, c_scT.r], [TR.r])

        def ada_evac(j0, j1):
            tt("dve", modT.t[:, j0:j1, :], TR.t[:, j0 * NSEQ:j1 * NSEQ].rearrange("p (j s) -> p j s", s=NSEQ),
               c_bada.t[:, j0:j1].unsqueeze(2).to_broadcast([128, j1 - j0, NSEQ]),
               ALU.add, [TR.r, c_bada.r], [modT.r])

        def ada_gm(gmx, gi, sc0):
            stt("dve", gmx.t[:], modT.t[:, sc0:sc0 + 8, :], 1.0,
                c_gvec.t[:, gi, :].unsqueeze(2).to_broadcast([128, KC, NSEQ]), ALU.add, ALU.mult,
                [modT.r, c_gvec.r], [gmx.r])

        for pc in range(4):
            ada_piece(pc)
        ada_evac(0, 16)
        ada_gm(gm1, 0, 8)
        stage("prologue")
        mset("pool", ga1.t[:], 1.0, [ga1.r])
        mset("pool", c_eps.t[:], EPS, [c_eps.r])

        SH1, G1, SH2, G2 = 0, 16, 24, 40

        def norm(*a, **k):
            for _ in norm_gen(*a, **k):
                pass

        def norm_gen(ti, xs, NT, segs, gm, sh0, out_fn, final=False, sq_pool=False):
            st = TR if sq_pool else next_bank()

            def sqmm(kc, do_sq, do_mm):
                sqk = sqr[kc % 4]
                if do_sq:
                    if sq_pool:
                        tt("pool", sqk.t[:, 0:NT], xs.t[:, kc, 0:NT], xs.t[:, kc, 0:NT], ALU.mult, [xs.rk[kc]], [sqk.r])
                    else:
                        act(sqk.t[:, 0:NT], xs.t[:, kc, 0:NT], AF.Square, [xs.rk[kc]], [sqk.r])
                if do_mm:
                    mm(f32v(st)[:, 0:NT], c_onesb.t[:], sqk.t[:, 0:NT], kc == 0, kc == KC - 1, [c_onesb.r, sqk.r], [st.r])

            if sq_pool:
                for kc in range(4):
                    sqmm(kc, True, False)
                yield
                for kc in range(4):
                    sqmm(kc, False, True)
                for kc in range(4, 8):
                    sqmm(kc, True, False)
                yield
                for kc in range(4, 8):
                    sqmm(kc, False, True)
                yield
            else:
                for kc in range(KC):
                    sqmm(kc, True, True)
                yield
            rb = rstd_from(st, NT, 1.0 / D)
            for kc in range(KC):
                if final:
                    stt("dve", out_fn(kc)[0], xs.t[:, kc, 0:NT], c_gvec.t[:, 2, kc:kc + 1], rb.t[:, 0:NT],
                        ALU.mult, ALU.mult, [xs.rk[kc], c_gvec.r, rb.r], [out_fn(kc)[1]])
                    continue
                tm = tmpr[kc % 2]
                if len(segs) > 1:
                    ns_, s0_ = len(segs), segs[0][0]
                    ls_ = NT // ns_
                    v3 = lambda ap: ap.rearrange("p (s l) -> p s l", s=ns_)
                    o_ap, o_r = out_fn(kc)
                    tt("dve", v3(tm.t[:, 0:NT]), v3(xs.t[:, kc, 0:NT]), v3(rb.t[:, 0:NT]), ALU.mult,
                       [xs.rk[kc], rb.r], [tm.r])
                    tt("dve", v3(tm.t[:, 0:NT]), v3(tm.t[:, 0:NT]),
                       gm.t[:, kc, s0_:s0_ + ns_].unsqueeze(2).to_broadcast([128, ns_, ls_]), ALU.mult, [tm.r, gm.r], [tm.r])
                    tt("pool", v3(o_ap[:, 0:NT]), v3(tm.t[:, 0:NT]),
                       modT.t[:, sh0 + kc, s0_:s0_ + ns_].unsqueeze(2).to_broadcast([128, ns_, ls_]), ALU.add,
                       [tm.r, modT.r], [o_r])
                    continue
                for (sid, c0, n) in segs:
                    stt("dve", tm.t[:, c0:c0 + n], xs.t[:, kc, c0:c0 + n], gm.t[:, kc, sid:sid + 1],
                        rb.t[:, c0:c0 + n], ALU.mult, ALU.mult, [xs.rk[kc], gm.r, rb.r], [tm.r])
                    o_ap, o_r = out_fn(kc)
                    act(o_ap[:, c0:c0 + n], tm.t[:, c0:c0 + n], AF.Identity, [tm.r, modT.r], [o_r],
                        bias=modT.t[:, sh0 + kc, sid:sid + 1])

        rsc = [0]

        def rstd_from(st, NT, scale):
            i = rsc[0] % 4
            rsc[0] += 1
            rd = (rsdr + rsbr)[i]
            act(rd.t[:, 0:NT], f32v(st)[:, 0:NT], AF.Ln, [st.r, c_eps.r], [rd.r], bias=c_eps.t[:, 0:1], scale=scale)
            act(rd.t[:, 0:NT], rd.t[:, 0:NT], AF.Exp, [rd.r], [rd.r], scale=-0.5)
            return rd

        bkc = [0]
        BANKS = [PJ[0], SC, PJ[1], OC, PJ[2], MS, PJ[3]]

        def next_bank():
            b = BANKS[bkc[0] % 7]
            bkc[0] += 1
            return b

        def f32v(b):
            return b.t[:, :]

        def bf16v(b):
            return b.t[:, :].bitcast(BF16)

        next_pj = next_bank

        def proj_fm(si, col, NT, act_t, nk=KC, rhs_fn=None):
            b = next_pj()
            wv = wsl[si]
            for kc in range(nk):
                lhsT = wv.t[:, kc * (col[1]) + col[0]: kc * (col[1]) + col[0] + col[2]]
                rhs, rr = rhs_fn(kc)
                mm(b.t[0:col[2], 0:NT], lhsT, rhs, kc == 0, kc == nk - 1, [wv.r, rr], [b.r])
            return b

        n_tiles = len(tiles)

        def load_x(ti):
            tl = tiles[ti]
            NT, c0 = tl["NT"], tl["col0"]
            xs = x_sb[ti % 2]
            P.dma("pool", xs.t[:, :, 0:NT], xT[:, :, c0:c0 + NT].rearrange("k p t -> p k t"), writes=xs.rk,
                  key="x%d" % (ti % 2))

        def load_rot(ti):
            tl = tiles[ti]
            NT, c0 = tl["NT"], tl["col0"]
            P.dma("pool", rC[0].t[:, 0:NT], rotC_d[:, c0:c0 + NT], writes=[rC[0].r], key="rc")
            P.dma("pool", rS[0].t[:, 0:NT], rotS_d[:, c0:c0 + NT], writes=[rS[0].r], key="rs")

        load_x(0)
        load_rot(0)

        def t_front(ti):
            tl = tiles[ti]
            NT, L, segs = tl["NT"], tl["L"], tl["segs"]
            xs = x_sb[ti % 2]
            nch = NT // L
            zsel = 0 if L == 128 else 1
            if tl["kind"] == "p":
                ch_slot = [tl["slot"]] * nch
                ch_seq = [segs[0][0]] * nch
            else:
                ch_slot = [i % 2 for i in range(nch)]
                ch_seq = [segs[i][0] for i in range(nch)]
            par = ti % 2
            OSG, SGG = osg[par], sgg[par]

            def h_rhs(kc):
                return hT.t[:, kc, 0:NT], hT.rk[kc]


            yield from norm_gen(ti, xs, NT, segs, gm1, SH1, lambda kc: (hT.t[:, kc, :], hT.rk[kc]), sq_pool=(ti > 0))
            if tl["kind"] == "p" and tl["first"]:
                sl = tl["slot"]
                mset("pool", Rst[sl].t[:], 0.0, [Rst[sl].r])
                mset("pool", Gst[sl].t[:], 0.0, [Gst[sl].r])
                mset("pool", hist.t[:], 0.0, [hist.r])
            if tl["kind"] == "s":
                P.dma("pool", hist.t[:], cc_in, writes=[hist.r], key="hist")

            stage("t%d_init" % ti)
            stage("t%d_norm1" % ti)
            si = next_w(0)
            b = next_pj()
            for kc in range(KC):
                mm(b.t[0:16, 0:NT], wsl[si].t[:, kc * 16:(kc + 1) * 16], hT.t[:, kc, 0:NT], kc == 0, kc == KC - 1,
                   [wsl[si].r, hT.rk[kc]], [b.r])
            act(ga1.t[0:16, 0:NT], b.t[0:16, 0:NT], AF.Copy, [b.r], [ga1.r])
            yield
            for c in range(nch):
                cs = slice(c * L, (c + 1) * L)
                bu_ = next_bank()
                mm(f32v(bu_)[0:L, 0:256], ga1.t[:, cs], c_wa2.t[:], True, True, [ga1.r, c_wa2.r], [bu_.r])
                act(E2.t[0:L, c, :], f32v(bu_)[0:L, 0:256], AF.Exp, [bu_.r], [E2.r], scale=-1.0)
                act(lsp.t[0:L, c, :], E2.t[0:L, c, :], AF.Ln, [E2.r], [lsp.r], bias=1.0)
            yield
            for c in range(nch):
                cs = slice(c * L, (c + 1) * L)
                bb_ = next_bank()
                for p in range(2):
                    mm(f32v(bb_)[:, p * 128:p * 128 + L], lsp.t[0:L, c, p * 128:(p + 1) * 128], c_tri.t[0:L, 0:L],
                       True, True, [lsp.r, c_tri.r], [bb_.r])
                bview = f32v(bb_)[:, 0:256].rearrange("p (a t) -> p a t", a=2)[:, :, 0:L]
                act(Aex.t[:, :, cs], bview, AF.Exp, [bb_.r], [Aex.r])
                act(Ain.t[:, :, cs], bview, AF.Exp, [bb_.r], [Ain.r], scale=-1.0)
                bl_ = next_bank()
                mm(f32v(bl_)[0:L, 0:256], c_utri.t[0:L, 0:L], lsp.t[0:L, c, :], True, True, [c_utri.r, lsp.r], [bl_.r])
                act(E2.t[0:L, c, :], f32v(bl_)[0:L, 0:256], AF.Exp, [bl_.r], [E2.r])
            stage("t%d_glaprep" % ti)

        def t_mid(ti, pre_last=None):
            tl = tiles[ti]
            NT, L, segs = tl["NT"], tl["L"], tl["segs"]
            xs = x_sb[ti % 2]
            nch = NT // L
            zsel = 0 if L == 128 else 1
            if tl["kind"] == "p":
                ch_slot = [tl["slot"]] * nch
                ch_seq = [segs[0][0]] * nch
            else:
                ch_slot = [i % 2 for i in range(nch)]
                ch_seq = [segs[i][0] for i in range(nch)]
            par = ti % 2
            OSG, SGG = osg[par], sgg[par]

            def h_rhs(kc):
                return hT.t[:, kc, 0:NT], hT.rk[kc]


            PEX = "dve" if ti == 0 else "pool"

            def proj_v(vdst, g):
                si = next_w(g)
                for c in range(nch):
                    b = next_pj()
                    for kc in range(KC):
                        mm(b.t[0:L, 0:512], hT.t[:, kc, c * L:(c + 1) * L], wsl[si].t[:, kc * 512:(kc + 1) * 512],
                           kc == 0, kc == KC - 1, [hT.rk[kc], wsl[si].r], [b.r])
                    act(vdst.t[0:L, c, :], b.t[0:L, 0:512], AF.Copy, [b.r], [vdst.r])

            def gen_gates(gdst, g):
                si = next_w(g)
                for h in range(4):
                    b = proj_fm(si, (h * 128, 512, 128), NT, None, rhs_fn=h_rhs)
                    act(gdst.t[:, h, 0:NT], b.t[:, 0:NT], AF.Silu, [b.r], [gdst.r])
                    yield

            for gi, dst, vdst, gv in ((1, qrot, vtm, 3), (2, krot, vtm2, 6)):
                si = next_w(gi)
                for h in range(4):
                    b = proj_fm(si, (h * 128, 512, 128), NT, None, rhs_fn=h_rhs)
                    t1, t2 = t1r[h % 2], t2r[h % 2]
                    tt("dve", t1.t[:, 0:NT], b.t[:, 0:NT], rC[0].t[:, 0:NT], ALU.mult, [b.r, rC[0].r], [t1.r])
                    tt("dve", t2.t[0:64, 0:NT], b.t[64:128, 0:NT], rS[0].t[64:128, 0:NT], ALU.mult,
                       [b.r, rS[0].r], [t2.r])
                    tt("dve", t2.t[64:128, 0:NT], b.t[0:64, 0:NT], rS[0].t[0:64, 0:NT], ALU.mult,
                       [b.r, rS[0].r], [t2.r])
                    tt(PEX, dst.t[:, h, 0:NT], t1.t[:, 0:NT], t2.t[:, 0:NT], ALU.add, [t1.r, t2.r], [dst.r])
                proj_v(vdst, gv)
            if ti + 1 < n_tiles:
                load_rot(ti + 1)
            tt(PEX, qxi.t[:, :, 0:NT].rearrange("p h (c l) -> p h c l", l=L),
               qrot.t[:, :, 0:NT].rearrange("p h (c l) -> p h c l", l=L),
               c_xi.t[:, :, 0:L].unsqueeze(2).to_broadcast([128, 4, nch, L]), ALU.mult, [qrot.r, c_xi.r], [qxi.r])
            stage("t%d_retproj" % ti)

            si = next_w(5)
            for p in range(2):
                b = proj_fm(si, (p * 128, 512, 128), NT, None, rhs_fn=h_rhs)
                for two in range(2):
                    rows = slice(two * 64, two * 64 + 64)
                    stt("dve", qg.t[0:64, 2 * p + two, 0:NT], b.t[rows, 0:NT], 0.125, Aex.t[rows, p, 0:NT],
                        ALU.mult, ALU.mult, [b.r, Aex.r], [qg.r])
            for p in range(2):
                b = proj_fm(si, (256 + p * 128, 512, 128), NT, None, rhs_fn=h_rhs)
                for two in range(2):
                    rows = slice(two * 64, two * 64 + 64)
                    tt("dve", kg.t[0:64, 2 * p + two, 0:NT], b.t[rows, 0:NT], Ain.t[rows, p, 0:NT], ALU.mult,
                       [b.r, Ain.r], [kg.r])
                act(kraw.t[:, p, 0:NT], b.t[:, 0:NT], AF.Copy, [b.r], [kraw.r])
            stage("t%d_g5" % ti)

            def gen_passA():
                for c in range(nch):
                    cs = slice(c * L, (c + 1) * L)
                    bt = next_bank()
                    for h in range(4):
                        trp(bf16v(bt)[0:L, h * 128:(h + 1) * 128], krot.t[:, h, cs], [krot.r], [bt.r])
                    tt("dve", kztm.t[0:L, c, :].rearrange("p (h d) -> p h d", h=4),
                       bf16v(bt)[0:L, 0:512].rearrange("p (h d) -> p h d", h=4),
                       c_zeta.t[0:L, zsel, :].unsqueeze(2).to_broadcast([L, 4, 128]), ALU.mult, [bt.r, c_zeta.r], [kztm.r])
                    bs = next_bank()
                    for h in range(4):
                        mm(f32v(bs)[0:L, h * 128:h * 128 + L], krot.t[:, h, cs], qrot.t[:, h, cs], True, True,
                           [krot.r, qrot.r], [bs.r])
                    tt("dve", stmr[c].t[0:L, :, 0:L], f32v(bs)[0:L, :].rearrange("p (h t) -> p h t", h=4)[:, :, 0:L],
                       c_maskR.t[0:L, :, 0:L], ALU.mult, [bs.r, c_maskR.r], [stmr[c].r])
                    bt = next_bank()
                    for p in range(2):
                        trp(bf16v(bt)[0:L, p * 128:(p + 1) * 128], kraw.t[:, p, cs], [kraw.r], [bt.r])
                    tt("dve", k2tm.t[0:L, c, :], bf16v(bt)[0:L, 0:256], E2.t[0:L, c, :], ALU.mult, [bt.r, E2.r], [k2tm.r])
                    bs = next_bank()
                    for h in range(4):
                        mm(f32v(bs)[0:L, h * 128:h * 128 + L], kg.t[0:64, h, cs], qg.t[0:64, h, cs], True, True,
                           [kg.r, qg.r], [bs.r])
                    tt("dve", stmg[c].t[0:L, :, 0:L], f32v(bs)[0:L, :].rearrange("p (h t) -> p h t", h=4)[:, :, 0:L],
                       c_mask01.t[0:L, :, 0:L], ALU.mult, [bs.r, c_mask01.r], [stmg[c].r])
                    yield

            def interleave(a, b, ratio):
                for _ in a:
                    for _ in range(ratio):
                        next(b, None)
                for _ in b:
                    pass

            interleave(gen_passA(), gen_gates(sg, 4), 1)
            stage("t%d_passA" % ti)

            def gen_passB():
                for c in range(nch):
                    cs = slice(c * L, (c + 1) * L)
                    sl = ch_slot[c]
                    R, G = Rst[sl], Gst[sl]
                    if tl["kind"] == "s":
                        sid = ch_seq[c] - NPS
                        P.dma("pool", R.t[:], st_ret_in[sid].rearrange("h d e -> d h e"), writes=[R.r], key="rin%d" % sl)
                        P.dma("pool", G.t[:], st_gla_in[sid].rearrange("h d e -> d h e"), writes=[G.r], key="gin%d" % sl)
                    cp("dve", Rbf.t[:], R.t[:], [R.r], [Rbf.r])
                    cp("dve", Gbf.t[:], G.t[:], [G.r], [Gbf.r])
                    bu_ = next_bank()
                    for h in range(4):
                        mm(f32v(bu_)[:, h * 128:(h + 1) * 128], kztm.t[0:L, c, h * 128:(h + 1) * 128],
                           vtm.t[0:L, c, h * 128:(h + 1) * 128], True, True, [kztm.r, vtm.r], [bu_.r])
                    for h in range(4):
                        stt("dve", R.t[:, h, :], R.t[:, h, :], consts["gL"][L][h], f32v(bu_)[:, h * 128:(h + 1) * 128],
                            ALU.mult, ALU.add, [R.r, bu_.r], [R.r])
                    bu_ = next_bank()
                    for h in range(4):
                        mm(f32v(bu_)[0:64, h * 128:(h + 1) * 128], k2tm.t[0:L, c, h * 64:(h + 1) * 64],
                           vtm2.t[0:L, c, h * 128:(h + 1) * 128], True, True, [k2tm.r, vtm2.r], [bu_.r])
                    lastc = c * L + L - 1
                    for h in range(4):
                        p, two = h // 2, h % 2
                        rows = slice(two * 64, two * 64 + 64)
                        cp("dve", eLt.t[0:64, h, c:c + 1], Aex.t[rows, p, lastc:lastc + 1], [Aex.r], [eLt.r])
                    for h in range(4):
                        stt("dve", G.t[0:64, h, :], G.t[0:64, h, :], eLt.t[0:64, h, c:c + 1],
                            f32v(bu_)[0:64, h * 128:(h + 1) * 128], ALU.mult, ALU.add, [G.r, eLt.r, bu_.r], [G.r])
                    bo = next_bank()
                    for h in range(4):
                        mm(f32v(bo)[:, h * 128:h * 128 + L], vtm.t[0:L, c, h * 128:(h + 1) * 128], stmr[c].t[0:L, h, 0:L],
                           True, False, [vtm.r, stmr[c].r], [bo.r])
                        mm(f32v(bo)[:, h * 128:h * 128 + L], Rbf.t[:, h, :], qxi.t[:, h, cs], False, True,
                           [Rbf.r, qxi.r], [bo.r])
                    act(osb.t[:, :, cs], f32v(bo).rearrange("p (h t) -> p h t", h=4)[:, :, 0:L], AF.Copy, [bo.r], [osb.r])
                    bo = next_bank()
                    for h in range(4):
                        mm(f32v(bo)[:, h * 128:h * 128 + L], vtm2.t[0:L, c, h * 128:(h + 1) * 128], stmg[c].t[0:L, h, 0:L],
                           True, False, [vtm2.r, stmg[c].r], [bo.r])
                        mm(f32v(bo)[:, h * 128:h * 128 + L], Gbf.t[0:64, h, :], qg.t[0:64, h, cs], False, True,
                           [Gbf.r, qg.r], [bo.r])
                    act(OSG.t[:, :, cs], f32v(bo).rearrange("p (h t) -> p h t", h=4)[:, :, 0:L], AF.Copy, [bo.r], [OSG.r])
                    if tl["kind"] == "s" or (tl["last"] and c == nch - 1):
                        seq = ch_seq[c]
                        out_keys.add("oret%d" % sl)
                        out_keys.add("ogla%d" % sl)
                        P.dma("pool", o_ret[seq].rearrange("h d e -> d h e"), R.t[:], reads=[R.r], key="oret%d" % sl)
                        P.dma("pool", o_gla[seq].rearrange("h d e -> d h e"), G.t[:], reads=[G.r], key="ogla%d" % sl)
                    yield

            interleave(gen_passB(), gen_gates(SGG, 7), 1)
            stage("t%d_passB" % ti)


            for h in range(4):
                st = next_bank()
                mm(f32v(st)[:, 0:NT], c_onesf.t[:], osb.t[:, h, 0:NT], True, True, [c_onesf.r, osb.r], [st.r])
                stt("dve", osb.t[:, h, 0:NT], f32v(st)[:, 0:NT], -1.0 / 128, osb.t[:, h, 0:NT], ALU.mult, ALU.add,
                    [st.r, osb.r], [osb.r])
                sqa = sqr[h % 2]
                act(sqa.t[:, 0:NT], osb.t[:, h, 0:NT], AF.Square, [osb.r], [sqa.r])
                st2 = next_bank()
                mm(f32v(st2)[:, 0:NT], c_onesb.t[:], sqa.t[:, 0:NT], True, True, [c_onesb.r, sqa.r], [st2.r])
                rb = rstd_from(st2, NT, 1.0 / 128)
                tm = tmpr[h % 2]
                stt("dve", tm.t[:, 0:NT], osb.t[:, h, 0:NT], c_ghead.t[:, 0, h:h + 1], rb.t[:, 0:NT], ALU.mult, ALU.mult,
                    [osb.r, c_ghead.r, rb.r], [tm.r])
                tt(PEX, mixT[h].t[:, 0:NT], tm.t[:, 0:NT], sg.t[:, h, 0:NT], ALU.mult, [tm.r, sg.r], [mixT[h].r])
                sqa = sqr[2 + h % 2]
                act(sqa.t[:, 0:NT], OSG.t[:, h, 0:NT], AF.Square, [OSG.r], [sqa.r])
                st = next_bank()
                mm(f32v(st)[:, 0:NT], c_onesb.t[:], sqa.t[:, 0:NT], True, True, [c_onesb.r, sqa.r], [st.r])
                rb = rstd_from(st, NT, 1.0 / 128)
                tm = t2r[h % 2]
                stt("dve", tm.t[:, 0:NT], OSG.t[:, h, 0:NT], c_ghead.t[:, 1, h:h + 1], rb.t[:, 0:NT], ALU.mult, ALU.mult,
                    [OSG.r, c_ghead.r, rb.r], [tm.r])
                tt(PEX, mixT[4 + h].t[:, 0:NT], tm.t[:, 0:NT], SGG.t[:, h, 0:NT], ALU.mult, [tm.r, SGG.r],
                   [mixT[4 + h].r])
            stage("t%d_gla" % ti)
            if ti + 1 < n_tiles:
                load_x(ti + 1)

            def mix_rhs(kc):
                return mixT[kc].t[:, 0:NT], mixT[kc].r

            for g2 in range(2):
                si = next_w(8 + g2)
                for m in range(4):
                    j = g2 * 4 + m
                    b = proj_fm(si, (m * 128, 512, 128), NT, None, rhs_fn=mix_rhs)
                    for (sid, c0, n) in segs:
                        stt("dve", xs.t[:, j, c0:c0 + n], b.t[:, c0:c0 + n], modT.t[:, G1 + j, sid:sid + 1],
                            xs.t[:, j, c0:c0 + n], ALU.mult, ALU.add, [b.r, modT.r, xs.rk[j]], [xs.rk[j]])

            stage("t%d_wout" % ti)
            norm(ti, xs, NT, segs, gm2, SH2, lambda kc: (hT.t[:, kc, :], hT.rk[kc]))
            nseg = len(segs)
            Ls = NT // nseg
            for g in range(11):
                if g in (4, 7, 10) and pre_last is not None:
                    pre_last()
                si = next_w(10 + g)
                banks = []
                for m in range(4):
                    banks.append(proj_fm(si, ((m // 2) * 2048 + (m % 2) * 128, 256, 128), NT, None, rhs_fn=h_rhs))
                for q in range(2):
                    j = 2 * g + q
                    ba, bu = banks[q], banks[2 + q]
                    B = bufr[j % 2]
                    Bv = B.t[:, 0:nseg * (Ls + 2)].rearrange("p (s l) -> p s l", s=nseg)
                    C0 = c0r[j % 2]
                    C0v = C0.t[:, 0:NT].rearrange("p (s l) -> p s l", s=nseg)
                    cp(PEX, Bv[:, :, 0:2], hist.t[:, j, 0:nseg, :], [hist.r], [B.r])
                    act(Bv[:, :, 2:Ls + 2], ba.t[:, 0:NT].rearrange("p (s l) -> p s l", s=nseg), AF.Copy, [ba.r], [B.r])
                    act(C0v, Bv[:, :, 0:Ls], AF.Identity, [B.r, c_conv.r], [C0.r],
                        bias=c_conv.t[:, j, 3:4], scale=c_conv.t[:, j, 0:1])
                    stt("dve", C0v, Bv[:, :, 1:Ls + 1], c_conv.t[:, j, 1:2], C0v, ALU.mult, ALU.add,
                        [B.r, c_conv.r, C0.r], [C0.r])
                    stt("dve", C0v, Bv[:, :, 2:Ls + 2], c_conv.t[:, j, 2:3], C0v, ALU.mult, ALU.add,
                        [B.r, c_conv.r, C0.r], [C0.r])
                    cp(PEX, hist.t[:, j, 0:nseg, :], Bv[:, :, Ls:Ls + 2], [B.r], [hist.r])
                    act(C0.t[:, 0:NT], C0.t[:, 0:NT], AF.Silu, [C0.r], [C0.r])
                    tt("dve", yTt[j].t[:, 0:NT], C0.t[:, 0:NT], bu.t[:, 0:NT], ALU.mult, [C0.r, bu.r], [yTt[j].r])
            if tl["last"]:
                if tl["kind"] == "p":
                    s0 = segs[0][0]
                    out_keys.add("occ")
                    P.dma("pool", o_cc[:, :, s0:s0 + 1, :], hist.t[:, :, 0:1, :], reads=[hist.r], key="occ")
                else:
                    out_keys.add("occ")
                    P.dma("pool", o_cc[:, :, NPS:NPS + NSS, :], hist.t[:], reads=[hist.r], key="occ")

            stage("t%d_ffn1" % ti)

        def t_back(ti):
            tl = tiles[ti]
            NT, L, segs = tl["NT"], tl["L"], tl["segs"]
            xs = x_sb[ti % 2]
            nch = NT // L
            zsel = 0 if L == 128 else 1
            if tl["kind"] == "p":
                ch_slot = [tl["slot"]] * nch
                ch_seq = [segs[0][0]] * nch
            else:
                ch_slot = [i % 2 for i in range(nch)]
                ch_seq = [segs[i][0] for i in range(nch)]
            par = ti % 2
            OSG, SGG = osg[par], sgg[par]

            def h_rhs(kc):
                return hT.t[:, kc, 0:NT], hT.rk[kc]

            def y_rhs(kc):
                return yTt[kc].t[:, 0:NT], yTt[kc].r

            def resid(j, b):
                for (sid, c0, n) in segs:
                    stt("dve", xs.t[:, j, c0:c0 + n], b.t[:, c0:c0 + n], modT.t[:, G2 + j, sid:sid + 1],
                        xs.t[:, j, c0:c0 + n], ALU.mult, ALU.add, [b.r, modT.r, xs.rk[j]], [xs.rk[j]])

            s0_, s1_ = next_w(21), next_w(22, la=NSLOT - 2)
            b0_, b1_ = next_pj(), next_pj()
            KSPLIT = 16
            for (lo, hi) in ((0, KSPLIT), (KSPLIT, FC)):
                for (si, b) in ((s0_, b0_), (s1_, b1_)):
                    for kc in range(lo, hi):
                        mm(b.t[:, 0:NT], wsl[si].t[:, kc * 128:(kc + 1) * 128], yTt[kc].t[:, 0:NT],
                           kc == 0, kc == FC - 1, [wsl[si].r, yTt[kc].r], [b.r])
            resid(0, b0_)
            yield
            resid(1, b1_)
            yield
            for j in range(2, KC):
                si = next_w(21 + j)
                b = proj_fm(si, (0, 128, 128), NT, None, nk=FC, rhs_fn=y_rhs)
                resid(j, b)
                yield

            stage("t%d_ffn2" % ti)
            norm(ti, xs, NT, segs, None, None, lambda kc: (xs.t[:, kc, 0:NT], xs.rk[kc]), final=True)
            k = "y%d" % (ti % 2)
            out_keys.add(k)
            c0 = tl["col0"]
            P.dma("pool", yT[:, :, c0:c0 + NT].rearrange("k p t -> p k t"), xs.t[:, :, 0:NT], reads=xs.rk, key=k)
            stage("t%d_end" % ti)

        fg0 = t_front(0)
        for pc in range(4, 12):
            ada_piece(pc)
            if pc in (4, 5, 6, 7):
                next(fg0, None)
        for _ in fg0:
            pass
        ada_evac(16, 48)
        ada_gm(gm2, 1, 32)
        for ti in range(n_tiles):
            fg = t_front(ti + 1) if ti + 1 < n_tiles else iter(())
            t_mid(ti, pre_last=lambda: next(fg, None))
            bg = t_back(ti)
            plan = {2: 1, 3: 1, 5: 1}
            for j in range(KC):
                next(bg)
                for _ in range(plan.get(j, 0)):
                    next(fg, None)
            for _ in fg:
                pass
            for _ in bg:
                pass

    try:
        body()
    except _Stop:
        pass
    P.emit("pool", sorted(out_keys))
    print("[kernel] sbuf bytes remaining/partition:", nc.sbuf_bytes_remaining, flush=True)
    es.close()
    return nc


_CACHE = {}


def _prep(x_prompt, x_sample, c_prompt, c_sample, state_ret, state_gla, cache_ffn_conv,
          w_ada, b_ada, g_norm_mix, w_in, w_a2, b_a2, g_ret_norm, g_gla_norm, w_out,
          g_norm_ffn, w_ffn_in, conv_w, conv_b, w_ffn_out, g_final, cores=None):
    f = np.float32
    A = lambda a: np.ascontiguousarray(np.asarray(a, dtype=f))
    if "c" not in _CACHE:
        _CACHE["c"] = _consts()
    consts = _CACHE["c"]
    x_prompt, x_sample = A(x_prompt), A(x_sample)
    c_prompt, c_sample = A(c_prompt), A(c_sample)
    shared = {
        "w_ada": A(w_ada[0]),
        "b_ada": A(A(b_ada[0]).reshape(48, 128).T),
        "gvec": A(np.stack([A(g_norm_mix[0]).reshape(KC, 128).T, A(g_norm_ffn[0]).reshape(KC, 128).T,
                            A(g_final).reshape(KC, 128).T], 1)),
        "ghead": A(np.stack([A(g_ret_norm[0]).reshape(4, 128).T, A(g_gla_norm[0]).reshape(4, 128).T], 1)),
        "w_in": A(w_in[0]),
        "w_a2b": A(np.concatenate([A(w_a2[0]), A(b_a2[0])[None, :]], 0)),
        "w_out": A(w_out[0]),
        "w_f1": A(w_ffn_in[0]),
        "convp": A(np.concatenate([A(conv_w[0]), A(conv_b[0])[None, :]], 0).reshape(4, FC, 128).transpose(2, 1, 0)),
        "w_f2": A(w_ffn_out[0]),
        "rotC": consts["rotC"], "rotS": consts["rotS"], "maskR": consts["maskR"], "mask01": consts["mask01"],
        "triN": consts["triN"], "utriN": consts["utriN"], "xi": consts["xi"], "zeta": consts["zeta"],
        "ident": consts["ident"], "ones_b": consts["ones_b"], "ones_f": consts["ones_f"],
    }
    in_maps = []
    for c in (range(NCORE) if cores is None else cores):
        xp = x_prompt[c * NPS:(c + 1) * NPS]
        xsm = x_sample[c * NSS:(c + 1) * NSS]
        cols = np.concatenate([xp.reshape(NPS * SEQ, D), xsm.reshape(NSS * DSEQ, D)], 0)
        xTc = A(cols.T.reshape(KC, 128, TOK))
        cc = np.concatenate([c_prompt[c * NPS:(c + 1) * NPS], c_sample[c * NSS:(c + 1) * NSS]], 0)
        cTc = A(cc.T.reshape(KC, 128, NSEQ).transpose(1, 0, 2))
        cci = A(cache_ffn_conv[0, c * NSS:(c + 1) * NSS])
        cci = A(cci.transpose(2, 0, 1).reshape(FC, 128, NSS, 2).transpose(1, 0, 2, 3))
        m = dict(shared)
        m.update({"xT": xTc, "cT": cTc, "st_ret": A(state_ret[0, c * NSS:(c + 1) * NSS]),
                  "st_gla": A(state_gla[0, c * NSS:(c + 1) * NSS]), "cc_in": cci})
        in_maps.append(m)
    return in_maps


def _gather(results, ncores=NCORE):
    f = np.float32
    B, DB = ncores * NPS, ncores * NSS
    y_p = np.empty((B, SEQ, D), f)
    y_s = np.empty((DB, DSEQ, D), f)
    r_p = np.empty((1, B, 4, 128, 128), f)
    g_p = np.empty((1, B, 4, 64, 128), f)
    c_p = np.empty((1, B, 2, DFF), f)
    r_s = np.empty((1, DB, 4, 128, 128), f)
    g_s = np.empty((1, DB, 4, 64, 128), f)
    c_s = np.empty((1, DB, 2, DFF), f)
    for c in range(ncores):
        r = results[c]
        yc = np.asarray(r["yT"]).reshape(D, TOK).T
        y_p[c * NPS:(c + 1) * NPS] = yc[:NPS * SEQ].reshape(NPS, SEQ, D)
        y_s[c * NSS:(c + 1) * NSS] = yc[NPS * SEQ:].reshape(NSS, DSEQ, D)
        orr, og = np.asarray(r["o_ret"]), np.asarray(r["o_gla"])
        r_p[0, c * NPS:(c + 1) * NPS] = orr[:NPS]
        r_s[0, c * NSS:(c + 1) * NSS] = orr[NPS:]
        g_p[0, c * NPS:(c + 1) * NPS] = og[:NPS]
        g_s[0, c * NSS:(c + 1) * NSS] = og[NPS:]
        occ = np.asarray(r["o_cc"])
        occ = occ.transpose(2, 3, 1, 0).reshape(NSEQ, 2, DFF)
        c_p[0, c * NPS:(c + 1) * NPS] = occ[:NPS]
        c_s[0, c * NSS:(c + 1) * NSS] = occ[NPS:]
    return (y_p, y_s, r_p, g_p, c_p, r_s, g_s, c_s)


def kernel(**inputs):
    in_maps = _prep(**inputs)
    if "nc" not in _CACHE:
        _CACHE["nc"] = build_nc(_CACHE["c"])
    res = run_bass_kernel_spmd(_CACHE["nc"], in_maps, core_ids=list(range(NCORE)))
    return _gather(res.results)
```

```python
import contextlib
import numpy as np
import ml_dtypes
import concourse.bass as bass
import concourse.mybir as mybir
from concourse.bass_utils import run_bass_kernel_spmd

F32 = mybir.dt.float32
BF16 = mybir.dt.bfloat16
AF = mybir.ActivationFunctionType
ALU = mybir.AluOpType

D = 1024
KC = 8
DFF = 2816
FC = 22
DIN = 3600
NCORE = 8
NPS = 2
NSS = 4
NSEQ = NPS + NSS
SEQ = 2048
DSEQ = 64
PAST = 1024
NTP = 512
TOK = NPS * SEQ + NSS * DSEQ
EPS = 1e-6
NSLOT = 4
COMPUTE = ("pe", "act", "dve", "pool")


class Res:
    __slots__ = ("name", "last_w", "readers", "al", "psum")

    def __init__(self, name, psum=False):
        self.name = name
        self.last_w = None
        self.readers = []
        self.al = (self,)
        self.psum = psum


def alias(a_list, b_list):
    for a in [x.r for x in a_list]:
        for b in [x.r for x in b_list]:
            if b not in a.al:
                a.al = a.al + (b,)
            if a not in b.al:
                b.al = b.al + (a,)


class Op:
    __slots__ = ("eng", "fn", "reads", "writes", "deps", "sig", "cnt", "key", "is_dma")

    def __init__(self, eng, fn, reads, writes, key=None):
        self.eng = eng
        self.fn = fn
        self.reads = reads
        self.writes = writes
        self.deps = ()
        self.sig = False
        self.cnt = 0
        self.key = key
        self.is_dma = key is not None


class Prog:
    def __init__(self, nc):
        self.nc = nc
        self.ops = []

    def op(self, eng, fn, reads=(), writes=()):
        o = Op(eng, fn, tuple(reads), tuple(writes))
        self.ops.append(o)
        return o

    def dma(self, queue, out, in_, reads=(), writes=(), key=None):
        o = Op(queue, (out, in_), tuple(reads), tuple(writes), key=key)
        self.ops.append(o)
        return o

    def _analyse(self):
        for o in self.ops:
            raw = set()
            oth = set()
            for r0 in o.reads:
                for r in r0.al:
                    if r.last_w is not None:
                        raw.add(r.last_w)
                    if r.psum:
                        for rd in r.readers:
                            if rd.eng != o.eng:
                                raw.add(rd)
            for w0 in o.writes:
                for w in w0.al:
                    if w.last_w is not None:
                        oth.add(w.last_w)
                    for rd in w.readers:
                        oth.add(rd)
            raw.discard(o)
            oth.discard(o)
            need = []
            for d in raw | oth:
                if (not d.is_dma) and (not o.is_dma) and d.eng == o.eng:
                    if o.eng == "pe":
                        continue
                need.append(d)
            o.deps = need
            for d in need:
                d.sig = True
            for r in o.reads:
                r.readers.append(o)
            for w in o.writes:
                w.last_w = o
                w.readers = []

    def emit(self, final_eng, final_keys):
        nc = self.nc
        self._analyse()
        cnt = {e: 0 for e in COMPUTE}
        kcnt = {}
        for o in self.ops:
            if o.is_dma:
                kcnt[o.key] = kcnt.get(o.key, 0) + 1
                o.cnt = 16 * kcnt[o.key]
            elif o.sig:
                cnt[o.eng] += 1
                o.cnt = cnt[o.eng]
        stack = contextlib.ExitStack()
        sems = {e: stack.enter_context(nc.semaphore("s_" + e)) for e in COMPUTE}
        ksems = {k: stack.enter_context(nc.semaphore("d_" + str(k))) for k in kcnt}
        per_eng = {}
        for o in self.ops:
            per_eng.setdefault(o.eng, []).append(o)

        def run(eng_name, eng):
            waited = {}
            for o in per_eng.get(eng_name, []):
                want = {}
                for d in o.deps:
                    s = ("k", d.key) if d.is_dma else ("e", d.eng)
                    if d.cnt > want.get(s, 0):
                        want[s] = d.cnt
                for s, v in want.items():
                    if waited.get(s, 0) >= v:
                        continue
                    waited[s] = v
                    eng.wait_ge(ksems[s[1]] if s[0] == "k" else sems[s[1]], v)
                if o.is_dma:
                    out, in_ = o.fn
                    eng.dma_start(out=out, in_=in_).then_inc(ksems[o.key], 16)
                else:
                    ins = o.fn(eng)
                    if o.sig:
                        ins.then_inc(sems[o.eng], 1)
            if eng_name == final_eng:
                for k in final_keys:
                    if k in kcnt:
                        eng.wait_ge(ksems[k], 16 * kcnt[k])

        with nc.Block() as block:
            @block.tensor
            def _(e):
                run("pe", e)

            @block.scalar
            def _(e):
                run("act", e)

            @block.vector
            def _(e):
                run("dve", e)

            @block.gpsimd
            def _(e):
                run("pool", e)

            @block.sync
            def _(e):
                run("sp", e)
        stack.close()


def _consts():
    c = {}
    pos = np.concatenate([np.arange(SEQ)] * NPS + [PAST + np.arange(DSEQ)] * NSS).astype(np.float32)
    inv = (10000.0 ** (-np.arange(0, 128, 2, dtype=np.float32) / 128.0)).astype(np.float32)
    ang = pos[None, :] * inv[:, None]
    cos = np.cos(ang).astype(np.float32)
    sin = np.sin(ang).astype(np.float32)
    c["rotC"] = np.ascontiguousarray(np.concatenate([cos, cos], 0))
    c["rotS"] = np.ascontiguousarray(np.concatenate([sin, -sin], 0))
    h = np.arange(4, dtype=np.float64)
    log_g = np.log1p(-np.power(2.0, -5.0 - h))
    i = np.arange(128, dtype=np.float64)
    diff = i[None, :] - i[:, None]
    m = np.where(diff[None] >= 0, np.exp(log_g[:, None, None] * np.maximum(diff[None], 0)), 0.0)
    c["maskR"] = np.ascontiguousarray((128 ** -0.5 * m).transpose(1, 0, 2)).astype(np.float32)
    c["mask01"] = np.ascontiguousarray(np.broadcast_to((diff >= 0)[:, None, :], (128, 4, 128))).astype(np.float32)
    c["triN"] = ((diff >= 0) * (-1.0 / 16.0)).astype(np.float32)
    c["utriN"] = ((diff.T > 0) * (-1.0 / 16.0)).astype(np.float32)
    xi = np.exp(log_g[:, None] * (i[None, :] + 1.0))
    c["xi"] = np.ascontiguousarray(np.broadcast_to(xi[None], (128, 4, 128))).astype(np.float32)
    z128 = 128 ** -0.5 * np.exp(log_g[None, :] * (127.0 - i[:, None]))
    z64 = np.zeros((128, 4))
    z64[:64] = 128 ** -0.5 * np.exp(log_g[None, :] * (63.0 - i[:64, None]))
    c["zeta"] = np.ascontiguousarray(np.stack([z128, z64], 1)).astype(np.float32)
    c["gL"] = {128: [float(np.exp(log_g[k] * 128)) for k in range(4)],
               64: [float(np.exp(log_g[k] * 64)) for k in range(4)]}
    c["ident"] = np.eye(128, dtype=np.float32).astype(ml_dtypes.bfloat16)
    c["ones_b"] = np.ones((128, 128), dtype=np.float32).astype(ml_dtypes.bfloat16)
    c["ones_f"] = np.ones((128, 128), dtype=np.float32)
    return c


class _Stop(Exception):
    pass


def build_nc(consts, stop_at=None):
    nc = bass.Bass("TRN2", target_bir_lowering=False)

    def stage(name):
        if stop_at is not None and name == stop_at:
            raise _Stop()

    P = Prog(nc)
    es = contextlib.ExitStack()

    def dram(name, shape, dt, kind="ExternalInput"):
        return nc.dram_tensor(name, list(shape), dt, kind=kind).ap()

    xT = dram("xT", [KC, 128, TOK], F32)
    cT = dram("cT", [128, KC, NSEQ], F32)
    st_ret_in = dram("st_ret", [NSS, 4, 128, 128], F32)
    st_gla_in = dram("st_gla", [NSS, 4, 64, 128], F32)
    cc_in = dram("cc_in", [128, FC, NSS, 2], F32)
    w_ada = dram("w_ada", [D, 6 * D], F32)
    b_ada = dram("b_ada", [128, 48], F32)
    gvec = dram("gvec", [128, 3, KC], F32)
    ghead = dram("ghead", [128, 2, 4], F32)
    w_in = dram("w_in", [D, DIN], F32)
    w_a2b = dram("w_a2b", [17, 256], F32)
    w_out = dram("w_out", [D, D], F32)
    w_f1 = dram("w_f1", [D, 2 * DFF], F32)
    convp = dram("convp", [128, FC, 4], F32)
    w_f2 = dram("w_f2", [DFF, D], F32)
    rotC_d = dram("rotC", [128, TOK], F32)
    rotS_d = dram("rotS", [128, TOK], F32)
    maskR_d = dram("maskR", [128, 4, 128], F32)
    mask01_d = dram("mask01", [128, 4, 128], F32)
    triN_d = dram("triN", [128, 128], F32)
    utriN_d = dram("utriN", [128, 128], F32)
    xi_d = dram("xi", [128, 4, 128], F32)
    zeta_d = dram("zeta", [128, 2, 4], F32)
    ident_d = dram("ident", [128, 128], BF16)
    onesb_d = dram("ones_b", [128, 128], BF16)
    onesf_d = dram("ones_f", [128, 128], F32)

    yT = dram("yT", [KC, 128, TOK], F32, "ExternalOutput")
    o_ret = dram("o_ret", [NSEQ, 4, 128, 128], F32, "ExternalOutput")
    o_gla = dram("o_gla", [NSEQ, 4, 64, 128], F32, "ExternalOutput")
    o_cc = dram("o_cc", [128, FC, NSEQ, 2], F32, "ExternalOutput")

    NG = 29
    scratch = dram("wscr", [NG, 128, 4096], BF16, "Internal")
    scr_res = [Res("scr%d" % g) for g in range(NG)]

    class T:
        rk = None

        def __init__(self, name, shape, dt, psum=False, res=None):
            if psum:
                self.t = es.enter_context(nc.psum_tensor(name, list(shape), dt))
            else:
                self.t = es.enter_context(nc.sbuf_tensor(name, list(shape), dt))
            self.r = res if res is not None else Res(name, psum)

    def ring(name, n, shape, dt, psum=False):
        return [T("%s%d" % (name, i), shape, dt, psum) for i in range(n)]

    PJ = ring("pj", 4, [128, 512], F32, True)
    SC = T("sc", [128, 512], F32, True)
    OC = T("oc", [128, 512], F32, True)
    MS = T("ms", [128, 512], F32, True)
    TR = T("tr", [128, 512], F32, True)
    STAT = [MS, SC, OC]

    c_maskR = T("c_maskR", [128, 4, 128], F32)
    c_mask01 = T("c_mask01", [128, 4, 128], F32)
    c_tri = T("c_tri", [128, 128], F32)
    c_utri = T("c_utri", [128, 128], F32)
    c_xi = T("c_xi", [128, 4, 128], F32)
    c_zeta = T("c_zeta", [128, 2, 4], F32)
    c_ident = T("c_ident", [128, 128], BF16)
    c_onesb = T("c_onesb", [128, 128], BF16)
    c_onesf = T("c_onesf", [128, 128], F32)
    c_bada = T("c_bada", [128, 48], F32)
    c_gvec = T("c_gvec", [128, 3, KC], F32)
    c_ghead = T("c_ghead", [128, 2, 4], F32)
    c_conv = T("c_conv", [128, FC, 4], F32)
    c_wa2 = T("c_wa2", [17, 256], F32)
    c_cT = T("c_cT", [128, KC, NSEQ], F32)
    c_scT = T("c_scT", [128, KC, NSEQ], BF16)
    modT = T("modT", [128, 48, NSEQ], F32)
    gm1 = T("gm1", [128, KC, NSEQ], F32)
    gm2 = T("gm2", [128, KC, NSEQ], F32)

    cq = 0

    def cload(dst, src):
        nonlocal cq
        P.dma("sp", dst.t[:], src, writes=[dst.r], key="c%d" % cq)
        cq += 1

    for dst, src in ((c_maskR, maskR_d), (c_mask01, mask01_d), (c_tri, triN_d), (c_utri, utriN_d),
                     (c_xi, xi_d), (c_zeta, zeta_d), (c_ident, ident_d), (c_onesb, onesb_d),
                     (c_onesf, onesf_d), (c_bada, b_ada), (c_gvec, gvec), (c_ghead, ghead),
                     (c_conv, convp), (c_wa2, w_a2b), (c_cT, cT)):
        cload(dst, src)

    x_sb = ring("x", 2, [128, KC, NTP], F32)
    for xb in x_sb:
        xb.rk = [Res(xb.r.name + "_k%d" % k) for k in range(KC)]
    rC = ring("rC", 1, [128, NTP], F32)
    rS = ring("rS", 1, [128, NTP], F32)
    hT = T("hT", [128, KC, NTP], BF16)
    hT.rk = [Res("hT_k%d" % k) for k in range(KC)]
    sqr = ring("sq", 4, [128, NTP], BF16)
    c_eps = T("c_eps", [128, 1], F32)
    tmpr = ring("tmp", 2, [128, NTP], F32)
    rsdr = ring("rsd", 2, [128, NTP], F32)
    rsbr = ring("rsb", 2, [128, NTP], F32)
    rsd, rsb = rsdr[0], rsbr[0]
    wsl = ring("wsl", NSLOT, [128, 4096], BF16)
    ga1 = T("ga1", [17, NTP], F32)
    lsp = T("lsp", [128, 4, 256], F32)
    Aex = T("Aex", [128, 2, NTP], F32)
    Ain = T("Ain", [128, 2, NTP], F32)
    E2 = T("E2", [128, 4, 256], F32)
    qg = T("qg", [64, 4, NTP], BF16)
    kg = T("kg", [64, 4, NTP], BF16)
    eLt = T("eLt", [64, 4, 4], F32)
    kraw = T("kraw", [128, 2, NTP], BF16)
    class V:
        def __init__(self, name, ap):
            self.t = ap
            self.r = Res(name)

    U1 = es.enter_context(nc.sbuf_tensor("U1", [128, FC * NTP], BF16))
    U2 = es.enter_context(nc.sbuf_tensor("U2", [128, 2080], F32))

    def u1v(name, lo, n, k):
        return V(name, U1[:, lo:lo + n].rearrange("p (k c) -> p k c", k=k))

    qrot = u1v("qrot", 0, 2048, 4)
    qxi = u1v("qxi", 2048, 2048, 4)
    krot = u1v("krot", 4096, 2048, 4)
    kztm = u1v("kztm", 6144, 2048, 4)
    k2tm = u1v("k2tm", 8192, 1024, 4)
    vtm = u1v("vtm", 9216, 2048, 4)
    yTt = [V("y%d" % i, U1[:, i * NTP:(i + 1) * NTP]) for i in range(FC)]
    alias([qrot], yTt[0:4])
    alias([qxi], yTt[4:8])
    alias([krot], yTt[8:12])
    alias([kztm], yTt[12:16])
    alias([k2tm], yTt[16:18])
    alias([vtm], yTt[18:22])
    t1r = tmpr
    t2r = ring("t2", 2, [128, NTP], F32)
    sg = V("sg", U2[:, 0:2048].rearrange("p (k c) -> p k c", k=4))
    bufr = [V("buf%d" % i, U2[:, i * 520:(i + 1) * 520]) for i in range(2)]
    c0r = [V("c0%d" % i, U2[:, 1040 + i * 512:1040 + (i + 1) * 512]) for i in range(2)]
    alias([sg], bufr + c0r)
    osb = T("osb", [128, 4, NTP], F32)
    stmr = ring("stmr", 4, [128, 4, 128], BF16)
    stmg = ring("stmg", 4, [128, 4, 128], BF16)
    vtm2 = T("vtm2", [128, 4, 512], BF16)
    osg = [V("osg%d" % i, x_sb[1 - i].t[:, 0:4, :]) for i in range(2)]
    sgg = [V("sgg%d" % i, x_sb[1 - i].t[:, 4:8, :]) for i in range(2)]
    class _R:
        def __init__(self, r):
            self.r = r

    for i in range(2):
        alias([osg[i]], [_R(r) for r in x_sb[1 - i].rk[0:4]])
        alias([sgg[i]], [_R(r) for r in x_sb[1 - i].rk[4:8]])
    mixT = [T("mix%d" % i, [128, NTP], BF16) for i in range(8)]
    Rst = ring("Rst", 2, [128, 4, 128], F32)
    Rbf = T("Rbf", [128, 4, 128], BF16)
    Gst = ring("Gst", 2, [64, 4, 128], F32)
    Gbf = T("Gbf", [64, 4, 128], BF16)
    hist = T("hist", [128, FC, NSS, 2], F32)

    def mm(out, lhsT, rhs, start, stop, rd, wr):
        P.op("pe", lambda e: e.matmul(out, lhsT, rhs, start=start, stop=stop), rd, wr)

    def trp(out, in_, rd, wr):
        P.op("pe", lambda e: e.transpose(out, in_, c_ident.t[:]), list(rd) + [c_ident.r], wr)

    def act(out, in_, func, rd, wr, bias=None, scale=None):
        kw = {}
        if bias is not None:
            kw["bias"] = bias
        if scale is not None:
            kw["scale"] = scale
        P.op("act", lambda e: e.activation(out=out, in_=in_, func=func, **kw), rd, wr)

    def tt(eng, out, in0, in1, op, rd, wr):
        P.op(eng, lambda e: e.tensor_tensor(out=out, in0=in0, in1=in1, op=op), rd, wr)

    def stt(eng, out, in0, scalar, in1, op0, op1, rd, wr):
        P.op(eng, lambda e: e.scalar_tensor_tensor(out=out, in0=in0, scalar=scalar, in1=in1,
                                                   op0=op0, op1=op1), rd, wr)

    def ts(eng, out, in0, s1, s2, op0, op1, rd, wr):
        P.op(eng, lambda e: e.tensor_scalar(out=out, in0=in0, scalar1=s1, scalar2=s2, op0=op0, op1=op1), rd, wr)

    def cp(eng, out, in_, rd, wr):
        P.op(eng, lambda e: e.tensor_copy(out=out, in_=in_), rd, wr)

    def mset(eng, ap, val, wr):
        P.op(eng, lambda e: e.memset(ap, val), (), wr)

    def recip(out, in_, rd, wr):
        P.op("dve", lambda e: e.reciprocal(out=out, in_=in_), rd, wr)

    tiles = []
    for s in range(NPS):
        for t in range(SEQ // NTP):
            tiles.append(dict(kind="p", col0=s * SEQ + t * NTP, NT=NTP, L=128, first=(t == 0),
                              last=(t == SEQ // NTP - 1), segs=[(s, 0, NTP)], slot=s % 2))
    tiles.append(dict(kind="s", col0=NPS * SEQ, NT=NSS * DSEQ, L=64, first=True, last=True,
                      segs=[(NPS + i, i * DSEQ, DSEQ) for i in range(NSS)], slot=None))

    def group_srcs(g):
        if g == 0:
            return [(0, 8, 16, w_in[:, 3584:3600].rearrange("(k p) c -> p k c", p=128))]
        if 1 <= g <= 7:
            c0 = (g - 1) * 512
            return [(0, 8, 512, w_in[:, c0:c0 + 512].rearrange("(k p) c -> p k c", p=128))]
        if g in (8, 9):
            c0 = (g - 8) * 512
            return [(0, 8, 512, w_out[:, c0:c0 + 512].rearrange("(k p) c -> p k c", p=128))]
        if 10 <= g <= 20:
            j = g - 10
            return [(0, 8, 256, w_f1[:, 256 * j:256 * j + 256].rearrange("(k p) c -> p k c", p=128)),
                    (8 * 256, 8, 256, w_f1[:, DFF + 256 * j:DFF + 256 * j + 256].rearrange("(k p) c -> p k c", p=128))]
        j = g - 21
        return [(0, FC, 128, w_f2[:, 128 * j:128 * j + 128].rearrange("(k p) c -> p k c", p=128))]

    sched = [("ada", i) for i in range(6)]
    GORDER = [0, 1, 2, 5, 3, 6, 4, 7, 8, 9] + list(range(10, 21)) + list(range(21, 29))
    sched.append(("w", 0, 0))
    sched += [("ada", i) for i in range(6, 12)]
    for ti in range(len(tiles)):
        for g in [1, 3, 2, 6, 5, 4, 7, 8, 9] + list(range(10, 21)):
            sched.append(("w", ti, g))
        for g in range(21, 24):
            sched.append(("w", ti, g))
        if ti + 1 < len(tiles):
            sched.append(("w", ti + 1, 0))
        for g in range(24, 29):
            sched.append(("w", ti, g))
    loaded = [0]

    def slot_view(si, nk, ncols, off=0):
        return wsl[si].t[:, off:off + nk * ncols].rearrange("p (k c) -> p k c", k=nk)

    def ensure(idx, la=NSLOT - 1):
        while loaded[0] <= min(idx + la, len(sched) - 1):
            i = loaded[0]
            si = i % NSLOT
            ent = sched[i]
            key = "w%d" % si
            if ent[0] == "ada":
                pc = ent[1]
                src = w_ada[:, pc * 512:(pc + 1) * 512].rearrange("(k p) c -> p k c", p=128)
                P.dma("pool", slot_view(si, 8, 512), src, writes=[wsl[si].r], key="wc%d" % si)
            else:
                _, ti, g = ent
                if ti == 0:
                    for (off, nk, ncols, src) in group_srcs(g):
                        P.dma("pool", slot_view(si, nk, ncols, off), src, writes=[wsl[si].r], key="wc%d" % si)
                    gsz = 128 if g == 0 else (FC * 128 if g >= 21 else 4096)
                    P.dma("sp", scratch[g][:, 0:gsz], wsl[si].t[:, 0:gsz], reads=[wsl[si].r], writes=[scr_res[g]],
                          key="ws%d" % si)
                else:
                    gsz = 128 if g == 0 else (FC * 128 if g >= 21 else 4096)
                    P.dma("sp", wsl[si].t[:, 0:gsz], scratch[g][:, 0:gsz], reads=[scr_res[g]], writes=[wsl[si].r], key=key)
            loaded[0] += 1

    spos = [0]

    def next_w(expect=None, la=NSLOT - 1):
        i = spos[0]
        spos[0] += 1
        assert expect is None or sched[i][-1] == expect or sched[i] == expect, (i, sched[i], expect)
        ensure(i, la)
        return i % NSLOT

    out_keys = set()

    def body():
        act(c_scT.t[:], c_cT.t[:], AF.Silu, [c_cT.r], [c_scT.r])

        def ada_piece(pc):
            si = next_w(("ada", pc))
            wv = slot_view(si, 8, 512)
            for jj in range(4):
                j = pc * 4 + jj
                for kc in range(KC):
                    mm(TR.t[:, j * NSEQ:(j + 1) * NSEQ], wv[:, kc, jj * 128:(jj + 1) * 128], c_scT.t[:, kc, :],
                       kc == 0, kc == KC - 1, [wsl[si].r, c_scT.r], [TR.r])

        def ada_evac(j0, j1):
            tt("dve", modT.t[:, j0:j1, :], TR.t[:, j0 * NSEQ:j1 * NSEQ].rearrange("p (j s) -> p j s", s=NSEQ),
               c_bada.t[:, j0:j1].unsqueeze(2).to_broadcast([128, j1 - j0, NSEQ]),
               ALU.add, [TR.r, c_bada.r], [modT.r])

        def ada_gm(gmx, gi, sc0):
            stt("dve", gmx.t[:], modT.t[:, sc0:sc0 + 8, :], 1.0,
                c_gvec.t[:, gi, :].unsqueeze(2).to_broadcast([128, KC, NSEQ]), ALU.add, ALU.mult,
                [modT.r, c_gvec.r], [gmx.r])

        for pc in range(4):
            ada_piece(pc)
        ada_evac(0, 16)
        ada_gm(gm1, 0, 8)
        stage("prologue")
        mset("pool", ga1.t[:], 1.0, [ga1.r])
        mset("pool", c_eps.t[:], EPS, [c_eps.r])

        SH1, G1, SH2, G2 = 0, 16, 24, 40

        def norm(*a, **k):
            for _ in norm_gen(*a, **k):
                pass

        def norm_gen(ti, xs, NT, segs, gm, sh0, out_fn, final=False, sq_pool=False):
            st = TR if sq_pool else next_bank()

            def sqmm(kc, do_sq, do_mm):
                sqk = sqr[kc % 4]
                if do_sq:
                    if sq_pool:
                        tt("pool", sqk.t[:, 0:NT], xs.t[:, kc, 0:NT], xs.t[:, kc, 0:NT], ALU.mult, [xs.rk[kc]], [sqk.r])
                    else:
                        act(sqk.t[:, 0:NT], xs.t[:, kc, 0:NT], AF.Square, [xs.rk[kc]], [sqk.r])
                if do_mm:
                    mm(f32v(st)[:, 0:NT], c_onesb.t[:], sqk.t[:, 0:NT], kc == 0, kc == KC - 1, [c_onesb.r, sqk.r], [st.r])

            if sq_pool:
                for kc in range(4):
                    sqmm(kc, True, False)
                yield
                for kc in range(4):
                    sqmm(kc, False, True)
                for kc in range(4, 8):
                    sqmm(kc, True, False)
                yield
                for kc in range(4, 8):
                    sqmm(kc, False, True)
                yield
            else:
                for kc in range(KC):
                    sqmm(kc, True, True)
                yield
            rb = rstd_from(st, NT, 1.0 / D)
            for kc in range(KC):
                if final:
                    stt("dve", out_fn(kc)[0], xs.t[:, kc, 0:NT], c_gvec.t[:, 2, kc:kc + 1], rb.t[:, 0:NT],
                        ALU.mult, ALU.mult, [xs.rk[kc], c_gvec.r, rb.r], [out_fn(kc)[1]])
                    continue
                tm = tmpr[kc % 2]
                if len(segs) > 1:
                    ns_, s0_ = len(segs), segs[0][0]
                    ls_ = NT // ns_
                    v3 = lambda ap: ap.rearrange("p (s l) -> p s l", s=ns_)
                    o_ap, o_r = out_fn(kc)
                    tt("dve", v3(tm.t[:, 0:NT]), v3(xs.t[:, kc, 0:NT]), v3(rb.t[:, 0:NT]), ALU.mult,
                       [xs.rk[kc], rb.r], [tm.r])
                    tt("dve", v3(tm.t[:, 0:NT]), v3(tm.t[:, 0:NT]),
                       gm.t[:, kc, s0_:s0_ + ns_].unsqueeze(2).to_broadcast([128, ns_, ls_]), ALU.mult, [tm.r, gm.r], [tm.r])
                    tt("pool", v3(o_ap[:, 0:NT]), v3(tm.t[:, 0:NT]),
                       modT.t[:, sh0 + kc, s0_:s0_ + ns_].unsqueeze(2).to_broadcast([128, ns_, ls_]), ALU.add,
                       [tm.r, modT.r], [o_r])
                    continue
                for (sid, c0, n) in segs:
                    stt("dve", tm.t[:, c0:c0 + n], xs.t[:, kc, c0:c0 + n], gm.t[:, kc, sid:sid + 1],
                        rb.t[:, c0:c0 + n], ALU.mult, ALU.mult, [xs.rk[kc], gm.r, rb.r], [tm.r])
                    o_ap, o_r = out_fn(kc)
                    act(o_ap[:, c0:c0 + n], tm.t[:, c0:c0 + n], AF.Identity, [tm.r, modT.r], [o_r],
                        bias=modT.t[:, sh0 + kc, sid:sid + 1])

        rsc = [0]

        def rstd_from(st, NT, scale):
            i = rsc[0] % 4
            rsc[0] += 1
            rd = (rsdr + rsbr)[i]
            act(rd.t[:, 0:NT], f32v(st)[:, 0:NT], AF.Ln, [st.r, c_eps.r], [rd.r], bias=c_eps.t[:, 0:1], scale=scale)
            act(rd.t[:, 0:NT], rd.t[:, 0:NT], AF.Exp, [rd.r], [rd.r], scale=-0.5)
            return rd

        bkc = [0]
        BANKS = [PJ[0], SC, PJ[1], OC, PJ[2], MS, PJ[3]]

        def next_bank():
            b = BANKS[bkc[0] % 7]
            bkc[0] += 1
            return b

        def f32v(b):
            return b.t[:, :]

        def bf16v(b):
            return b.t[:, :].bitcast(BF16)

        next_pj = next_bank

        def proj_fm(si, col, NT, act_t, nk=KC, rhs_fn=None):
            b = next_pj()
            wv = wsl[si]
            for kc in range(nk):
                lhsT = wv.t[:, kc * (col[1]) + col[0]: kc * (col[1]) + col[0] + col[2]]
                rhs, rr = rhs_fn(kc)
                mm(b.t[0:col[2], 0:NT], lhsT, rhs, kc == 0, kc == nk - 1, [wv.r, rr], [b.r])
            return b

        n_tiles = len(tiles)

        def load_x(ti):
            tl = tiles[ti]
            NT, c0 = tl["NT"], tl["col0"]
            xs = x_sb[ti % 2]
            P.dma("pool", xs.t[:, :, 0:NT], xT[:, :, c0:c0 + NT].rearrange("k p t -> p k t"), writes=xs.rk,
                  key="x%d" % (ti % 2))

        def load_rot(ti):
            tl = tiles[ti]
            NT, c0 = tl["NT"], tl["col0"]
            P.dma("pool", rC[0].t[:, 0:NT], rotC_d[:, c0:c0 + NT], writes=[rC[0].r], key="rc")
            P.dma("pool", rS[0].t[:, 0:NT], rotS_d[:, c0:c0 + NT], writes=[rS[0].r], key="rs")

        load_x(0)
        load_rot(0)

        def t_front(ti):
            tl = tiles[ti]
            NT, L, segs = tl["NT"], tl["L"], tl["segs"]
            xs = x_sb[ti % 2]
            nch = NT // L
            zsel = 0 if L == 128 else 1
            if tl["kind"] == "p":
                ch_slot = [tl["slot"]] * nch
                ch_seq = [segs[0][0]] * nch
            else:
                ch_slot = [i % 2 for i in range(nch)]
                ch_seq = [segs[i][0] for i in range(nch)]
            par = ti % 2
            OSG, SGG = osg[par], sgg[par]

            def h_rhs(kc):
                return hT.t[:, kc, 0:NT], hT.rk[kc]


            yield from norm_gen(ti, xs, NT, segs, gm1, SH1, lambda kc: (hT.t[:, kc, :], hT.rk[kc]), sq_pool=(ti > 0))
            if tl["kind"] == "p" and tl["first"]:
                sl = tl["slot"]
                mset("pool", Rst[sl].t[:], 0.0, [Rst[sl].r])
                mset("pool", Gst[sl].t[:], 0.0, [Gst[sl].r])
                mset("pool", hist.t[:], 0.0, [hist.r])
            if tl["kind"] == "s":
                P.dma("pool", hist.t[:], cc_in, writes=[hist.r], key="hist")

            stage("t%d_init" % ti)
            stage("t%d_norm1" % ti)
            si = next_w(0)
            b = next_pj()
            for kc in range(KC):
                mm(b.t[0:16, 0:NT], wsl[si].t[:, kc * 16:(kc + 1) * 16], hT.t[:, kc, 0:NT], kc == 0, kc == KC - 1,
                   [wsl[si].r, hT.rk[kc]], [b.r])
            act(ga1.t[0:16, 0:NT], b.t[0:16, 0:NT], AF.Copy, [b.r], [ga1.r])
            yield
            for c in range(nch):
                cs = slice(c * L, (c + 1) * L)
                bu_ = next_bank()
                mm(f32v(bu_)[0:L, 0:256], ga1.t[:, cs], c_wa2.t[:], True, True, [ga1.r, c_wa2.r], [bu_.r])
                act(E2.t[0:L, c, :], f32v(bu_)[0:L, 0:256], AF.Exp, [bu_.r], [E2.r], scale=-1.0)
                act(lsp.t[0:L, c, :], E2.t[0:L, c, :], AF.Ln, [E2.r], [lsp.r], bias=1.0)
            yield
            for c in range(nch):
                cs = slice(c * L, (c + 1) * L)
                bb_ = next_bank()
                for p in range(2):
                    mm(f32v(bb_)[:, p * 128:p * 128 + L], lsp.t[0:L, c, p * 128:(p + 1) * 128], c_tri.t[0:L, 0:L],
                       True, True, [lsp.r, c_tri.r], [bb_.r])
                bview = f32v(bb_)[:, 0:256].rearrange("p (a t) -> p a t", a=2)[:, :, 0:L]
                act(Aex.t[:, :, cs], bview, AF.Exp, [bb_.r], [Aex.r])
                act(Ain.t[:, :, cs], bview, AF.Exp, [bb_.r], [Ain.r], scale=-1.0)
                bl_ = next_bank()
                mm(f32v(bl_)[0:L, 0:256], c_utri.t[0:L, 0:L], lsp.t[0:L, c, :], True, True, [c_utri.r, lsp.r], [bl_.r])
                act(E2.t[0:L, c, :], f32v(bl_)[0:L, 0:256], AF.Exp, [bl_.r], [E2.r])
            stage("t%d_glaprep" % ti)

        def t_mid(ti, pre_last=None):
            tl = tiles[ti]
            NT, L, segs = tl["NT"], tl["L"], tl["segs"]
            xs = x_sb[ti % 2]
            nch = NT // L
            zsel = 0 if L == 128 else 1
            if tl["kind"] == "p":
                ch_slot = [tl["slot"]] * nch
                ch_seq = [segs[0][0]] * nch
            else:
                ch_slot = [i % 2 for i in range(nch)]
                ch_seq = [segs[i][0] for i in range(nch)]
            par = ti % 2
            OSG, SGG = osg[par], sgg[par]

            def h_rhs(kc):
                return hT.t[:, kc, 0:NT], hT.rk[kc]


            PEX = "dve" if ti == 0 else "pool"

            def proj_v(vdst, g):
                si = next_w(g)
                for c in range(nch):
                    b = next_pj()
                    for kc in range(KC):
                        mm(b.t[0:L, 0:512], hT.t[:, kc, c * L:(c + 1) * L], wsl[si].t[:, kc * 512:(kc + 1) * 512],
                           kc == 0, kc == KC - 1, [hT.rk[kc], wsl[si].r], [b.r])
                    act(vdst.t[0:L, c, :], b.t[0:L, 0:512], AF.Copy, [b.r], [vdst.r])

            def gen_gates(gdst, g):
                si = next_w(g)
                for h in range(4):
                    b = proj_fm(si, (h * 128, 512, 128), NT, None, rhs_fn=h_rhs)
                    act(gdst.t[:, h, 0:NT], b.t[:, 0:NT], AF.Silu, [b.r], [gdst.r])
                    yield

            for gi, dst, vdst, gv in ((1, qrot, vtm, 3), (2, krot, vtm2, 6)):
                si = next_w(gi)
                for h in range(4):
                    b = proj_fm(si, (h * 128, 512, 128), NT, None, rhs_fn=h_rhs)
                    t1, t2 = t1r[h % 2], t2r[h % 2]
                    tt("dve", t1.t[:, 0:NT], b.t[:, 0:NT], rC[0].t[:, 0:NT], ALU.mult, [b.r, rC[0].r], [t1.r])
                    tt("dve", t2.t[0:64, 0:NT], b.t[64:128, 0:NT], rS[0].t[64:128, 0:NT], ALU.mult,
                       [b.r, rS[0].r], [t2.r])
                    tt("dve", t2.t[64:128, 0:NT], b.t[0:64, 0:NT], rS[0].t[0:64, 0:NT], ALU.mult,
                       [b.r, rS[0].r], [t2.r])
                    tt(PEX, dst.t[:, h, 0:NT], t1.t[:, 0:NT], t2.t[:, 0:NT], ALU.add, [t1.r, t2.r], [dst.r])
                proj_v(vdst, gv)
            if ti + 1 < n_tiles:
                load_rot(ti + 1)
            tt(PEX, qxi.t[:, :, 0:NT].rearrange("p h (c l) -> p h c l", l=L),
               qrot.t[:, :, 0:NT].rearrange("p h (c l) -> p h c l", l=L),
               c_xi.t[:, :, 0:L].unsqueeze(2).to_broadcast([128, 4, nch, L]), ALU.mult, [qrot.r, c_xi.r], [qxi.r])
            stage("t%d_retproj" % ti)

            si = next_w(5)
            for p in range(2):
                b = proj_fm(si, (p * 128, 512, 128), NT, None, rhs_fn=h_rhs)
                for two in range(2):
                    rows = slice(two * 64, two * 64 + 64)
                    stt("dve", qg.t[0:64, 2 * p + two, 0:NT], b.t[rows, 0:NT], 0.125, Aex.t[rows, p, 0:NT],
                        ALU.mult, ALU.mult, [b.r, Aex.r], [qg.r])
            for p in range(2):
                b = proj_fm(si, (256 + p * 128, 512, 128), NT, None, rhs_fn=h_rhs)
                for two in range(2):
                    rows = slice(two * 64, two * 64 + 64)
                    tt("dve", kg.t[0:64, 2 * p + two, 0:NT], b.t[rows, 0:NT], Ain.t[rows, p, 0:NT], ALU.mult,
                       [b.r, Ain.r], [kg.r])
                act(kraw.t[:, p, 0:NT], b.t[:, 0:NT], AF.Copy, [b.r], [kraw.r])
            stage("t%d_g5" % ti)

            def gen_passA():
                for c in range(nch):
                    cs = slice(c * L, (c + 1) * L)
                    bt = next_bank()
                    for h in range(4):
                        trp(bf16v(bt)[0:L, h * 128:(h + 1) * 128], krot.t[:, h, cs], [krot.r], [bt.r])
                    tt("dve", kztm.t[0:L, c, :].rearrange("p (h d) -> p h d", h=4),
                       bf16v(bt)[0:L, 0:512].rearrange("p (h d) -> p h d", h=4),
                       c_zeta.t[0:L, zsel, :].unsqueeze(2).to_broadcast([L, 4, 128]), ALU.mult, [bt.r, c_zeta.r], [kztm.r])
                    bs = next_bank()
                    for h in range(4):
                        mm(f32v(bs)[0:L, h * 128:h * 128 + L], krot.t[:, h, cs], qrot.t[:, h, cs], True, True,
                           [krot.r, qrot.r], [bs.r])
                    tt("dve", stmr[c].t[0:L, :, 0:L], f32v(bs)[0:L, :].rearrange("p (h t) -> p h t", h=4)[:, :, 0:L],
                       c_maskR.t[0:L, :, 0:L], ALU.mult, [bs.r, c_maskR.r], [stmr[c].r])
                    bt = next_bank()
                    for p in range(2):
                        trp(bf16v(bt)[0:L, p * 128:(p + 1) * 128], kraw.t[:, p, cs], [kraw.r], [bt.r])
                    tt("dve", k2tm.t[0:L, c, :], bf16v(bt)[0:L, 0:256], E2.t[0:L, c, :], ALU.mult, [bt.r, E2.r], [k2tm.r])
                    bs = next_bank()
                    for h in range(4):
                        mm(f32v(bs)[0:L, h * 128:h * 128 + L], kg.t[0:64, h, cs], qg.t[0:64, h, cs], True, True,
                           [kg.r, qg.r], [bs.r])
                    tt("dve", stmg[c].t[0:L, :, 0:L], f32v(bs)[0:L, :].rearrange("p (h t) -> p h t", h=4)[:, :, 0:L],
                       c_mask01.t[0:L, :, 0:L], ALU.mult, [bs.r, c_mask01.r], [stmg[c].r])
                    yield

            def interleave(a, b, ratio):
                for _ in a:
                    for _ in range(ratio):
                        next(b, None)
                for _ in b:
                    pass

            interleave(gen_passA(), gen_gates(sg, 4), 1)
            stage("t%d_passA" % ti)

            def gen_passB():
                for c in range(nch):
                    cs = slice(c * L, (c + 1) * L)
                    sl = ch_slot[c]
                    R, G = Rst[sl], Gst[sl]
                    if tl["kind"] == "s":
                        sid = ch_seq[c] - NPS
                        P.dma("pool", R.t[:], st_ret_in[sid].rearrange("h d e -> d h e"), writes=[R.r], key="rin%d" % sl)
                        P.dma("pool", G.t[:], st_gla_in[sid].rearrange("h d e -> d h e"), writes=[G.r], key="gin%d" % sl)
                    cp("dve", Rbf.t[:], R.t[:], [R.r], [Rbf.r])
                    cp("dve", Gbf.t[:], G.t[:], [G.r], [Gbf.r])
                    bu_ = next_bank()
                    for h in range(4):
                        mm(f32v(bu_)[:, h * 128:(h + 1) * 128], kztm.t[0:L, c, h * 128:(h + 1) * 128],
                           vtm.t[0:L, c, h * 128:(h + 1) * 128], True, True, [kztm.r, vtm.r], [bu_.r])
                    for h in range(4):
                        stt("dve", R.t[:, h, :], R.t[:, h, :], consts["gL"][L][h], f32v(bu_)[:, h * 128:(h + 1) * 128],
                            ALU.mult, ALU.add, [R.r, bu_.r], [R.r])
                    bu_ = next_bank()
                    for h in range(4):
                        mm(f32v(bu_)[0:64, h * 128:(h + 1) * 128], k2tm.t[0:L, c, h * 64:(h + 1) * 64],
                           vtm2.t[0:L, c, h * 128:(h + 1) * 128], True, True, [k2tm.r, vtm2.r], [bu_.r])
                    lastc = c * L + L - 1
                    for h in range(4):
                        p, two = h // 2, h % 2
                        rows = slice(two * 64, two * 64 + 64)
                        cp("dve", eLt.t[0:64, h, c:c + 1], Aex.t[rows, p, lastc:lastc + 1], [Aex.r], [eLt.r])
                    for h in range(4):
                        stt("dve", G.t[0:64, h, :], G.t[0:64, h, :], eLt.t[0:64, h, c:c + 1],
                            f32v(bu_)[0:64, h * 128:(h + 1) * 128], ALU.mult, ALU.add, [G.r, eLt.r, bu_.r], [G.r])
                    bo = next_bank()
                    for h in range(4):
                        mm(f32v(bo)[:, h * 128:h * 128 + L], vtm.t[0:L, c, h * 128:(h + 1) * 128], stmr[c].t[0:L, h, 0:L],
                           True, False, [vtm.r, stmr[c].r], [bo.r])
                        mm(f32v(bo)[:, h * 128:h * 128 + L], Rbf.t[:, h, :], qxi.t[:, h, cs], False, True,
                           [Rbf.r, qxi.r], [bo.r])
                    act(osb.t[:, :, cs], f32v(bo).rearrange("p (h t) -> p h t", h=4)[:, :, 0:L], AF.Copy, [bo.r], [osb.r])
                    bo = next_bank()
                    for h in range(4):
                        mm(f32v(bo)[:, h * 128:h * 128 + L], vtm2.t[0:L, c, h * 128:(h + 1) * 128], stmg[c].t[0:L, h, 0:L],
                           True, False, [vtm2.r, stmg[c].r], [bo.r])
                        mm(f32v(bo)[:, h * 128:h * 128 + L], Gbf.t[0:64, h, :], qg.t[0:64, h, cs], False, True,
                           [Gbf.r, qg.r], [bo.r])
                    act(OSG.t[:, :, cs], f32v(bo).rearrange("p (h t) -> p h t", h=4)[:, :, 0:L], AF.Copy, [bo.r], [OSG.r])
                    if tl["kind"] == "s" or (tl["last"] and c == nch - 1):
                        seq = ch_seq[c]
                        out_keys.add("oret%d" % sl)
                        out_keys.add("ogla%d" % sl)
                        P.dma("pool", o_ret[seq].rearrange("h d e -> d h e"), R.t[:], reads=[R.r], key="oret%d" % sl)
                        P.dma("pool", o_gla[seq].rearrange("h d e -> d h e"), G.t[:], reads=[G.r], key="ogla%d" % sl)
                    yield

            interleave(gen_passB(), gen_gates(SGG, 7), 1)
            stage("t%d_passB" % ti)


            for h in range(4):
                st = next_bank()
                mm(f32v(st)[:, 0:NT], c_onesf.t[:], osb.t[:, h, 0:NT], True, True, [c_onesf.r, osb.r], [st.r])
                stt("dve", osb.t[:, h, 0:NT], f32v(st)[:, 0:NT], -1.0 / 128, osb.t[:, h, 0:NT], ALU.mult, ALU.add,
                    [st.r, osb.r], [osb.r])
                sqa = sqr[h % 2]
                tt("dve", sqa.t[:, 0:NT], osb.t[:, h, 0:NT], osb.t[:, h, 0:NT], ALU.mult, [osb.r], [sqa.r])
                st2 = next_bank()
                mm(f32v(st2)[:, 0:NT], c_onesb.t[:], sqa.t[:, 0:NT], True, True, [c_onesb.r, sqa.r], [st2.r])
                rb = rstd_from(st2, NT, 1.0 / 128)
                tm = tmpr[h % 2]
                stt("dve", tm.t[:, 0:NT], osb.t[:, h, 0:NT], c_ghead.t[:, 0, h:h + 1], rb.t[:, 0:NT], ALU.mult, ALU.mult,
                    [osb.r, c_ghead.r, rb.r], [tm.r])
                tt(PEX, mixT[h].t[:, 0:NT], tm.t[:, 0:NT], sg.t[:, h, 0:NT], ALU.mult, [tm.r, sg.r], [mixT[h].r])
                sqa = sqr[2 + h % 2]
                act(sqa.t[:, 0:NT], OSG.t[:, h, 0:NT], AF.Square, [OSG.r], [sqa.r])
                st = next_bank()
                mm(f32v(st)[:, 0:NT], c_onesb.t[:], sqa.t[:, 0:NT], True, True, [c_onesb.r, sqa.r], [st.r])
                rb = rstd_from(st, NT, 1.0 / 128)
                tm = t2r[h % 2]
                stt("dve", tm.t[:, 0:NT], OSG.t[:, h, 0:NT], c_ghead.t[:, 1, h:h + 1], rb.t[:, 0:NT], ALU.mult, ALU.mult,
                    [OSG.r, c_ghead.r, rb.r], [tm.r])
                tt(PEX, mixT[4 + h].t[:, 0:NT], tm.t[:, 0:NT], SGG.t[:, h, 0:NT], ALU.mult, [tm.r, SGG.r],
                   [mixT[4 + h].r])
            stage("t%d_gla" % ti)
            if ti + 1 < n_tiles:
                load_x(ti + 1)

            def mix_rhs(kc):
                return mixT[kc].t[:, 0:NT], mixT[kc].r

            for g2 in range(2):
                si = next_w(8 + g2)
                for m in range(4):
                    j = g2 * 4 + m
                    b = proj_fm(si, (m * 128, 512, 128), NT, None, rhs_fn=mix_rhs)
                    for (sid, c0, n) in segs:
                        stt("dve", xs.t[:, j, c0:c0 + n], b.t[:, c0:c0 + n], modT.t[:, G1 + j, sid:sid + 1],
                            xs.t[:, j, c0:c0 + n], ALU.mult, ALU.add, [b.r, modT.r, xs.rk[j]], [xs.rk[j]])

            stage("t%d_wout" % ti)
            norm(ti, xs, NT, segs, gm2, SH2, lambda kc: (hT.t[:, kc, :], hT.rk[kc]))
            nseg = len(segs)
            Ls = NT // nseg
            for g in range(11):
                if g in (4, 7, 10) and pre_last is not None:
                    pre_last()
                si = next_w(10 + g)
                banks = []
                for m in range(4):
                    banks.append(proj_fm(si, ((m // 2) * 2048 + (m % 2) * 128, 256, 128), NT, None, rhs_fn=h_rhs))
                for q in range(2):
                    j = 2 * g + q
                    ba, bu = banks[q], banks[2 + q]
                    B = bufr[j % 2]
                    Bv = B.t[:, 0:nseg * (Ls + 2)].rearrange("p (s l) -> p s l", s=nseg)
                    C0 = c0r[j % 2]
                    C0v = C0.t[:, 0:NT].rearrange("p (s l) -> p s l", s=nseg)
                    cp(PEX, Bv[:, :, 0:2], hist.t[:, j, 0:nseg, :], [hist.r], [B.r])
                    act(Bv[:, :, 2:Ls + 2], ba.t[:, 0:NT].rearrange("p (s l) -> p s l", s=nseg), AF.Copy, [ba.r], [B.r])
                    act(C0v, Bv[:, :, 0:Ls], AF.Identity, [B.r, c_conv.r], [C0.r],
                        bias=c_conv.t[:, j, 3:4], scale=c_conv.t[:, j, 0:1])
                    stt("dve", C0v, Bv[:, :, 1:Ls + 1], c_conv.t[:, j, 1:2], C0v, ALU.mult, ALU.add,
                        [B.r, c_conv.r, C0.r], [C0.r])
                    stt("dve", C0v, Bv[:, :, 2:Ls + 2], c_conv.t[:, j, 2:3], C0v, ALU.mult, ALU.add,
                        [B.r, c_conv.r, C0.r], [C0.r])
                    cp(PEX, hist.t[:, j, 0:nseg, :], Bv[:, :, Ls:Ls + 2], [B.r], [hist.r])
                    act(C0.t[:, 0:NT], C0.t[:, 0:NT], AF.Silu, [C0.r], [C0.r])
                    tt("dve", yTt[j].t[:, 0:NT], C0.t[:, 0:NT], bu.t[:, 0:NT], ALU.mult, [C0.r, bu.r], [yTt[j].r])
            if tl["last"]:
                if tl["kind"] == "p":
                    s0 = segs[0][0]
                    out_keys.add("occ")
                    P.dma("pool", o_cc[:, :, s0:s0 + 1, :], hist.t[:, :, 0:1, :], reads=[hist.r], key="occ")
                else:
                    out_keys.add("occ")
                    P.dma("pool", o_cc[:, :, NPS:NPS + NSS, :], hist.t[:], reads=[hist.r], key="occ")

            stage("t%d_ffn1" % ti)

        def t_back(ti):
            tl = tiles[ti]
            NT, L, segs = tl["NT"], tl["L"], tl["segs"]
            xs = x_sb[ti % 2]
            nch = NT // L
            zsel = 0 if L == 128 else 1
            if tl["kind"] == "p":
                ch_slot = [tl["slot"]] * nch
                ch_seq = [segs[0][0]] * nch
            else:
                ch_slot = [i % 2 for i in range(nch)]
                ch_seq = [segs[i][0] for i in range(nch)]
            par = ti % 2
            OSG, SGG = osg[par], sgg[par]

            def h_rhs(kc):
                return hT.t[:, kc, 0:NT], hT.rk[kc]

            def y_rhs(kc):
                return yTt[kc].t[:, 0:NT], yTt[kc].r

            def resid(j, b):
                for (sid, c0, n) in segs:
                    stt("dve", xs.t[:, j, c0:c0 + n], b.t[:, c0:c0 + n], modT.t[:, G2 + j, sid:sid + 1],
                        xs.t[:, j, c0:c0 + n], ALU.mult, ALU.add, [b.r, modT.r, xs.rk[j]], [xs.rk[j]])

            s0_, s1_ = next_w(21), next_w(22, la=NSLOT - 2)
            b0_, b1_ = next_pj(), next_pj()
            KSPLIT = 16
            for (lo, hi) in ((0, KSPLIT), (KSPLIT, FC)):
                for (si, b) in ((s0_, b0_), (s1_, b1_)):
                    for kc in range(lo, hi):
                        mm(b.t[:, 0:NT], wsl[si].t[:, kc * 128:(kc + 1) * 128], yTt[kc].t[:, 0:NT],
                           kc == 0, kc == FC - 1, [wsl[si].r, yTt[kc].r], [b.r])
            resid(0, b0_)
            yield
            resid(1, b1_)
            yield
            for j in range(2, KC):
                si = next_w(21 + j)
                b = proj_fm(si, (0, 128, 128), NT, None, nk=FC, rhs_fn=y_rhs)
                resid(j, b)
                yield

            stage("t%d_ffn2" % ti)
            norm(ti, xs, NT, segs, None, None, lambda kc: (xs.t[:, kc, 0:NT], xs.rk[kc]), final=True)
            k = "y%d" % (ti % 2)
            out_keys.add(k)
            c0 = tl["col0"]
            P.dma("pool", yT[:, :, c0:c0 + NT].rearrange("k p t -> p k t"), xs.t[:, :, 0:NT], reads=xs.rk, key=k)
            stage("t%d_end" % ti)

        fg0 = t_front(0)
        for pc in range(4, 12):
            ada_piece(pc)
            if pc in (4, 5, 6, 7):
                next(fg0, None)
        for _ in fg0:
            pass
        ada_evac(16, 48)
        ada_gm(gm2, 1, 32)
        for ti in range(n_tiles):
            fg = t_front(ti + 1) if ti + 1 < n_tiles else iter(())
            t_mid(ti, pre_last=lambda: next(fg, None))
            bg = t_back(ti)
            plan = {2: 1, 3: 1, 5: 1}
            for j in range(KC):
                next(bg)
                for _ in range(plan.get(j, 0)):
                    next(fg, None)
            for _ in fg:
                pass
            for _ in bg:
                pass

    try:
        body()
    except _Stop:
        pass
    P.emit("pool", sorted(out_keys))
    print("[kernel] sbuf bytes remaining/partition:", nc.sbuf_bytes_remaining, flush=True)
    es.close()
    return nc


_CACHE = {}


def _prep(x_prompt, x_sample, c_prompt, c_sample, state_ret, state_gla, cache_ffn_conv,
          w_ada, b_ada, g_norm_mix, w_in, w_a2, b_a2, g_ret_norm, g_gla_norm, w_out,
          g_norm_ffn, w_ffn_in, conv_w, conv_b, w_ffn_out, g_final, cores=None):
    f = np.float32
    A = lambda a: np.ascontiguousarray(np.asarray(a, dtype=f))
    if "c" not in _CACHE:
        _CACHE["c"] = _consts()
    consts = _CACHE["c"]
    x_prompt, x_sample = A(x_prompt), A(x_sample)
    c_prompt, c_sample = A(c_prompt), A(c_sample)
    shared = {
        "w_ada": A(w_ada[0]),
        "b_ada": A(A(b_ada[0]).reshape(48, 128).T),
        "gvec": A(np.stack([A(g_norm_mix[0]).reshape(KC, 128).T, A(g_norm_ffn[0]).reshape(KC, 128).T,
                            A(g_final).reshape(KC, 128).T], 1)),
        "ghead": A(np.stack([A(g_ret_norm[0]).reshape(4, 128).T, A(g_gla_norm[0]).reshape(4, 128).T], 1)),
        "w_in": A(w_in[0]),
        "w_a2b": A(np.concatenate([A(w_a2[0]), A(b_a2[0])[None, :]], 0)),
        "w_out": A(w_out[0]),
        "w_f1": A(w_ffn_in[0]),
        "convp": A(np.concatenate([A(conv_w[0]), A(conv_b[0])[None, :]], 0).reshape(4, FC, 128).transpose(2, 1, 0)),
        "w_f2": A(w_ffn_out[0]),
        "rotC": consts["rotC"], "rotS": consts["rotS"], "maskR": consts["maskR"], "mask01": consts["mask01"],
        "triN": consts["triN"], "utriN": consts["utriN"], "xi": consts["xi"], "zeta": consts["zeta"],
        "ident": consts["ident"], "ones_b": consts["ones_b"], "ones_f": consts["ones_f"],
    }
    in_maps = []
    for c in (range(NCORE) if cores is None else cores):
        xp = x_prompt[c * NPS:(c + 1) * NPS]
        xsm = x_sample[c * NSS:(c + 1) * NSS]
        cols = np.concatenate([xp.reshape(NPS * SEQ, D), xsm.reshape(NSS * DSEQ, D)], 0)
        xTc = A(cols.T.reshape(KC, 128, TOK))
        cc = np.concatenate([c_prompt[c * NPS:(c + 1) * NPS], c_sample[c * NSS:(c + 1) * NSS]], 0)
        cTc = A(cc.T.reshape(KC, 128, NSEQ).transpose(1, 0, 2))
        cci = A(cache_ffn_conv[0, c * NSS:(c + 1) * NSS])
        cci = A(cci.transpose(2, 0, 1).reshape(FC, 128, NSS, 2).transpose(1, 0, 2, 3))
        m = dict(shared)
        m.update({"xT": xTc, "cT": cTc, "st_ret": A(state_ret[0, c * NSS:(c + 1) * NSS]),
                  "st_gla": A(state_gla[0, c * NSS:(c + 1) * NSS]), "cc_in": cci})
        in_maps.append(m)
    return in_maps


def _gather(results, ncores=NCORE):
    f = np.float32
    B, DB = ncores * NPS, ncores * NSS
    y_p = np.empty((B, SEQ, D), f)
    y_s = np.empty((DB, DSEQ, D), f)
    r_p = np.empty((1, B, 4, 128, 128), f)
    g_p = np.empty((1, B, 4, 64, 128), f)
    c_p = np.empty((1, B, 2, DFF), f)
    r_s = np.empty((1, DB, 4, 128, 128), f)
    g_s = np.empty((1, DB, 4, 64, 128), f)
    c_s = np.empty((1, DB, 2, DFF), f)
    for c in range(ncores):
        r = results[c]
        yc = np.asarray(r["yT"]).reshape(D, TOK).T
        y_p[c * NPS:(c + 1) * NPS] = yc[:NPS * SEQ].reshape(NPS, SEQ, D)
        y_s[c * NSS:(c + 1) * NSS] = yc[NPS * SEQ:].reshape(NSS, DSEQ, D)
        orr, og = np.asarray(r["o_ret"]), np.asarray(r["o_gla"])
        r_p[0, c * NPS:(c + 1) * NPS] = orr[:NPS]
        r_s[0, c * NSS:(c + 1) * NSS] = orr[NPS:]
        g_p[0, c * NPS:(c + 1) * NPS] = og[:NPS]
        g_s[0, c * NSS:(c + 1) * NSS] = og[NPS:]
        occ = np.asarray(r["o_cc"])
        occ = occ.transpose(2, 3, 1, 0).reshape(NSEQ, 2, DFF)
        c_p[0, c * NPS:(c + 1) * NPS] = occ[:NPS]
        c_s[0, c * NSS:(c + 1) * NSS] = occ[NPS:]
    return (y_p, y_s, r_p, g_p, c_p, r_s, g_s, c_s)


def kernel(**inputs):
    in_maps = _prep(**inputs)
    if "nc" not in _CACHE:
        _CACHE["nc"] = build_nc(_CACHE["c"])
    res = run_bass_kernel_spmd(_CACHE["nc"], in_maps, core_ids=list(range(NCORE)))
    return _gather(res.results)
```
